# Optimizing a Trainium2 kernel written in Bass

```python
import math
import jax, jax.numpy as jnp
from jax import lax
import numpy as np

D_MODEL = 1024
BATCH = 8
SEQ = 2048
DEPTH = 1
DEC_BATCH = 128
DEC_SEQ = 1
PAST_LEN = 16384
PAGE_SIZE = 128

D_MIX = D_MODEL
D_RET = D_MIX // 2
D_LRU = D_MIX - D_RET
N_RET_HEADS = 4
RET_HEAD_DIM = D_RET // N_RET_HEADS
N_LRU_BLOCKS = 8
LRU_BLOCK = D_LRU // N_LRU_BLOCKS
CONV_LRU = 4
CONV_FFN = 3
D_FF = ((8 * D_MODEL // 3 + 127) // 128) * 128
CHUNK = 128
ROPE_BASE = 10000.0
LRU_C = 8.0
EPS = 1e-6
N_MOD = 6
D_PROJ = 4 * D_RET + 2 * D_LRU

kernel_name = "hybrid_retention_rglru_convffn_step"


def rmsnorm(x, g):
    x32 = x.astype(jnp.float32)
    r = x32 * lax.rsqrt(jnp.mean(x32 * x32, axis=-1, keepdims=True) + EPS)
    return (r * g.astype(jnp.float32)).astype(x.dtype)


def rotary(t, pos):
    d = t.shape[-1]
    inv_freq = ROPE_BASE ** (-jnp.arange(0, d, 2, dtype=jnp.float32) / d)
    ang = pos[:, None] * inv_freq[None, :]
    cos, sin = jnp.cos(ang), jnp.sin(ang)
    t1, t2 = t[..., : d // 2], t[..., d // 2:]
    return jnp.concatenate([t1 * cos - t2 * sin, t1 * sin + t2 * cos], axis=-1)


def causal_dwconv(x, buf, w, b):
    K = w.shape[0]
    T = x.shape[1]
    xp = jnp.concatenate([buf.astype(x.dtype), x], axis=1)
    out = b + sum(xp[:, k:k + T] * w[k] for k in range(K))
    return out, xp[:, T:]


def retention(q, k, v, s0):
    B, H, T, d = q.shape
    C = math.gcd(CHUNK, T)
    n = T // C
    log_g = jnp.log(1.0 - 2.0 ** (-5.0 - jnp.arange(H, dtype=jnp.float32)))
    idx = jnp.arange(C, dtype=jnp.float32)
    diff = idx[:, None] - idx[None, :]
    dmask = jnp.where(diff[None] >= 0, jnp.exp(jnp.maximum(diff, 0.0)[None] * log_g[:, None, None]), 0.0)
    q_decay = jnp.exp((idx[None] + 1.0) * log_g[:, None])
    k_decay = jnp.exp((C - 1.0 - idx[None]) * log_g[:, None])
    chunk_decay = jnp.exp(C * log_g)

    def to_chunks(t):
        return t.reshape(B, H, n, C, d).transpose(2, 0, 1, 3, 4)

    def step(s, xs):
        qc, kc, vc = xs
        scores = jnp.einsum('bhid,bhjd->bhij', qc, kc) * dmask
        o = (jnp.einsum('bhij,bhjv->bhiv', scores, vc)
             + jnp.einsum('bhid,bhdv->bhiv', qc * q_decay[None, :, :, None], s))
        s_new = (s * chunk_decay[None, :, None, None]
                 + jnp.einsum('bhjd,bhjv->bhdv', kc * k_decay[None, :, :, None], vc))
        return s_new, o

    s_last, o = lax.scan(step, s0.astype(jnp.float32), (to_chunks(q), to_chunks(k), to_chunks(v)))
    o = o.transpose(1, 2, 0, 3, 4).reshape(B, H, T, d)
    return o, s_last


def rg_lru(x, h0, pos, w_r, b_r, w_i, b_i, lam):
    B, T, W = x.shape
    x32 = x.astype(jnp.float32)
    xb = x32.reshape(B, T, N_LRU_BLOCKS, LRU_BLOCK)
    r = jax.nn.sigmoid(jnp.einsum('btnd,nde->btne', xb, w_r.astype(jnp.float32)).reshape(B, T, W) + b_r)
    i = jax.nn.sigmoid(jnp.einsum('btnd,nde->btne', xb, w_i.astype(jnp.float32)).reshape(B, T, W) + b_i)
    log_a = -LRU_C * r * jax.nn.softplus(-lam.astype(jnp.float32))
    a = jnp.exp(log_a)
    mult = jnp.sqrt(-jnp.expm1(2.0 * log_a))
    mult = jnp.where((pos == 0)[None, :, None], 1.0, mult)
    bterm = mult * i * x32

    def combine(l, rr):
        a1, b1 = l
        a2, b2 = rr
        return a1 * a2, a2 * b1 + b2

    a_cum, b_cum = lax.associative_scan(combine, (a, bterm), axis=1)
    h = a_cum * h0.astype(jnp.float32)[:, None, :] + b_cum
    return h, h[:, -1]


def hybrid_layer(x, c, pos0, s_ret, s_h, s_conv_lru, s_conv_ffn,
                 w_ada, b_ada, g_mix, w_in, conv_lru_w, conv_lru_b, w_r, b_r, w_i, b_i, lam,
                 w_out, g_ffn, w_up_conv, w_up_gate, conv_ffn_w, conv_ffn_b, w_down):
    B, T, D = x.shape
    pos = pos0 + jnp.arange(T, dtype=jnp.float32)
    mod = (jax.nn.silu(c) @ w_ada + b_ada).reshape(B, N_MOD, D)[:, :, None, :]
    shift_m, scale_m, gate_m, shift_f, scale_f, gate_f = [mod[:, j] for j in range(N_MOD)]

    h = rmsnorm(x, g_mix) * (1.0 + scale_m) + shift_m
    proj = h @ w_in
    offs = np.cumsum([D_RET, D_RET, D_RET, D_RET, D_LRU])
    q, k, v, g_ret, x_lru, g_lru = jnp.split(proj, offs, axis=-1)

    def heads(t):
        return t.reshape(B, T, N_RET_HEADS, RET_HEAD_DIM).transpose(0, 2, 1, 3).astype(jnp.float32)

    qh = rotary(heads(q), pos)
    kh = rotary(heads(k), pos) * (RET_HEAD_DIM ** -0.5)
    vh = heads(v)
    o, s_ret_new = retention(qh, kh, vh, s_ret)
    mu = jnp.mean(o, axis=-1, keepdims=True)
    var = jnp.mean((o - mu) ** 2, axis=-1, keepdims=True)
    o = ((o - mu) * lax.rsqrt(var + EPS)).transpose(0, 2, 1, 3).reshape(B, T, D_RET)
    ret_out = (jax.nn.silu(g_ret.astype(jnp.float32)) * o).astype(x.dtype)

    xc, s_conv_lru_new = causal_dwconv(x_lru, s_conv_lru, conv_lru_w, conv_lru_b)
    hl, s_h_new = rg_lru(xc, s_h, pos, w_r, b_r, w_i, b_i, lam)
    lru_out = (hl * jax.nn.gelu(g_lru.astype(jnp.float32))).astype(x.dtype)

    mix = jnp.concatenate([ret_out, lru_out], axis=-1) @ w_out
    x = x + gate_m * mix

    h = rmsnorm(x, g_ffn) * (1.0 + scale_f) + shift_f
    u = h @ w_up_conv
    gv = h @ w_up_gate
    uc, s_conv_ffn_new = causal_dwconv(u, s_conv_ffn, conv_ffn_w, conv_ffn_b)
    f = (jax.nn.gelu(uc) * gv) @ w_down
    x = x + gate_f * f
    return x, s_ret_new, s_h_new, s_conv_lru_new, s_conv_ffn_new


def setup_inputs(seed: int = 0) -> dict:
    key = jax.random.key(seed)
    ks = jax.random.split(key, 40)
    f32 = jnp.float32
    nrm = lambda k, shape, s: jax.random.normal(k, shape, f32) * s
    L = DEPTH
    u = jax.random.uniform(ks[20], (L, D_LRU), f32, 0.9, 0.999)
    sa = u ** (1.0 / LRU_C)
    lam = jnp.log(sa) - jnp.log1p(-sa)
    return {
        "x_prompt": nrm(ks[0], (BATCH, SEQ, D_MODEL), 1.0),
        "x_sample": nrm(ks[1], (DEC_BATCH, DEC_SEQ, D_MODEL), 1.0),
        "c_prompt": nrm(ks[2], (BATCH, D_MODEL), 1.0),
        "c_sample": nrm(ks[3], (DEC_BATCH, D_MODEL), 1.0),
        "state_ret": nrm(ks[4], (L, DEC_BATCH, N_RET_HEADS, RET_HEAD_DIM, RET_HEAD_DIM), 0.5),
        "state_lru_h": nrm(ks[5], (L, DEC_BATCH, D_LRU), 0.5),
        "state_lru_conv": nrm(ks[6], (L, DEC_BATCH, CONV_LRU - 1, D_LRU), 1.0),
        "state_ffn_conv": nrm(ks[7], (L, DEC_BATCH, CONV_FFN - 1, D_FF), 1.0),
        "w_ada": nrm(ks[8], (L, D_MODEL, N_MOD * D_MODEL), 0.3 * D_MODEL ** -0.5),
        "b_ada": nrm(ks[9], (L, N_MOD * D_MODEL), 0.02),
        "g_mix": 1.0 + nrm(ks[10], (L, D_MODEL), 0.02),
        "w_in": nrm(ks[11], (L, D_MODEL, D_PROJ), D_MODEL ** -0.5),
        "conv_lru_w": nrm(ks[12], (L, CONV_LRU, D_LRU), CONV_LRU ** -0.5),
        "conv_lru_b": nrm(ks[13], (L, D_LRU), 0.02),
        "w_r": nrm(ks[14], (L, N_LRU_BLOCKS, LRU_BLOCK, LRU_BLOCK), LRU_BLOCK ** -0.5),
        "b_r": nrm(ks[15], (L, D_LRU), 0.02),
        "w_i": nrm(ks[16], (L, N_LRU_BLOCKS, LRU_BLOCK, LRU_BLOCK), LRU_BLOCK ** -0.5),
        "b_i": nrm(ks[17], (L, D_LRU), 0.02),
        "lam": lam,
        "w_out": nrm(ks[18], (L, D_MIX, D_MODEL), D_MIX ** -0.5),
        "g_ffn": 1.0 + nrm(ks[19], (L, D_MODEL), 0.02),
        "w_up_conv": nrm(ks[21], (L, D_MODEL, D_FF), D_MODEL ** -0.5),
        "w_up_gate": nrm(ks[22], (L, D_MODEL, D_FF), D_MODEL ** -0.5),
        "conv_ffn_w": nrm(ks[23], (L, CONV_FFN, D_FF), CONV_FFN ** -0.5),
        "conv_ffn_b": nrm(ks[24], (L, D_FF), 0.02),
        "w_down": nrm(ks[25], (L, D_FF, D_MODEL), D_FF ** -0.5),
        "g_final": 1.0 + nrm(ks[26], (D_MODEL,), 0.02),
    }


def reference(x_prompt, x_sample, c_prompt, c_sample, state_ret, state_lru_h, state_lru_conv,
              state_ffn_conv, w_ada, b_ada, g_mix, w_in, conv_lru_w, conv_lru_b, w_r, b_r, w_i,
              b_i, lam, w_out, g_ffn, w_up_conv, w_up_gate, conv_ffn_w, conv_ffn_b, w_down,
              g_final):
    yp, ys = x_prompt, x_sample
    ret_p, h_p, cl_p, cf_p = [], [], [], []
    ret_s, h_s, cl_s, cf_s = [], [], [], []
    for l in range(DEPTH):
        wts = (w_ada[l], b_ada[l], g_mix[l], w_in[l], conv_lru_w[l], conv_lru_b[l], w_r[l], b_r[l],
               w_i[l], b_i[l], lam[l], w_out[l], g_ffn[l], w_up_conv[l], w_up_gate[l],
               conv_ffn_w[l], conv_ffn_b[l], w_down[l])
        z_ret = jnp.zeros((BATCH, N_RET_HEADS, RET_HEAD_DIM, RET_HEAD_DIM), jnp.float32)
        z_h = jnp.zeros((BATCH, D_LRU), jnp.float32)
        z_cl = jnp.zeros((BATCH, CONV_LRU - 1, D_LRU), yp.dtype)
        z_cf = jnp.zeros((BATCH, CONV_FFN - 1, D_FF), yp.dtype)
        yp, sr, sh, scl, scf = hybrid_layer(yp, c_prompt, 0, z_ret, z_h, z_cl, z_cf, *wts)
        ret_p.append(sr); h_p.append(sh); cl_p.append(scl); cf_p.append(scf)
        ys, sr, sh, scl, scf = hybrid_layer(ys, c_sample, PAST_LEN, state_ret[l], state_lru_h[l],
                                            state_lru_conv[l], state_ffn_conv[l], *wts)
        ret_s.append(sr); h_s.append(sh); cl_s.append(scl); cf_s.append(scf)
    y_prompt = rmsnorm(yp, g_final)
    y_sample = rmsnorm(ys, g_final)
    return (y_prompt, y_sample,
            jnp.stack(ret_p), jnp.stack(h_p), jnp.stack(cl_p), jnp.stack(cf_p),
            jnp.stack(ret_s), jnp.stack(h_s), jnp.stack(cl_s), jnp.stack(cf_s))
```

```python
import math
from contextlib import ExitStack

import numpy as np
import concourse.bass as bass
import concourse.mybir as mybir
from concourse.bass_utils import run_bass_kernel_spmd

F32 = mybir.dt.float32
BF16 = mybir.dt.bfloat16
AF = mybir.ActivationFunctionType
ALU = mybir.AluOpType
AX = mybir.AxisListType

D = 1024
SEQ = 2048
NS = 16
DFF = 2816
NFF = 22
DPROJ = 3072
NT = 16
NST = 4
EPS = 1e-6
PAST_LEN = 16384
N_CORES = 8
NDMASEM = 32


class Op:
    __slots__ = ("eng", "fn", "deps", "dma", "sem", "val", "flag", "idx")


class Sched:
    def __init__(self):
        self.ops = []
        self.lastw = {}
        self.lastr = {}
        self.dma_last = [None] * NDMASEM
        self.dma_cnt = 0
        self.dma_cnt_sw = 0

    def add(self, eng, fn, reads=(), writes=(), dma=False):
        op = Op()
        op.eng, op.fn, op.dma, op.flag = eng, fn, dma, dma
        op.idx = len(self.ops)
        op.sem = None
        op.val = 0
        deps = set()
        for k in reads:
            w = self.lastw.get(k)
            if w is not None:
                deps.add(w)
            if isinstance(k, tuple) and k[0] == "ps":
                for e2, i2 in self.lastr.get(k, {}).items():
                    if e2 != eng:
                        deps.add(i2)
        for k in writes:
            w = self.lastw.get(k)
            if w is not None:
                deps.add(w)
            r = self.lastr.get(k)
            if r:
                deps.update(r.values())
        ek = ("dma", op.idx) if dma else eng
        for k in reads:
            self.lastr.setdefault(k, {})[ek] = op.idx
        for k in writes:
            self.lastw[k] = op.idx
            self.lastr[k] = {}
        if dma:
            half = NDMASEM // 2
            if eng == "pool":
                slot = half + self.dma_cnt_sw % half
                self.dma_cnt_sw += 1
            else:
                slot = self.dma_cnt % half
                self.dma_cnt += 1
            op.sem = ("dma", slot)
            prev = self.dma_last[slot]
            if prev is not None:
                deps.add(prev)
            self.dma_last[slot] = op.idx
        deps.discard(op.idx)
        op.deps = deps
        self.ops.append(op)
        return op.idx

    def emit(self, nc, block, sems, dma_sems):
        ops = self.ops
        for op in ops:
            for d in op.deps:
                dep = ops[d]
                if dep.eng == "pe" and op.eng == "pe" and not dep.dma and not op.dma:
                    continue
                dep.flag = True
        cnt = {e: 0 for e in sems}
        dcnt = [0] * NDMASEM
        for op in ops:
            if op.dma:
                s = op.sem[1]
                dcnt[s] += 16
                op.val = dcnt[s]
            elif op.flag:
                cnt[op.eng] += 1
                op.val = cnt[op.eng]
                op.sem = ("eng", op.eng)
        per_eng = {e: [] for e in sems}
        for op in ops:
            per_eng[op.eng].append(op)

        def run(engname, handle):
            waited = {}
            for op in per_eng[engname]:
                need = {}
                for d in op.deps:
                    dep = ops[d]
                    if dep.eng == "pe" and op.eng == "pe" and not dep.dma and not op.dma:
                        continue
                    if dep.val > need.get(dep.sem, 0):
                        need[dep.sem] = dep.val
                for sk, v in need.items():
                    if waited.get(sk, 0) >= v:
                        continue
                    waited[sk] = v
                    sem = dma_sems[sk[1]] if sk[0] == "dma" else sems[sk[1]]
                    handle.wait_ge(sem, v)
                inst = op.fn(handle)
                if op.dma:
                    inst.then_inc(dma_sems[op.sem[1]], 16)
                elif op.flag:
                    inst.then_inc(sems[op.eng], 1)
            if engname == "sp":
                for s in range(NDMASEM):
                    if dcnt[s] > 0:
                        handle.wait_ge(dma_sems[s], dcnt[s])

        @block.sync
        def _(e):
            run("sp", e)

        @block.tensor
        def _(e):
            run("pe", e)

        @block.scalar
        def _(e):
            run("act", e)

        @block.vector
        def _(e):
            run("dve", e)

        @block.gpsimd
        def _(e):
            run("pool", e)


class K:
    def __init__(self, S):
        self.S = S

    def mm(self, out, lhsT, rhs, start, stop, r, w):
        return self.S.add("pe", lambda e: e.matmul(out, lhsT, rhs, start=start, stop=stop), r, w)

    def tr(self, out, in_, ident, r, w):
        return self.S.add("pe", lambda e: e.transpose(out, in_, ident), r, w)

    def act(self, out, in_, func, r, w, bias=None, scale=None, accum_out=None, eng="act"):
        kw = {}
        if bias is not None:
            kw["bias"] = bias
        if scale is not None:
            kw["scale"] = scale
        if accum_out is not None:
            kw["accum_out"] = accum_out
        return self.S.add(eng, lambda e: e.activation(out, in_, func, **kw), r, w)

    def tt(self, out, in0, in1, op, r, w, eng="dve"):
        return self.S.add(eng, lambda e: e.tensor_tensor(out, in0, in1, op), r, w)

    def ts(self, out, in0, s1, s2, op0, op1, r, w, eng="dve"):
        if op1 is None:
            return self.S.add(eng, lambda e: e.tensor_scalar(out, in0, s1, None, op0), r, w)
        return self.S.add(eng, lambda e: e.tensor_scalar(out, in0, s1, s2, op0, op1), r, w)

    def stt(self, out, in0, scalar, in1, op0, op1, r, w):
        return self.S.add("dve", lambda e: e.scalar_tensor_tensor(out, in0, scalar, in1, op0, op1), r, w)

    def copy(self, out, in_, r, w, eng="dve"):
        if eng == "act":
            return self.S.add(eng, lambda e: e.activation(out, in_, AF.Copy), r, w)
        return self.S.add(eng, lambda e: e.tensor_copy(out, in_), r, w)

    def memset(self, ap, val, w, eng="dve"):
        return self.S.add(eng, lambda e: e.memset(ap, val), (), w)

    def dma(self, out, in_, r, w, eng="sp", **kw):
        return self.S.add(eng, lambda e: e.dma_start(out=out, in_=in_, **kw), r, w, dma=True)


def _consts():
    c = {}
    c["ident"] = np.eye(128, dtype=np.float32)
    j = np.arange(128)[:, None]
    i = np.arange(128)[None, :]
    c["maskT"] = (j <= i).astype(np.float32)
    inv_freq = (10000.0 ** (-np.arange(0, 128, 2, dtype=np.float32) / np.float32(128))).astype(np.float32)
    pos = np.arange(SEQ, dtype=np.float32)
    ang = (pos[:, None] * inv_freq[None, :]).astype(np.float32)
    cosP = np.cos(ang).astype(np.float32).reshape(NT, 128, 64).transpose(1, 0, 2)
    sinP = np.sin(ang).astype(np.float32).reshape(NT, 128, 64).transpose(1, 0, 2)
    c["cosP"] = np.ascontiguousarray(cosP)
    c["sinP"] = np.ascontiguousarray(sinP)
    log_g = np.log(1.0 - 2.0 ** (-5.0 - np.arange(4, dtype=np.float64)))
    ii = np.arange(128, dtype=np.float64)
    dq = np.exp((ii[None, :] + 1.0) * log_g[:, None])
    dk = (128.0 ** -0.5) * np.exp(-(ii[None, :] + 1.0) * log_g[:, None])
    c["dqT"] = np.ascontiguousarray(np.broadcast_to(dq[None], (128, 4, 128))).astype(np.float32)
    c["dkT"] = np.ascontiguousarray(np.broadcast_to(dk[None], (128, 4, 128))).astype(np.float32)
    c["dkP"] = np.ascontiguousarray(dk.T).astype(np.float32)
    c["gC"] = [float(np.exp(128.0 * log_g[h])) for h in range(4)]
    c["g1"] = [float(np.exp(log_g[h])) for h in range(4)]
    angS = (np.float32(PAST_LEN) * inv_freq).astype(np.float32)
    cs = np.cos(angS).astype(np.float32)
    sn = np.sin(angS).astype(np.float32)
    sc = np.float32(128.0 ** -0.5)
    c["rotS"] = np.stack([cs, sn, cs * sc, sn * sc]).astype(np.float32)
    sel = np.zeros((17, 129), np.float32)
    sel[16, :] = 1.0
    c["sel"] = sel
    c["i16"] = np.ascontiguousarray(np.broadcast_to(np.eye(16, dtype=np.float32)[None], (128, 16, 16)))
    return c


CONST = _consts()

VFM = {}
_off = 0
for _n, _w in [("g_mix", 8), ("g_ffn", 8), ("clw", 16), ("clb", 4), ("b_r", 4), ("b_i", 4), ("lam", 4),
               ("cfw", 66), ("cfb", 22)]:
    VFM[_n] = (_off, _w)
    _off += _w
NVFM = _off


import os
KPH = int(os.environ.get("KPHASE", "99"))


def build_program():
    nc = bass.Bass("TRN2", target_bir_lowering=False)
    S = Sched()
    k = K(S)

    def din(name, shape):
        return nc.dram_tensor(name, list(shape), F32, kind="ExternalInput").ap()

    def dout(name, shape):
        return nc.dram_tensor(name, list(shape), F32, kind="ExternalOutput").ap()

    xp = din("xp", [SEQ, D])
    xs = din("xs", [NS, D])
    c17 = din("c17", [17, D])
    st_ret = din("st_ret", [NS, 4, 128, 128])
    st_h = din("st_h", [NS, 512])
    st_cl = din("st_cl", [NS, 3, 512])
    st_cf = din("st_cf", [NS, 2, DFF])
    w_ada = din("w_ada", [D, 6 * D])
    b_ada = din("b_ada", [6 * D])
    w_in = din("w_in", [D, DPROJ])
    w_out = din("w_out", [D, D])
    w_upc = din("w_upc", [D, DFF])
    w_upg = din("w_upg", [D, DFF])
    w_dn = din("w_dn", [DFF, D])
    wr_bd = din("wr_bd", [128, 4, 128])
    wi_bd = din("wi_bd", [128, 4, 128])
    vfm_d = din("vfm", [128, NVFM])
    rows_d = {}
    for n, ln in [("g_mix", D), ("g_ffn", D), ("g_final", D), ("clw", 4 * 512), ("clb", 512), ("b_r", 512),
                  ("b_i", 512), ("lam", 512), ("cfw", 3 * DFF), ("cfb", DFF)]:
        rows_d[n] = din("row_" + n, [ln])
    cd = {}
    for n in ["ident", "maskT", "cosP", "sinP", "dqT", "dkT", "dkP", "rotS", "sel", "i16"]:
        cd[n] = din("c_" + n, CONST[n].shape)

    yp = dout("yp", [SEQ, D])
    ys = dout("ys", [NS, D])
    o_retp = dout("o_retp", [4, 128, 128])
    o_hp = dout("o_hp", [4, 128])
    o_clp = dout("o_clp", [3, 512])
    o_cfp = dout("o_cfp", [2, DFF])
    o_rets = dout("o_rets", [NS, 4, 128, 128])
    o_hs = dout("o_hs", [NS, 512])
    o_cls = dout("o_cls", [NS, 3, 512])
    o_cfs = dout("o_cfs", [NS, 2, DFF])

    x1_scr = nc.dram_tensor("x1_scr", [SEQ, D], F32, kind="Internal").ap()
    x1s_scr = nc.dram_tensor("x1s_scr", [NS, D], F32, kind="Internal").ap()
    modf_scr = nc.dram_tensor("modf_scr", [NS, 3 * D], F32, kind="Internal").ap()
    gatef_scr = nc.dram_tensor("gatef_scr", [1, D], F32, kind="Internal").ap()

    es = ExitStack()
    with es:
        def sb(name, shape, dt=F32):
            return es.enter_context(nc.sbuf_tensor("sb_" + name, list(shape), dt))

        ps = [es.enter_context(nc.psum_tensor("ps%d" % i, [128, 512], F32)) for i in range(8)]
        sems = {e: es.enter_context(nc.semaphore("sem_" + e)) for e in ["pe", "act", "dve", "pool", "sp"]}
        dma_sems = [es.enter_context(nc.semaphore("dsem%d" % i)) for i in range(NDMASEM)]

        ident = sb("ident", [128, 128])
        identb = sb("identb", [128, 128], BF16)
        vfm = sb("vfm", [128, NVFM])
        modfm = sb("modfm", [128, 4, 8])
        gsm = sb("gsm", [128, 8])
        gsf = sb("gsf", [128, 8])
        cf_fm = sb("cf_fm", [128, 4])
        tmp_fm = sb("tmp_fm", [128, 4])

        k.dma(ident[:], cd["ident"], (), ["ident"])
        k.copy(identb[:], ident[:], ["ident"], ["identb"], eng="act")
        k.dma(vfm[:], vfm_d, (), ["vfm"])

        def V(name):
            o, w = VFM[name]
            return vfm[:, o:o + w]

        k.act(tmp_fm[:], V("lam"), AF.Exp, ["vfm"], ["tmp_fm"], scale=-1.0)
        k.act(tmp_fm[:], tmp_fm[:], AF.Ln, ["tmp_fm"], ["tmp_fm"], bias=1.0)
        k.ts(cf_fm[:], tmp_fm[:], -8.0, None, ALU.mult, None, ["tmp_fm"], ["cf_fm"])

        esA = ExitStack()
        with esA:
            def sbA(name, shape, dt=F32):
                return esA.enter_context(nc.sbuf_tensor("sb_" + name, list(shape), dt))

            w_in_sb = sbA("w_in_sb", [128, 8, DPROJ], BF16)
            w_out_sb = sbA("w_out_sb", [128, 8, D], BF16)
            wr_sb = sbA("wr_sb", [128, 4, 128], BF16)
            wi_sb = sbA("wi_sb", [128, 4, 128], BF16)
            gate_m_row = sbA("gate_m_row", [128, D])

            es0 = ExitStack()
            with es0:
                def sb0(name, shape, dt=F32):
                    return es0.enter_context(nc.sbuf_tensor("sb_" + name, list(shape), dt))

                mod = sb0("mod", [17, 6 * D])
                early = dict(
                    x=sb0("sm_x", [NS, D]), cat=sb0("sm_cat", [NS, D]), stc=sb0("sm_stc", [NS, 3, 512]),
                    clb=sb0("sm_clb", [NS, 512]), brr=sb0("sm_brr", [NS, 512]), bir=sb0("sm_bir", [NS, 512]),
                    lamr=sb0("sm_lamr", [NS, 512]), proj=sb0("sm_proj", [NS, DPROJ]), qr=sb0("sm_qr", [NS, 512]),
                    kr=sb0("sm_kr", [NS, 512]))
                k.dma(early["x"][:], xs, (), ["smx"])
                k.dma(early["cat"][:], rows_d["g_mix"].partition_broadcast(NS), (), ["sm_grow"])
                k.dma(early["stc"][:], st_cl, (), ["sm_stc"])
                k.dma(early["clb"][:], rows_d["clb"].partition_broadcast(NS), (), ["sm_clb"])
                k.dma(early["brr"][:], rows_d["b_r"].partition_broadcast(NS), (), ["sm_brr"])
                k.dma(early["bir"][:], rows_d["b_i"].partition_broadcast(NS), (), ["sm_bir"])
                k.dma(early["lamr"][:], rows_d["lam"].partition_broadcast(NS), (), ["sm_lamr"])
                k.act(early["lamr"][:], early["lamr"][:], AF.Exp, ["sm_lamr"], ["sm_lamr"], scale=-1.0)
                k.act(early["lamr"][:], early["lamr"][:], AF.Ln, ["sm_lamr"], ["sm_lamr"], bias=1.0)
                esM = ExitStack()
                with esM:
                    def sbM(name, shape, dt=F32):
                        return esM.enter_context(nc.sbuf_tensor("sb_" + name, list(shape), dt))

                    c_sb = sbM("c_sb", [17, D])
                    sc_sb = sbM("sc_sb", [17, D])
                    scT = sbM("scT", [128, 8, 17], BF16)
                    NWB = 2
                    wada = [sbM("wada%d" % i, [128, 8, 1024], BF16) for i in range(NWB)]
                    bada = [sbM("bada%d" % i, [17, 1024]) for i in range(NWB)]
                    sel = sbM("sel", [17, 129])

                    k.dma(c_sb[:], c17, (), ["c_sb"])
                    k.dma(sel[:], cd["sel"], (), ["sel"])
                    wada_v = w_ada.rearrange("(kc p) n -> p kc n", p=128)

                    def load_wada(cb):
                        k.dma(wada[cb % NWB][:], wada_v[:, :, cb * 1024:(cb + 1) * 1024], (), [("wada", cb % NWB)],
                              eng="pool")
                        k.dma(bada[cb % NWB][:], b_ada[cb * 1024:(cb + 1) * 1024].partition_broadcast(17), (),
                              [("bada", cb % NWB)])

                    win_v = w_in.rearrange("(kc p) n -> p kc n", p=128)
                    wout_v = w_out.rearrange("(kc p) n -> p kc n", p=128)
                    for cb_ in range(NWB):
                        load_wada(cb_)
                    for kc in range(8):
                        k.dma(w_in_sb[:, kc, :].rearrange("p (a b) -> p a b", b=1024),
                              win_v[:, kc, :].rearrange("p (a b) -> p a b", b=1024), (), [("w_in", kc)], eng="pool")

                    k.act(sc_sb[:], c_sb[:], AF.Silu, ["c_sb"], ["sc_sb"])
                    for kc in range(8):
                        k.tr(ps[0][:, kc * 17:(kc + 1) * 17], sc_sb[0:17, kc * 128:(kc + 1) * 128], ident[0:17, 0:17],
                             ["sc_sb", "ident"], [("ps", 0)])
                    k.copy(scT[:].rearrange("p a b -> p (a b)"), ps[0][:, 0:136], [("ps", 0)], ["scT"])

                    for cb in range(6):
                        for hh in range(2):
                            pb = ps[1 + hh]
                            for kc in range(8):
                                k.mm(pb[0:17, :], scT[:, kc, :], wada[cb % NWB][:, kc, hh * 512:(hh + 1) * 512], kc == 0,
                                     kc == 7, ["scT", ("wada", cb % NWB)], [("ps", 1 + hh)])
                            k.tt(mod[:, cb * 1024 + hh * 512:cb * 1024 + (hh + 1) * 512], pb[0:17, :],
                                 bada[cb % NWB][:, hh * 512:(hh + 1) * 512], ALU.add,
                                 [("ps", 1 + hh), ("bada", cb % NWB)], [("mod", 2 * cb + hh)])
                        if cb + NWB < 6:
                            load_wada(cb + NWB)
                        if cb == 2 and KPH >= 1:
                            sample_a(nc, k, sbM, ps, ident, mod, [("mod", i_) for i_ in range(6)], cd, w_in_sb, early)
                        if cb == 6 - NWB - 1:
                            for kc in range(0, 8, 2):
                                k.dma(w_out_sb[:, kc:kc + 2, :], wout_v[:, kc:kc + 2, :], (),
                                      [("w_out", kc), ("w_out", kc + 1)], eng="pool")
                            k.dma(wr_sb[:], wr_bd, (), ["wr_sb"], eng="pool")
                            k.dma(wi_sb[:], wi_bd, (), ["wi_sb"], eng="pool")
                    modkeys = [("mod", cb) for cb in range(12)]

                    for jj, j in enumerate([0, 1, 3, 4]):
                        for c in range(8):
                            col = jj * 8 + c
                            k.mm(ps[3][:, col:col + 1], mod[0:17, j * D + c * 128: j * D + (c + 1) * 128],
                                 sel[0:17, 0:1], True, True, modkeys + ["sel"], [("ps", 3)])
                    k.copy(modfm[:].rearrange("p a b -> p (a b)"), ps[3][:, 0:32], [("ps", 3)], ["modfm"])
                    k.stt(gsm[:], modfm[:, 1, :], 1.0, V("g_mix"), ALU.add, ALU.mult, ["modfm", "vfm"], ["gsm"])
                    k.stt(gsf[:], modfm[:, 3, :], 1.0, V("g_ffn"), ALU.add, ALU.mult, ["modfm", "vfm"], ["gsf"])
                    for gi, (j, dst, key) in enumerate([(2, gate_m_row, "gate_m_row")]):
                        for hf in range(2):
                            pb = ps[4 + hf]
                            k.mm(pb[:, :], sel[0:17, 1:129], mod[0:17, j * D + hf * 512: j * D + (hf + 1) * 512],
                                 True, True, modkeys + ["sel"], [("ps", 4 + hf)])
                            k.copy(dst[:, hf * 512:(hf + 1) * 512], pb[:, :], [("ps", 4 + hf)], [(key, hf)], eng="act")
                    k.dma(modf_scr, mod[0:NS, 3 * D:6 * D], modkeys, ["modf_scr"])
                    k.dma(gatef_scr, mod[16:17, 5 * D:6 * D], modkeys, ["gatef_scr"])
                barrier(S)
                modkeys = []

                if KPH >= 1:
                  sample_mixer(nc, k, es0, ps, ident, xs, mod, modkeys, rows_d, cd, w_in_sb, w_out_sb, wr_sb, wi_sb,
                               st_ret, st_h, st_cl, o_rets, o_hs, o_cls, x1s_scr, early)

            barrier(S)
            es1 = ExitStack()
            with es1:
                if KPH >= 2:
                  prompt_mixer(nc, k, es1, ps, ident, identb, xp, cd, V, gsm, modfm, gate_m_row, cf_fm,
                               w_in_sb, w_out_sb, wr_sb, wi_sb, x1_scr, o_retp, o_hp, o_clp)
            barrier(S)

        esB = ExitStack()
        with esB:
            def sbB(name, shape, dt=F32):
                return esB.enter_context(nc.sbuf_tensor("sb_" + name, list(shape), dt))

            wuc_sb = sbB("wuc_sb", [128, 8, DFF], BF16)
            wug_sb = sbB("wug_sb", [128, 8, DFF], BF16)
            wdn_sb = sbB("wdn_sb", [128, NFF, D], BF16)
            wuc_v = w_upc.rearrange("(kc p) n -> p kc n", p=128)
            wug_v = w_upg.rearrange("(kc p) n -> p kc n", p=128)
            wdn_v = w_dn.rearrange("(m p) n -> p m n", p=128)
            def _wl_up(gi, a, b):
                def f():
                    for h0 in (0, 4):
                        k.dma(wuc_sb[:, h0:h0 + 4, a:b], wuc_v[:, h0:h0 + 4, a:b], (),
                              [("wuc", kc, gi) for kc in range(h0, h0 + 4)], eng="pool")
                        k.dma(wug_sb[:, h0:h0 + 4, a:b], wug_v[:, h0:h0 + 4, a:b], (),
                              [("wug", kc, gi) for kc in range(h0, h0 + 4)], eng="pool")
                return f

            def _wl_dn():
                for m0 in range(0, NFF, 4):
                    m1 = min(NFF, m0 + 4)
                    k.dma(wdn_sb[:, m0:m1, :], wdn_v[:, m0:m1, :], (), [("wdn", m) for m in range(m0, m1)],
                          eng="pool")

            wload = [_wl_up(0, 0, 1024), _wl_up(1, 1024, 2048), _wl_up(2, 2048, DFF), _wl_dn]

            es3 = ExitStack()
            with es3:
                if KPH >= 4:
                  prompt_ffn(nc, k, es3, ps, ident, V, gsf, modfm, gatef_scr, rows_d, x1_scr, wuc_sb, wug_sb, wdn_sb,
                             yp, o_cfp, wload)
                else:
                  for f_ in wload:
                      f_()
            barrier(S)
            es2 = ExitStack()
            with es2:
                if KPH >= 3:
                  sample_ffn(nc, k, es2, ps, ident, x1s_scr, modf_scr, rows_d, wuc_sb, wug_sb, wdn_sb, st_cf, o_cfs, ys)

        with nc.Block() as block:
            S.emit(nc, block, sems, dma_sems)
    return nc


ALLK = "__all__"


def barrier(S, keep=None):
    keep = keep or (lambda key: False)
    allk = [kk for kk in (set(S.lastw.keys()) | set(S.lastr.keys())) if not keep(kk)]
    order = ["act", "dve", "pool", "sp", "pe"]
    for e in order:
        S.add(e, _drain, (), allk + [("barrier", e)])
    for e in order:
        S.add(e, _drain, [("barrier", x) for x in order], [("barrier2", e)])
    kw = {kk: v for kk, v in S.lastw.items() if keep(kk)}
    kr = {kk: v for kk, v in S.lastr.items() if keep(kk)}
    S.lastw = {("barrier2", e): S.lastw[("barrier2", e)] for e in order}
    S.lastw.update(kw)
    S.lastr = kr


def _drain(e):
    return e.drain()


def prompt_mixer(nc, k, es, ps, ident, identb, xp, cd, V, gsm, modfm, gate_m_row, cf_fm,
                 w_in_sb, w_out_sb, wr_sb, wi_sb, x1_scr, o_retp, o_hp, o_clp):
    S = k.S

    def sb(name, shape, dt=F32):
        return es.enter_context(nc.sbuf_tensor("sb_" + name, list(shape), dt))

    cosP = sb("cosP", [128, NT, 64]); sinP = sb("sinP", [128, NT, 64])
    maskT = sb("maskT", [128, 128]); dqT = sb("dqT", [128, 4, 128]); dkT = sb("dkT", [128, 4, 128])
    dkP = sb("dkP", [128, 4])
    for t_, n in [(cosP, "cosP"), (sinP, "sinP"), (maskT, "maskT"), (dqT, "dqT"), (dkT, "dkT"), (dkP, "dkP")]:
        k.dma(t_[:], cd[n], (), [n])
    neg_half = sb("neg_half", [128, 4])
    k.memset(neg_half[:], -0.5, ["neg_half"])
    hcf = sb("hcf", [128, 4]); hb_r = sb("hb_r", [128, 4]); hb_i = sb("hb_i", [128, 4])
    k.ts(hcf[:], cf_fm[:], 0.5, None, ALU.mult, None, ["cf_fm"], ["hcf"])
    k.ts(hb_r[:], V("b_r"), 0.5, None, ALU.mult, None, ["vfm"], ["hb_r"])
    k.ts(hb_i[:], V("b_i"), 0.5, None, ALU.mult, None, ["vfm"], ["hb_i"])

    xt = [sb("xt%d" % i, [128, D]) for i in range(2)]
    xr = sb("xr0", [128, D]); x1t = sb("x1t0", [128, D])
    ssb = sb("ssb", [128, 4]); msb = sb("msb", [128, 4]); rstd = sb("rstd", [128, 4])
    xn = sb("xn0", [128, D])
    hT = [sb("hT%d" % i, [128, 8, 512], BF16) for i in range(2)]
    catT = [sb("catT%d" % i, [128, 8, 512], BF16) for i in range(2)]
    xc = sb("xc", [128, 2, 512]); rr = sb("rr", [128, 2, 512]); aa = sb("aa", [128, 2, 512])
    ig = sb("ig", [128, 2, 512]); gl = sb("gl", [128, 2, 512]); xcb = sb("xcb", [128, 2, 512], BF16)
    xcar = sb("xcar", [128, 3, 4]); hcar = sb("hcar", [128, 4])
    st12 = sb("st12", [12, 128]); st4 = sb("st4", [4, 128])
    qrot = sb("qrot", [128, 512]); krot = sb("krot", [128, 512])
    mq = [sb("mq%d" % i, [128, 256]) for i in range(4)]
    ktok = [sb("ktok%d" % i, [128, 512], BF16) for i in range(2)]
    v_bf = [sb("v_bf%d" % i, [128, 512], BF16) for i in range(2)]
    sg = [sb("sg%d" % i, [128, 512]) for i in range(2)]
    qkT = [sb("qkT%d" % i, [128, 8, 128], BF16) for i in range(2)]
    PT = sb("PT", [128, 512], BF16)
    Z = sb("Z", [128, 512]); S_bf = sb("S_bf", [128, 512], BF16)
    stats = sb("stats", [128, 4, 6]); mv = sb("mv", [128, 4, 2]); vpe = sb("vpe", [128, 4]); rs = sb("rs", [128, 4])
    sgr = sb("sgr", [128, 512]); ret = [sb("ret%d" % i, [128, 512], BF16) for i in range(2)]
    Sl = sgr

    clw = V("clw"); clb = V("clb"); b_r = V("b_r"); b_i = V("b_i")
    shift_m = modfm[:, 0, :]
    gC = CONST["gC"]
    BK_L, BK_A0, BK_A1, BK_T, BK_KV, BK_O = 2, 3, 4, 5, 6, 7

    def h3(ap):
        return ap.rearrange("p (h d) -> p h d", h=4)

    def hTk(s):
        return [("hT", s % 2, c, t) for c in range(8) for t in range(4)]

    junkb = sb("junkb", [128, D], BF16)

    def N1(T):
        t = T % 4
        a = T % 2
        k.dma(xt[a][:], xp[T * 128:(T + 1) * 128, :], (), [("xt", a)])
        k.act(junkb[:], xt[a][:], AF.Square, [("xt", a)], ["junkb", ("ssb", t)], accum_out=ssb[:, t:t + 1])
        k.ts(msb[:, t:t + 1], ssb[:, t:t + 1], 1.0 / D, EPS, ALU.mult, ALU.add, [("ssb", t)], [("msb", t)])
        k.tt(rstd[:, t:t + 1], msb[:, t:t + 1], neg_half[:, 0:1], ALU.pow, [("msb", t), "neg_half"],
             [("rstd", t)], eng="pool")

    def N2(T):
        s, t = divmod(T, 4)
        a = T % 2
        tok = slice(t * 128, (t + 1) * 128)
        hTs = hT[s % 2]
        k.ts(xn[:], xt[a][:], rstd[:, t:t + 1], 0.0, ALU.mult, ALU.add, [("xt", a), ("rstd", t)], ["xn"], eng="pool")
        for half in range(2):
            b = half
            for c4 in range(4):
                c = half * 4 + c4
                k.tr(ps[b][:, c4 * 128:(c4 + 1) * 128], xn[:, c * 128:(c + 1) * 128], ident[:],
                     ["xn", "ident"], [("ps", b)])
            for c4 in range(4):
                c = half * 4 + c4
                if half == 0:
                    k.ts(hTs[:, c, tok], ps[b][:, c4 * 128:(c4 + 1) * 128], gsm[:, c:c + 1], shift_m[:, c:c + 1],
                         ALU.mult, ALU.add, [("ps", b), "gsm", "modfm"], [("hT", s % 2, c, t)])
                else:
                    k.act(hTs[:, c, tok], ps[b][:, c4 * 128:(c4 + 1) * 128], AF.Identity,
                          [("ps", b), "gsm", "modfm"], [("hT", s % 2, c, t)], scale=gsm[:, c:c + 1],
                          bias=shift_m[:, c:c + 1])

    def L1(s, hf):
        hTs = hT[s % 2]
        for ci in range(2):
            c = 2 * hf + ci
            bkx = ci
            for kc in range(8):
                k.mm(ps[bkx][:, :], w_in_sb[:, kc, 2048 + c * 128: 2048 + (c + 1) * 128], hTs[:, kc, :], kc == 0,
                     kc == 7, hTk(s) + [("w_in", kc)], [("ps", bkx)])
            for kc in range(8):
                k.mm(ps[BK_L][:, :], w_in_sb[:, kc, 2560 + c * 128: 2560 + (c + 1) * 128], hTs[:, kc, :], kc == 0,
                     kc == 7, hTk(s) + [("w_in", kc)], [("ps", BK_L)])
            k.act(xc[:, ci, :], ps[bkx][:, :], AF.Identity, [("ps", bkx), "vfm"], [("xc", ci)],
                  scale=clw[:, c * 4 + 3:c * 4 + 4], bias=clb[:, c:c + 1])
            k.act(gl[:, ci, :], ps[BK_L][:, :], AF.Gelu_apprx_tanh, [("ps", BK_L)], [("gl", ci)])
            for sh in (1, 2, 3):
                kk = 3 - sh
                wcol = clw[:, c * 4 + kk:c * 4 + kk + 1]
                k.stt(xc[:, ci, sh:512], ps[bkx][:, 0:512 - sh], wcol, xc[:, ci, sh:512], ALU.mult, ALU.add,
                      [("ps", bkx), ("xc", ci), "vfm"], [("xc", ci)])
                if s > 0:
                    k.stt(xc[:, ci, 0:sh], xcar[:, 3 - sh:3, c], wcol, xc[:, ci, 0:sh], ALU.mult, ALU.add,
                          [("xcar", c), ("xc", ci), "vfm"], [("xc", ci)])
            k.copy(xcar[:, :, c], ps[bkx][:, 509:512], [("ps", bkx)], [("xcar", c)])

    def L1c(s, hf):
        for ci in range(2):
            k.copy(xcb[:, ci, :], xc[:, ci, :], [("xc", ci)], [("xcb", ci)], eng="pool")

    def L2(s, hf):
        cts = catT[s % 2]
        for ci in range(2):
            c = 2 * hf + ci
            k.mm(ps[BK_L][:, :], wr_sb[:, c, :], xcb[:, ci, :], True, True, [("xcb", ci), "wr_sb"], [("ps", BK_L)])
            k.act(rr[:, ci, :], ps[BK_L][:, :], AF.Tanh, [("ps", BK_L), "hb_r"], [("rr", ci)], bias=hb_r[:, c:c + 1],
                  scale=0.5)
            k.mm(ps[ci][:, :], wi_sb[:, c, :], xcb[:, ci, :], True, True, [("xcb", ci), "wi_sb"], [("ps", ci)])
            k.act(ig[:, ci, :], ps[ci][:, :], AF.Tanh, [("ps", ci), "hb_i"], [("ig", ci)], bias=hb_i[:, c:c + 1],
                  scale=0.5)
        for ci in range(2):
            c = 2 * hf + ci
            k.act(aa[:, ci, :], rr[:, ci, :], AF.Exp, [("rr", ci), "hcf"], [("aa", ci)], scale=hcf[:, c:c + 1],
                  bias=hcf[:, c:c + 1])
            k.act(rr[:, ci, :], rr[:, ci, :], AF.Exp, [("rr", ci), "cf_fm"], [("rr", ci)], scale=cf_fm[:, c:c + 1],
                  bias=cf_fm[:, c:c + 1])
        for ci in range(2):
            k.act(rr[:, ci, :], rr[:, ci, :], AF.Sqrt, [("rr", ci)], [("rr", ci)], scale=-1.0, bias=1.0)

    def L2b(s, hf):
        cts = catT[s % 2]
        for ci in range(2):
            c = 2 * hf + ci
            if s == 0:
                k.memset(rr[:, ci, 0:1], 1.0, [("rr", ci)])
            k.stt(ig[:, ci, :], ig[:, ci, :], 1.0, xc[:, ci, :], ALU.add, ALU.mult, [("ig", ci), ("xc", ci)], [("ig", ci)])
            k.stt(ig[:, ci, :], ig[:, ci, :], 0.5, rr[:, ci, :], ALU.mult, ALU.mult, [("ig", ci), ("rr", ci)], [("ig", ci)])
            init = 0.0 if s == 0 else hcar[:, c:c + 1]
            S.add("dve", (lambda ci=ci, init=init: (lambda e: e.tensor_tensor_scan(
                xc[:, ci, :], aa[:, ci, :], ig[:, ci, :], init, ALU.mult, ALU.add)))(),
                [("aa", ci), ("ig", ci), ("hcar", c), ("xc", ci)], [("xc", ci)])
            k.copy(hcar[:, c:c + 1], xc[:, ci, 511:512], [("xc", ci)], [("hcar", c)])
            k.tt(cts[:, 4 + c, :], xc[:, ci, :], gl[:, ci, :], ALU.mult, [("xc", ci), ("gl", ci)],
                 [("catT", s % 2, 4 + c)])

    def A(T, js=(0, 1, 2, 3)):
        s, t = divmod(T, 4)
        a = T % 2
        tok = slice(t * 128, (t + 1) * 128)
        hTs = hT[s % 2]
        cosb = cosP[:, T:T + 1, :].broadcast_to([128, 4, 64])
        sinb = sinP[:, T:T + 1, :].broadcast_to([128, 4, 64])
        for j in js:
            bk = BK_A0 + j % 2
            for kc in range(8):
                k.mm(ps[bk][:, :], hTs[:, kc, tok], w_in_sb[:, kc, j * 512:(j + 1) * 512], kc == 0, kc == 7,
                     [("hT", s % 2, c_, t) for c_ in range(8)] + [("w_in", kc)], [("ps", bk)])
            p3 = h3(ps[bk][:, :])
            if j < 2:
                dst = qrot if j == 0 else krot
                nm = "q" if j == 0 else "k"
                m3 = [x[:, :].rearrange("p (h d) -> p h d", h=4) for x in mq]
                k.tt(m3[0], p3[:, :, 0:64], cosb, ALU.mult, [("ps", bk), "cosP"], [("qkm", 0)])
                k.tt(m3[1], p3[:, :, 64:128], sinb, ALU.mult, [("ps", bk), "sinP"], [("qkm", 1)])
                k.tt(m3[2], p3[:, :, 0:64], sinb, ALU.mult, [("ps", bk), "sinP"], [("qkm", 2)])
                k.tt(m3[3], p3[:, :, 64:128], cosb, ALU.mult, [("ps", bk), "cosP"], [("qkm", 3)])
                d3 = h3(dst[:, :])
                k.tt(d3[:, :, 0:64], m3[0], m3[1], ALU.subtract, [("qkm", 0), ("qkm", 1)], [(nm + "rot", 0)])
                k.tt(d3[:, :, 64:128], m3[2], m3[3], ALU.add, [("qkm", 2), ("qkm", 3)], [(nm + "rot", 1)])
                if j == 1:
                    for h in range(4):
                        k.ts(ktok[a][:, h * 128:(h + 1) * 128], krot[:, h * 128:(h + 1) * 128], dkP[:, h:h + 1], 0.0,
                             ALU.mult, ALU.add, [("krot", 0), ("krot", 1), "dkP"], [("ktok", a, h)], eng="pool")
            elif j == 2:
                k.copy(v_bf[a][:, :], ps[bk][:, :], [("ps", bk)], [("v_bf", a)], eng="act")

    def Ag_ev(T):
        a = T % 2
        bk = BK_A0 + 1
        k.act(sg[a][:, :], ps[bk][:, :], AF.Tanh, [("ps", bk)], [("sg", a)], scale=0.5)
        k.stt(sg[a][:, :], sg[a][:, :], 1.0, ps[bk][:, :], ALU.add, ALU.mult, [("ps", bk), ("sg", a)], [("sg", a)])

    def A2(T):
        for h in range(4):
            k.tr(ps[BK_T][:, h * 128:(h + 1) * 128], qrot[:, h * 128:(h + 1) * 128], ident[:],
                 [("qrot", 0), ("qrot", 1), "ident"], [("ps", BK_T)])
        for h in range(4):
            k.tr(ps[BK_KV][:, h * 128:(h + 1) * 128], krot[:, h * 128:(h + 1) * 128], ident[:],
                 [("krot", 0), ("krot", 1), "ident"], [("ps", BK_KV)])

    def A2ev(T):
        a = T % 2
        k.tt(qkT[a][:, 0:4, :], h3(ps[BK_T][:, :]), dqT[:], ALU.mult, [("ps", BK_T), "dqT"], [("qT", a)])
        k.tt(qkT[a][:, 4:8, :], h3(ps[BK_KV][:, :]), dkT[:], ALU.mult, [("ps", BK_KV), "dkT"], [("kT", a)])

    def B1(T):
        a = T % 2
        qk = qkT[a]
        for h in range(4):
            k.mm(ps[BK_T][:, h * 128:(h + 1) * 128], qk[:, 4 + h, :], qk[:, h, :], True, True,
                 [("qT", a), ("kT", a)], [("ps", BK_T)])
        k.tt(h3(PT[:, :]), h3(ps[BK_T][:, :]), maskT[:, :].unsqueeze(1).broadcast_to([128, 4, 128]), ALU.mult,
             [("ps", BK_T), "maskT"], ["PT"])
        for h in range(4):
            hs = slice(h * 128, (h + 1) * 128)
            k.mm(ps[BK_KV][:, hs], ktok[a][:, hs], v_bf[a][:, hs], True, True, [("ktok", a, h), ("v_bf", a)],
                 [("ps", BK_KV)])
        for h in range(4):
            hs = slice(h * 128, (h + 1) * 128)
            k.mm(ps[BK_O][:, hs], PT[:, hs], v_bf[a][:, hs], True, T == 0, ["PT", ("v_bf", a)], [("ps", BK_O)])
            if T > 0:
                k.mm(ps[BK_O][:, hs], qk[:, h, :], S_bf[:, hs], False, True, [("qT", a), ("S_bf", h)], [("ps", BK_O)])
        for h in range(4):
            hs = slice(h * 128, (h + 1) * 128)
            if T == 0:
                k.copy(Z[:, hs], ps[BK_KV][:, hs], [("ps", BK_KV)], [("Z", h)])
            else:
                k.stt(Z[:, hs], Z[:, hs], gC[h], ps[BK_KV][:, hs], ALU.mult, ALU.add, [("ps", BK_KV), ("Z", h)],
                      [("Z", h)])
            if T < NT - 1:
                k.ts(S_bf[:, hs], Z[:, hs], gC[h], None, ALU.mult, None, [("Z", h)], [("S_bf", h)])

    def B1b(T):
        a = T % 2
        for h in range(4):
            hs = slice(h * 128, (h + 1) * 128)
            S.add("dve", (lambda h=h, hs=hs: (lambda e: e.bn_stats(stats[:, h, :], ps[BK_O][:, hs])))(),
                  [("ps", BK_O)], [("stats", h)])
            S.add("dve", (lambda h=h: (lambda e: e.bn_aggr(mv[:, h, :], stats[:, h, :])))(),
                  [("stats", h)], [("mv", h)])
        mvk = [("mv", h) for h in range(4)]
        k.ts(vpe[:, :], mv[:, :, 1], EPS, 4.0, ALU.add, ALU.mult, mvk, ["vpe"])
        k.tt(rs[:, :], vpe[:, :], neg_half[:, :], ALU.pow, ["vpe", "neg_half"], ["rs"], eng="pool")

    def B1c(T):
        a = T % 2
        for h in range(4):
            hs = slice(h * 128, (h + 1) * 128)
            k.act(sgr[:, hs], sg[a][:, hs], AF.Copy, [("sg", a), "rs"], [("sgr", h)], scale=rs[:, h:h + 1])
            k.stt(ret[a][:, hs], ps[BK_O][:, hs], mv[:, h, 0:1], sgr[:, hs], ALU.subtract, ALU.mult,
                  [("ps", BK_O), ("mv", h), ("sgr", h)], [("ret", a, h)])
        if T == NT - 1:
            for h in range(4):
                hs = slice(h * 128, (h + 1) * 128)
                k.act(Sl[:, hs], Z[:, hs], AF.Copy, [("Z", h)], [("sgr", h)], scale=gC[h])
            k.dma(o_retp.rearrange("h k v -> k h v"), h3(Sl[:, :]), [("sgr", h_) for h_ in range(4)], ["o_retp"])

    def B2(T):
        s, t = divmod(T, 4)
        a = T % 2
        tok = slice(t * 128, (t + 1) * 128)
        pbf = ps[BK_KV][:, :].bitcast(BF16)
        for h in range(4):
            k.tr(pbf[:, h * 128:(h + 1) * 128], ret[a][:, h * 128:(h + 1) * 128], identb[:],
                 [("ret", a, h), "identb"], [("ps", BK_KV)])
        k.copy(catT[s % 2][:, 0:4, tok], pbf[:, 0:512].rearrange("p (h d) -> p h d", h=4), [("ps", BK_KV)],
               [("catTr", s % 2, t)])

    def O(T):
        s, t = divmod(T, 4)
        tok = slice(t * 128, (t + 1) * 128)
        cts = catT[s % 2]
        catk = [("catT", s % 2, 4 + c) for c in range(4)]
        k.dma(xr[:], xp[T * 128:(T + 1) * 128, :], (), ["xr"])
        for cb in range(2):
            for kc in range(8):
                k.mm(ps[cb][:, :], cts[:, kc, tok], w_out_sb[:, kc, cb * 512:(cb + 1) * 512], kc == 0, kc == 7,
                     catk + [("catTr", s % 2, t), ("w_out", kc)], [("ps", cb)])
            k.tt(x1t[:, cb * 512:(cb + 1) * 512], ps[cb][:, :], gate_m_row[:, cb * 512:(cb + 1) * 512],
                 ALU.mult, [("ps", cb), ("gate_m_row", 0), ("gate_m_row", 1)], [("x1t", cb)])

    def O2(T):
        k.tt(x1t[:, :], x1t[:, :], xr[:, :], ALU.add, [("x1t", 0), ("x1t", 1), "xr"], [("x1t", 0), ("x1t", 1)],
             eng="pool")
        k.dma(x1_scr[T * 128:(T + 1) * 128, :], x1t[:, :], [("x1t", 0), ("x1t", 1)], [("x1_scr", T)])

    def ok(T):
        return 0 <= T < NT

    def Lpiece(kind, idx):
        if not (0 <= idx < 2 * NST):
            return
        s, hf = divmod(idx, 2)
        {"L1": L1, "L2a": L2, "L2b": L2b, "L1c": L1c}[kind](s, hf)

    N1(0)
    N1(1)
    N2(0)
    N1(2)
    N2(1)
    N1(3)
    N2(2)
    N2(3)
    N1(4)
    for tau in range(NT + 6):
        if ok(tau):
            A(tau, (0, 1))
        if ok(tau - 1):
            A2ev(tau - 1)
        if ok(tau - 2):
            B1c(tau - 2)
        if ok(tau):
            A(tau, (2, 3))
        if ok(tau - 1):
            B1(tau - 1)
        if ok(tau):
            Ag_ev(tau)
        if ok(tau + 4) and tau + 4 >= 4:
            N2(tau + 4)
        if tau % 2 == 0:
            Lpiece("L2b", tau // 2 - 1)
            Lpiece("L1", tau // 2)
        else:
            Lpiece("L2a", tau // 2)
        if tau == 4 * NST:
            k.tr(ps[0][0:12, 0:128], xcar[:].rearrange("p k c -> p (k c)"), ident[:],
                 [("xcar", c) for c in range(4)] + ["ident"], [("ps", 0)])
            k.copy(st12[:, :], ps[0][0:12, 0:128], [("ps", 0)], ["st12"])
            k.dma(o_clp.rearrange("k (c p) -> (k c) p", p=128), st12[:, :], ["st12"], ["o_clp"])
            k.tr(ps[1][0:4, 0:128], hcar[:, :], ident[:], [("hcar", c) for c in range(4)] + ["ident"], [("ps", 1)])
            k.copy(st4[:, :], ps[1][0:4, 0:128], [("ps", 1)], ["st4"])
            k.dma(o_hp, st4[:, :], ["st4"], ["o_hp"])
        if ok(tau - 2):
            B2(tau - 2)
        if ok(tau - 5):
            O(tau - 5)
        if ok(tau - 1):
            B1b(tau - 1)
        if ok(tau + 5):
            N1(tau + 5)
        if ok(tau - 5):
            O2(tau - 5)
        if ok(tau):
            A2(tau)
        if tau % 2 == 0:
            Lpiece("L1c", tau // 2)


def prompt_ffn(nc, k, es, ps, ident, V, gsf, modfm, gatef_scr, rows_d, x1_scr, wuc_sb, wug_sb, wdn_sb, yp, o_cfp,
               wload):
    S = k.S

    def sb(name, shape, dt=F32):
        return es.enter_context(nc.sbuf_tensor("sb_" + name, list(shape), dt))

    gfin = sb("gfin", [128, D])
    k.dma(gfin[:], rows_d["g_final"].partition_broadcast(128), (), ["gfin"])
    gate_f_row = sb("gate_f_row", [128, D])
    k.dma(gate_f_row[:], gatef_scr[0].partition_broadcast(128), ["gatef_scr"], [("gate_f_row", 0), ("gate_f_row", 1)])
    neg_half = sb("neg_half2", [128, 1])
    k.memset(neg_half[:], -0.5, ["neg_half2"])
    xa = [sb("xa%d" % i, [128, D]) for i in range(2)]
    xb = [sb("xb0", [128, D])] * 2
    xn2 = [sb("xn2_0", [128, D])] * 2
    yt = sb("yt", [128, D]); yo = sb("yo", [128, D])
    ss = sb("ss2", [128, 4]); ms = sb("ms2", [128, 4]); rstd = sb("rstd2", [128, 4])
    ss3 = sb("ss3", [128, 2]); ms3 = sb("ms3", [128, 2]); rstd3 = sb("rstd3", [128, 2])
    h2T = sb("h2T", [128, 8, 512], BF16)
    aT = sb("aT", [128, NFF, 512], BF16)
    acc = [sb("acc%d" % i, [128, 512]) for i in range(2)]
    ucar = sb("ucar", [128, 2, NFF])
    st44 = sb("st44", [44, 128])
    junkb = sb("junkb2", [128, D], BF16)
    cfw = V("cfw"); cfb = V("cfb")
    shift_f = modfm[:, 2, :]

    def N2load(s, t):
        T = 4 * s + t
        a = T % 2
        k.dma(xa[a][:], x1_scr[T * 128:(T + 1) * 128, :], [("x1_scr", T)], [("xa", a)])

    def N2pre(s, t, load=True, part="all"):
        T = 4 * s + t
        a = T % 2
        if load:
            N2load(s, t)
        if part == "copy":
            k.act(xn2[a][:], xa[a][:], AF.Copy, [("xa", a), ("rstd2", t)], [("xn2", 0)], scale=rstd[:, t:t + 1])
            return
        k.act(junkb[:], xa[a][:], AF.Square, [("xa", a)], ["junkb2", ("ss2", t)], accum_out=ss[:, t:t + 1])
        k.ts(ms[:, t:t + 1], ss[:, t:t + 1], 1.0 / D, EPS, ALU.mult, ALU.add, [("ss2", t)], [("ms2", t)])
        k.tt(rstd[:, t:t + 1], ms[:, t:t + 1], neg_half[:, 0:1], ALU.pow, [("ms2", t), "neg_half2"],
             [("rstd2", t)], eng="pool")
        if part == "stats":
            return
        k.act(xn2[a][:], xa[a][:], AF.Copy, [("xa", a), ("rstd2", t)], [("xn2", 0)], scale=rstd[:, t:t + 1])

    def N2tr(s, t):
        T = 4 * s + t
        a = T % 2
        tok = slice(t * 128, (t + 1) * 128)
        for half in range(2):
            b = half
            for c4 in range(4):
                c = half * 4 + c4
                k.tr(ps[b][:, c4 * 128:(c4 + 1) * 128], xn2[a][:, c * 128:(c + 1) * 128], ident[:],
                     [("xn2", 0), "ident"], [("ps", b)])
            for c4 in range(4):
                c = half * 4 + c4
                if half == 0:
                    k.ts(h2T[:, c, tok], ps[b][:, c4 * 128:(c4 + 1) * 128], gsf[:, c:c + 1], shift_f[:, c:c + 1],
                         ALU.mult, ALU.add, [("ps", b), "gsf", "modfm"], [("h2T", c, t)])
                else:
                    k.act(h2T[:, c, tok], ps[b][:, c4 * 128:(c4 + 1) * 128], AF.Identity,
                          [("ps", b), "gsf", "modfm"], [("h2T", c, t)], scale=gsf[:, c:c + 1],
                          bias=shift_f[:, c:c + 1])

    h2k = [("h2T", c, t) for c in range(8) for t in range(4)]
    aTk = [("aT", m) for m in range(NFF)]

    def UP(s):
        for m in range(NFF):
            bu = 2 + 2 * (m % 2)
            bg = bu + 1
            ms_ = slice(m * 128, (m + 1) * 128)
            for kc in range(8):
                k.mm(ps[bu][:, :], wuc_sb[:, kc, ms_], h2T[:, kc, :], kc == 0, kc == 7, h2k + [("wuc", kc, m // 8)], [("ps", bu)])
            for kc in range(8):
                k.mm(ps[bg][:, :], wug_sb[:, kc, ms_], h2T[:, kc, :], kc == 0, kc == 7, h2k + [("wug", kc, m // 8)], [("ps", bg)])
            ac = acc[m % 2]
            ak = ("acc", m % 2)
            k.act(ac[:, :], ps[bu][:, :], AF.Identity, [("ps", bu), "vfm"], [ak], scale=cfw[:, m * 3 + 2:m * 3 + 3],
                  bias=cfb[:, m:m + 1])
            for sh in (1, 2):
                kk = 2 - sh
                wcol = cfw[:, m * 3 + kk:m * 3 + kk + 1]
                k.stt(ac[:, sh:512], ps[bu][:, 0:512 - sh], wcol, ac[:, sh:512], ALU.mult, ALU.add,
                      [("ps", bu), ak, "vfm"], [ak])
                if s > 0:
                    k.stt(ac[:, 0:sh], ucar[:, 2 - sh:2, m], wcol, ac[:, 0:sh], ALU.mult, ALU.add,
                          [("ucar", m), ak, "vfm"], [ak])
            k.copy(ucar[:, :, m], ps[bu][:, 510:512], [("ps", bu)], [("ucar", m)])
            k.act(ac[:, :], ac[:, :], AF.Gelu_apprx_tanh, [ak], [ak])
            k.tt(aT[:, m, :], ac[:, :], ps[bg][:, :], ALU.mult, [ak, ("ps", bg)], [("aT", m)])

    def DOWN(s, t):
        if True:
            T = 4 * s + t
            a = T % 2
            tok = slice(t * 128, (t + 1) * 128)
            k.dma(xb[a][:], x1_scr[T * 128:(T + 1) * 128, :], [("x1_scr", T)], [("xb", 0)])
            for cb in range(2):
                for m in range(NFF):
                    k.mm(ps[6 + cb][:, :], aT[:, m, tok], wdn_sb[:, m, cb * 512:(cb + 1) * 512], m == 0, m == NFF - 1,
                         aTk + [("wdn", m)], [("ps", 6 + cb)])
                k.tt(yt[:, cb * 512:(cb + 1) * 512], ps[6 + cb][:, :], gate_f_row[:, cb * 512:(cb + 1) * 512], ALU.mult,
                     [("ps", 6 + cb), ("gate_f_row", 0), ("gate_f_row", 1)], [("yt", cb)])

    def DOWN2(s, t):
        if True:
            T = 4 * s + t
            a = T % 2
            k.tt(yt[:, :], yt[:, :], xb[a][:, :], ALU.add, [("yt", 0), ("yt", 1), ("xb", 0)], [("yt", 0), ("yt", 1)],
                 eng="pool")
            k.act(yo[:, :], yt[:, :], AF.Square, [("yt", 0), ("yt", 1)], ["yo", ("ss3", a)], accum_out=ss3[:, a:a + 1])
            k.ts(ms3[:, a:a + 1], ss3[:, a:a + 1], 1.0 / D, EPS, ALU.mult, ALU.add, [("ss3", a)], [("ms3", a)])
            k.tt(rstd3[:, a:a + 1], ms3[:, a:a + 1], neg_half[:, 0:1], ALU.pow, [("ms3", a), "neg_half2"],
                 [("rstd3", a)], eng="pool")
            k.stt(yo[:, :], yt[:, :], rstd3[:, a:a + 1], gfin[:, :], ALU.mult, ALU.mult,
                  [("yt", 0), ("yt", 1), ("rstd3", a), "gfin"], ["yo"])
            k.dma(yp[T * 128:(T + 1) * 128, :], yo[:, :], ["yo"], [("yp", T)])

    N2load(0, 0)
    N2load(0, 1)
    wload[0]()
    N2pre(0, 0, load=False, part="stats")
    N2pre(0, 1, load=False, part="stats")
    for t in range(4):
        N2pre(0, t, load=False, part="copy")
        N2tr(0, t)
        if t + 2 < 4:
            N2load(0, t + 2)
            N2pre(0, t + 2, load=False, part="stats")
    wload[1]()
    wload[2]()
    wload[3]()
    for s in range(NST):
        UP(s)
        if s == NST - 1:
            k.tr(ps[0][0:44, 0:128], ucar[:].rearrange("p k m -> p (k m)"), ident[:],
                 [("ucar", m) for m in range(NFF)] + ["ident"], [("ps", 0)])
            k.copy(st44[:, :], ps[0][0:44, 0:128], [("ps", 0)], ["st44"])
            k.dma(o_cfp.rearrange("k (m p) -> (k m) p", p=128), st44[:, :], ["st44"], ["o_cfp"])
        for t in range(4):
            if s + 1 < NST:
                N2pre(s + 1, t)
            DOWN(s, t)
            if s + 1 < NST:
                N2tr(s + 1, t)
            DOWN2(s, t)


def _rms_rstd(k, sbf, pfx, x, junk):
    ss = sbf(pfx + "_ss", [NS, 1]); ms = sbf(pfx + "_ms", [NS, 1]); rstd = sbf(pfx + "_rstd", [NS, 1])
    nh = sbf(pfx + "_nh", [NS, 1])
    k.memset(nh[:], -0.5, [pfx + "nh"])
    k.act(junk, x, AF.Square, [pfx + "x"], [pfx + "junk", pfx + "ss"], accum_out=ss[:, 0:1])
    k.ts(ms[:], ss[:], 1.0 / D, EPS, ALU.mult, ALU.add, [pfx + "ss"], [pfx + "ms"])
    k.tt(rstd[:], ms[:], nh[:], ALU.pow, [pfx + "ms", pfx + "nh"], [pfx + "rstd"], eng="pool")
    return rstd


def _to_fm(k, ps_bank, bank_id, src, nchunk, dst, ident, rkeys, wkey):
    for c in range(nchunk):
        k.tr(ps_bank[:, c * NS:(c + 1) * NS], src[0:NS, c * 128:(c + 1) * 128], ident[0:NS, 0:NS],
             rkeys + ["ident"], [("ps", bank_id)])
    k.copy(dst[:].rearrange("p a b -> p (a b)"), ps_bank[:, 0:nchunk * NS], [("ps", bank_id)], [wkey])


def sample_a(nc, k, sbf, ps, ident, mod, modkeys, cd, w_in_sb, early):
    x = early["x"]; grow = early["cat"]; proj = early["proj"]; qr = early["qr"]; kr = early["kr"]
    xn = sbf("sm_xn", [NS, D]); gs = sbf("sm_gs", [NS, D])
    rstd = _rms_rstd(k, sbf, "sm", x[:], gs[:])
    k.act(xn[:], x[:], AF.Copy, ["smx", "smrstd"], ["sm_xn"], scale=rstd[:, 0:1])
    k.stt(gs[:], mod[0:NS, D:2 * D], 1.0, grow[:], ALU.add, ALU.mult, modkeys + ["sm_grow", "smjunk"], ["sm_gs", "smjunk"])
    k.tt(xn[:], xn[:], gs[:], ALU.mult, ["sm_xn", "sm_gs"], ["sm_xn"])
    k.tt(xn[:], xn[:], mod[0:NS, 0:D], ALU.add, ["sm_xn"] + modkeys, ["sm_xn"])
    hT = sbf("sm_hT", [128, 8, NS], BF16)
    _to_fm(k, ps[0], 0, xn, 8, hT, ident, ["sm_xn"], "sm_hT")
    for cb in range(6):
        b = 1 + cb % 2
        for kc in range(8):
            k.mm(ps[b][0:NS, :], hT[:, kc, :], w_in_sb[:, kc, cb * 512:(cb + 1) * 512], kc == 0, kc == 7,
                 ["sm_hT", ("w_in", kc)], [("ps", b)])
        k.copy(proj[:, cb * 512:(cb + 1) * 512], ps[b][0:NS, :], [("ps", b)], [("sm_proj", cb)], eng="act")

    def p3(cb):
        return proj[:, cb * 512:(cb + 1) * 512].rearrange("p (h d) -> p h d", h=4)

    rot = sbf("sm_rot", [NS, 4, 64])
    k.dma(rot[:], cd["rotS"].partition_broadcast(NS), (), ["sm_rot"])
    mt = [sbf("sm_m%d" % i, [NS, 4, 64]) for i in range(4)]
    for j, dst in enumerate([qr, kr]):
        src3 = p3(j)
        cosb = rot[:, 2 * j:2 * j + 1, :].broadcast_to([NS, 4, 64])
        sinb = rot[:, 2 * j + 1:2 * j + 2, :].broadcast_to([NS, 4, 64])
        d3 = dst[:, :].rearrange("p (h d) -> p h d", h=4)
        k.tt(mt[0][:], src3[:, :, 0:64], cosb, ALU.mult, [("sm_proj", j), "sm_rot"], [("sm_m", 0)])
        k.tt(mt[1][:], src3[:, :, 64:128], sinb, ALU.mult, [("sm_proj", j), "sm_rot"], [("sm_m", 1)])
        k.tt(mt[2][:], src3[:, :, 0:64], sinb, ALU.mult, [("sm_proj", j), "sm_rot"], [("sm_m", 2)])
        k.tt(mt[3][:], src3[:, :, 64:128], cosb, ALU.mult, [("sm_proj", j), "sm_rot"], [("sm_m", 3)])
        k.tt(d3[:, :, 0:64], mt[0][:], mt[1][:], ALU.subtract, [("sm_m", 0), ("sm_m", 1)], [("sm_rot_o", j, 0)])
        k.tt(d3[:, :, 64:128], mt[2][:], mt[3][:], ALU.add, [("sm_m", 2), ("sm_m", 3)], [("sm_rot_o", j, 1)])


def sample_mixer(nc, k, es, ps, ident, xs, mod, modkeys, rows_d, cd, w_in_sb, w_out_sb, wr_sb, wi_sb,
                 st_ret, st_h, st_cl, o_rets, o_hs, o_cls, x1s_scr, early):
    S = k.S

    def sb(name, shape, dt=F32):
        return es.enter_context(nc.sbuf_tensor("sb_" + name, list(shape), dt))

    g1 = CONST["g1"]
    x = early["x"]; cat = early["cat"]; proj = early["proj"]; qr = early["qr"]; kr = early["kr"]
    S_s = sb("sm_S", [128, NS, 4, 128])
    sv = st_ret.rearrange("b h k v -> k b h v")
    for h_ in range(4):
        k.dma(S_s[:, :, h_, :], sv[:, :, h_, :], (), [("sm_S_in", h_)])
    es = ExitStack()
    es.__enter__()

    def p3(cb):
        return proj[:, cb * 512:(cb + 1) * 512].rearrange("p (h d) -> p h d", h=4)

    qrk = []
    krk = []
    stc = early["stc"]; clb = early["clb"]; brr = early["brr"]; bir = early["bir"]; lamr = early["lamr"]
    clw = sb("sm_clw", [NS, 4, 512]); h0 = sb("sm_h0", [NS, 512])
    k.dma(clw[:], rows_d["clw"].partition_broadcast(NS), (), ["sm_clw"])
    k.dma(h0[:], st_h, (), ["sm_h0"])
    xl = proj[:, 2048:2560]
    k.dma(o_cls[:, 0:2, :], stc[:, 1:3, :], ["sm_stc"], ["o_cls01"])
    k.dma(o_cls[:, 2, :], xl, [("sm_proj", 4)], ["o_cls2"])
    xcs = sb("sm_xcs", [NS, 512]); t2 = sb("sm_t2", [NS, 512])
    k.tt(xcs[:], xl, clw[:, 3, :], ALU.mult, [("sm_proj", 4), "sm_clw"], ["sm_xcs"])
    k.tt(xcs[:], xcs[:], clb[:], ALU.add, ["sm_xcs", "sm_clb"], ["sm_xcs"])
    for kk in range(3):
        k.tt(t2[:], stc[:, kk, :], clw[:, kk, :], ALU.mult, ["sm_stc", "sm_clw"], ["sm_t2"])
        k.tt(xcs[:], xcs[:], t2[:], ALU.add, ["sm_xcs", "sm_t2"], ["sm_xcs"])
    xcT = sb("sm_xcT", [128, 4, NS], BF16)
    _to_fm(k, ps[3], 3, xcs, 4, xcT, ident, ["sm_xcs"], "sm_xcT")
    for c in range(4):
        cs_ = slice(c * 128, (c + 1) * 128)
        k.mm(ps[1][0:NS, cs_], xcT[:, c, :], wr_sb[:, c, :], True, True, ["sm_xcT", "wr_sb"], [("ps", 1)])
        k.mm(ps[2][0:NS, cs_], xcT[:, c, :], wi_sb[:, c, :], True, True, ["sm_xcT", "wi_sb"], [("ps", 2)])
    rg = sb("sm_rg", [NS, 512]); igs = sb("sm_ig", [NS, 512]); cfr = lamr; av = sb("sm_a", [NS, 512])
    k.tt(rg[:], ps[1][0:NS, :], brr[:], ALU.add, [("ps", 1), "sm_brr"], ["sm_rg"])
    k.tt(igs[:], ps[2][0:NS, :], bir[:], ALU.add, [("ps", 2), "sm_bir"], ["sm_ig"])
    k.act(rg[:], rg[:], AF.Sigmoid, ["sm_rg"], ["sm_rg"])
    k.act(igs[:], igs[:], AF.Sigmoid, ["sm_ig"], ["sm_ig"])
    k.stt(rg[:], cfr[:], -8.0, rg[:], ALU.mult, ALU.mult, ["sm_rg"], ["sm_rg"])
    k.act(av[:], rg[:], AF.Exp, ["sm_rg"], ["sm_a"])
    k.act(rg[:], rg[:], AF.Exp, ["sm_rg"], ["sm_rg"], scale=2.0)
    k.act(rg[:], rg[:], AF.Sqrt, ["sm_rg"], ["sm_rg"], scale=-1.0, bias=1.0)
    k.tt(igs[:], igs[:], xcs[:], ALU.mult, ["sm_ig", "sm_xcs"], ["sm_ig"])
    k.tt(igs[:], igs[:], rg[:], ALU.mult, ["sm_ig", "sm_rg"], ["sm_ig"])
    k.tt(av[:], av[:], h0[:], ALU.mult, ["sm_a", "sm_h0"], ["sm_a"])
    k.tt(av[:], av[:], igs[:], ALU.add, ["sm_a", "sm_ig"], ["sm_a"])
    k.dma(o_hs, av[:], ["sm_a"], ["o_hs"])
    k.act(t2[:], proj[:, 2560:3072], AF.Gelu_apprx_tanh, [("sm_proj", 5), "sm_t2"], ["sm_t2"])
    k.tt(cat[:, 512:1024], av[:], t2[:], ALU.mult, ["sm_a", "sm_t2"], [("sm_cat", 1)])
    barrier(S, keep=lambda kk: isinstance(kk, tuple) and kk[0] in ("o_rets", "sm_S_out", "sm_S_in"))
    es.__exit__(None, None, None)
    es = ExitStack()
    es.__enter__()
    tmp = sb("sm_tmp", [NS, 512]); qk = sb("sm_qk", [NS, 4]); o1 = sb("sm_o1", [NS, 512]); osb = sb("sm_o", [NS, 512])
    k.tt(tmp[:], qr[:], kr[:], ALU.mult, qrk + krk, ["sm_tmp"])
    S.add("dve", lambda e: e.tensor_reduce(qk[:], tmp[:, :].rearrange("p (h d) -> p h d", h=4), AX.X, ALU.add),
          ["sm_tmp"], ["sm_qk"])
    k.tt(o1[:, :].rearrange("p (h d) -> p h d", h=4), p3(2), qk[:, :].unsqueeze(2).broadcast_to([NS, 4, 128]),
         ALU.mult, [("sm_proj", 2), "sm_qk"], ["sm_o1"])
    Sk = [("sm_S_in", g) for g in range(4)]
    i16 = sb("sm_i16", [128, NS, NS])
    k.dma(i16[:], cd["i16"], (), ["sm_i16"])
    qT = sb("sm_qT", [128, 4, NS])
    _to_fm(k, ps[3], 3, qr, 4, qT, ident, qrk, "sm_qT")
    QM = sb("sm_QM", [128, 4, NS, NS])
    for h in range(4):
        k.tt(QM[:, h, :, :], qT[:, h, :].unsqueeze(2).broadcast_to([128, NS, NS]), i16[:], ALU.mult,
             ["sm_qT", "sm_i16"], [("sm_QM", h)])
    Vm = [sb("sm_Vm%d" % i, [NS, NS, 128], BF16) for i in range(4)]
    kr_bf = sb("sm_kr_bf", [NS, 512], BF16)
    k.copy(kr_bf[:], kr[:], krk, ["sm_kr_bf"], eng="act")
    for h in range(4):
        k.tt(Vm[h][:], p3(2)[:, h, :].unsqueeze(1).broadcast_to([NS, NS, 128]),
             ident[0:NS, 0:NS].unsqueeze(2).broadcast_to([NS, NS, 128]), ALU.mult,
             [("sm_proj", 2), "ident"], [("sm_Vm", h)])
    for h in range(4):
        for b in range(NS):
            k.mm(ps[4][0:NS, h * 128:(h + 1) * 128], QM[:, h, b, :], S_s[:, b, h, :], b == 0, b == NS - 1,
                 [("sm_QM", h), ("sm_S_in", h)], [("ps", 4)])
    for h in range(4):
        hs = slice(h * 128, (h + 1) * 128)
        k.stt(osb[:, hs], ps[4][0:NS, hs], g1[h], o1[:, hs], ALU.mult, ALU.add, [("ps", 4), "sm_o1"], [("sm_o", h)])
    v3 = p3(2)
    for h in range(4):
        vm = Vm[h]
        for g in range(4):
            bk = 4 + g if h % 2 == 0 else g
            k.mm(ps[bk][:, :], kr_bf[0:NS, h * 128:(h + 1) * 128], vm[0:NS, 4 * g:4 * g + 4, :], True, True,
                 ["sm_kr_bf", ("sm_Vm", h)], [("ps", bk)])
            k.stt(S_s[:, 4 * g:4 * g + 4, h, :], S_s[:, 4 * g:4 * g + 4, h, :], g1[h],
                  ps[bk][:, :].rearrange("p (b v) -> p b v", b=4), ALU.mult, ALU.add,
                  [("ps", bk)] + Sk, [("sm_S_out", g, h)])
    stats = sb("sm_stats", [NS, 4, 6]); mv = sb("sm_mv", [NS, 4, 2]); vpe = sb("sm_vpe", [NS, 4]); rs = sb("sm_rs", [NS, 4])
    nh4 = sb("sm_nh4", [NS, 4])
    k.memset(nh4[:], -0.5, ["sm_nh4"])
    for h in range(4):
        hs = slice(h * 128, (h + 1) * 128)
        S.add("dve", (lambda h=h, hs=hs: (lambda e: e.bn_stats(stats[:, h, :], osb[:, hs])))(), [("sm_o", h)],
              [("sm_stats", h)])
        S.add("dve", (lambda h=h: (lambda e: e.bn_aggr(mv[:, h, :], stats[:, h, :])))(), [("sm_stats", h)],
              [("sm_mv", h)])
    mvk = [("sm_mv", h) for h in range(4)]
    k.ts(vpe[:], mv[:, :, 1], EPS, None, ALU.add, None, mvk, ["sm_vpe"])
    k.tt(rs[:], vpe[:], nh4[:], ALU.pow, ["sm_vpe", "sm_nh4"], ["sm_rs"], eng="pool")
    sgs = sb("sm_sgs", [NS, 512])
    k.act(sgs[:], proj[:, 1536:2048], AF.Silu, [("sm_proj", 3)], ["sm_sgs"])
    for h in range(4):
        hs = slice(h * 128, (h + 1) * 128)
        k.ts(osb[:, hs], osb[:, hs], mv[:, h, 0:1], rs[:, h:h + 1], ALU.subtract, ALU.mult,
             [("sm_o", h), ("sm_mv", h), "sm_rs"], [("sm_o", h)])
    k.tt(cat[:, 0:512], osb[:], sgs[:], ALU.mult, [("sm_o", h) for h in range(4)] + ["sm_sgs"], [("sm_cat", 0)])
    ov = o_rets.rearrange("b h k v -> k b h v")
    for g in range(4):
        k.dma(ov[:, 4 * g:4 * g + 4, :, :], S_s[:, 4 * g:4 * g + 4, :, :], [("sm_S_out", g, h) for h in range(4)],
              [("o_rets", g)])
    catT = sb("sm_catT", [128, 8, NS], BF16)
    _to_fm(k, ps[0], 0, cat, 8, catT, ident, [("sm_cat", 0), ("sm_cat", 1)], "sm_catT")
    x1 = sb("sm_x1", [NS, D])
    for cb in range(2):
        b = 1 + cb
        for kc in range(8):
            k.mm(ps[b][0:NS, :], catT[:, kc, :], w_out_sb[:, kc, cb * 512:(cb + 1) * 512], kc == 0, kc == 7,
                 ["sm_catT", ("w_out", kc)], [("ps", b)])
        k.tt(x1[:, cb * 512:(cb + 1) * 512], ps[b][0:NS, :], mod[0:NS, 2 * D + cb * 512:2 * D + (cb + 1) * 512], ALU.mult,
             [("ps", b)] + modkeys, [("sm_x1", cb)])
    k.tt(x1[:], x1[:], x[:], ALU.add, [("sm_x1", 0), ("sm_x1", 1), "smx"], [("sm_x1", 0), ("sm_x1", 1)])
    k.dma(x1s_scr, x1[:], [("sm_x1", 0), ("sm_x1", 1)], ["x1s_scr"])
    barrier(S)
    es.__exit__(None, None, None)


def sample_ffn(nc, k, es, ps, ident, x1s_scr, modf_scr, rows_d, wuc_sb, wug_sb, wdn_sb, st_cf, o_cfs, ys):
    S = k.S

    def sb(name, shape, dt=F32):
        return es.enter_context(nc.sbuf_tensor("sb_" + name, list(shape), dt))

    x1 = sb("sf_x1", [NS, D]); modf = sb("sf_modf", [NS, 3 * D]); xn = sb("sf_xn", [NS, D]); junk = xn
    grow = sb("sf_grow", [NS, D]); gs = grow; gfin = sb("sf_gfin", [NS, D])
    k.dma(x1[:], x1s_scr, ["x1s_scr"], ["sfx"])
    k.dma(modf[:], modf_scr, ["modf_scr"], ["sf_modf"])
    k.dma(grow[:], rows_d["g_ffn"].partition_broadcast(NS), (), ["sf_grow"])
    k.dma(gfin[:], rows_d["g_final"].partition_broadcast(NS), (), ["sf_gfin"])
    rstd = _rms_rstd(k, sb, "sf", x1[:], junk[:])
    k.act(xn[:], x1[:], AF.Copy, ["sfx", "sfrstd"], ["sf_xn", "sfjunk"], scale=rstd[:, 0:1])
    k.stt(gs[:], modf[:, D:2 * D], 1.0, grow[:], ALU.add, ALU.mult, ["sf_modf", "sf_grow"], ["sf_grow"])
    k.tt(xn[:], xn[:], gs[:], ALU.mult, ["sf_xn", "sf_grow"], ["sf_xn"])
    k.tt(xn[:], xn[:], modf[:, 0:D], ALU.add, ["sf_xn", "sf_modf"], ["sf_xn"])
    hT = sb("sf_hT", [128, 8, NS], BF16)
    _to_fm(k, ps[0], 0, xn, 8, hT, ident, ["sf_xn"], "sf_hT")
    aT = sb("sf_aT", [128, NFF, NS], BF16)
    cfw_v = rows_d["cfw"].rearrange("(k f) -> k f", k=3)
    blocks = [(0, 512), (512, 512), (1024, 512), (1536, 512), (2048, 512), (2560, 256)]
    bufs = {}
    for a in range(2):
        bufs[a] = dict(
            cfw=sb("sf_cfw%d" % a, [NS, 3, 512]), cfb=sb("sf_cfb%d" % a, [NS, 512]), stf=sb("sf_stf%d" % a, [NS, 2, 512]),
            u=sb("sf_u%d" % a, [NS, 512]), uc=sb("sf_uc%d" % a, [NS, 512]), t=sb("sf_t%d" % a, [NS, 512]))

    def stage1(bi):
        c0, wd = blocks[bi]
        a = bi % 2
        B = bufs[a]
        cs_ = slice(c0, c0 + wd)
        k.dma(B["cfw"][:, :, 0:wd], cfw_v[:, cs_].partition_broadcast(NS), (), [("sf_cfw", a)])
        k.dma(B["cfb"][:, 0:wd], rows_d["cfb"][cs_].partition_broadcast(NS), (), [("sf_cfb", a)])
        k.dma(B["stf"][:, :, 0:wd], st_cf[:, :, cs_], (), [("sf_stf", a)])
        bu, bg = 1 + 2 * a, 2 + 2 * a
        for kc in range(8):
            k.mm(ps[bu][0:NS, 0:wd], hT[:, kc, :], wuc_sb[:, kc, cs_], kc == 0, kc == 7,
                 ["sf_hT", ("wuc", kc, c0 // 1024)], [("ps", bu)])
        for kc in range(8):
            k.mm(ps[bg][0:NS, 0:wd], hT[:, kc, :], wug_sb[:, kc, cs_], kc == 0, kc == 7,
                 ["sf_hT", ("wug", kc, c0 // 1024)], [("ps", bg)])
        tt_ = B["t"]
        for kk in range(2):
            k.tt(tt_[:, 0:wd], B["stf"][:, kk, 0:wd], B["cfw"][:, kk, 0:wd], ALU.mult, [("sf_stf", a), ("sf_cfw", a)],
                 [("sf_t", a)])
            k.tt(B["cfb"][:, 0:wd], B["cfb"][:, 0:wd], tt_[:, 0:wd], ALU.add, [("sf_cfb", a), ("sf_t", a)],
                 [("sf_cfb", a)])
        k.dma(o_cfs[:, 0, cs_], B["stf"][:, 1, 0:wd], [("sf_stf", a)], [("o_cfs0", bi)], eng="act")

    def stage2(bi):
        c0, wd = blocks[bi]
        a = bi % 2
        B = bufs[a]
        cs_ = slice(c0, c0 + wd)
        bu, bg = 1 + 2 * a, 2 + 2 * a
        u = B["u"]; uc = B["uc"]
        k.tt(uc[:, 0:wd], ps[bu][0:NS, 0:wd], B["cfw"][:, 2, 0:wd], ALU.mult, [("ps", bu), ("sf_cfw", a)], [("sf_uc", a)])
        k.copy(u[:, 0:wd], ps[bu][0:NS, 0:wd], [("ps", bu)], [("sf_u", a)], eng="act")
        k.tt(uc[:, 0:wd], uc[:, 0:wd], B["cfb"][:, 0:wd], ALU.add, [("sf_uc", a), ("sf_cfb", a)], [("sf_uc", a)])
        k.act(uc[:, 0:wd], uc[:, 0:wd], AF.Gelu_apprx_tanh, [("sf_uc", a)], [("sf_uc", a)])
        k.tt(uc[:, 0:wd], uc[:, 0:wd], ps[bg][0:NS, 0:wd], ALU.mult, [("sf_uc", a), ("ps", bg)], [("sf_uc", a)])
        k.dma(o_cfs[:, 1, cs_], u[:, 0:wd], [("sf_u", a)], [("o_cfs1", bi)], eng="act")
        nchk = wd // 128
        for c in range(nchk):
            k.tr(ps[5 + a][:, c * NS:(c + 1) * NS], uc[0:NS, c * 128:(c + 1) * 128], ident[0:NS, 0:NS],
                 [("sf_uc", a), "ident"], [("ps", 5 + a)])
        m0 = c0 // 128
        k.copy(aT[:, m0:m0 + nchk, :].rearrange("p a b -> p (a b)"), ps[5 + a][:, 0:nchk * NS], [("ps", 5 + a)],
               [("sf_aT", bi)])

    stage1(0)
    for bi in range(6):
        if bi + 1 < 6:
            stage1(bi + 1)
        stage2(bi)
    aTk = [("sf_aT", bi) for bi in range(6)]
    y = xn
    for cb in range(2):
        b = 1 + cb
        for m in range(NFF):
            k.mm(ps[b][0:NS, :], aT[:, m, :], wdn_sb[:, m, cb * 512:(cb + 1) * 512], m == 0, m == NFF - 1,
                 aTk + [("wdn", m)], [("ps", b)])
        k.tt(y[:, cb * 512:(cb + 1) * 512], ps[b][0:NS, :], modf[:, 2 * D + cb * 512:2 * D + (cb + 1) * 512], ALU.mult,
             [("ps", b), "sf_modf"], [("sf_y", cb), "sf_xn"])
    yk = [("sf_y", 0), ("sf_y", 1)]
    k.tt(y[:], y[:], x1[:], ALU.add, yk + ["sfx"], yk)
    ss = sb("sf_ss2", [NS, 1]); ms = sb("sf_ms2", [NS, 1]); r2 = sb("sf_r2", [NS, 1]); nh = sb("sf_nh2", [NS, 1])
    k.memset(nh[:], -0.5, ["sf_nh2"])
    k.act(grow[:], y[:], AF.Square, yk + ["sf_grow"], ["sf_grow", "sf_ss2"], accum_out=ss[:, 0:1])
    k.ts(ms[:], ss[:], 1.0 / D, EPS, ALU.mult, ALU.add, ["sf_ss2"], ["sf_ms2"])
    k.tt(r2[:], ms[:], nh[:], ALU.pow, ["sf_ms2", "sf_nh2"], ["sf_r2"], eng="pool")
    k.stt(y[:], y[:], r2[:, 0:1], gfin[:], ALU.mult, ALU.mult, yk + ["sf_r2", "sf_gfin"], yk)
    k.dma(ys, y[:], yk, ["ys"])


_NC_CACHE = {}


def _blockdiag(w):
    out = np.zeros((128, 4, 128), np.float32)
    for n in range(8):
        c, hh = n // 2, n % 2
        out[hh * 64:(hh + 1) * 64, c, hh * 64:(hh + 1) * 64] = w[n]
    return out


def _fm(v):
    return np.ascontiguousarray(v.reshape(-1, 128).T)


def kernel(x_prompt, x_sample, c_prompt, c_sample, state_ret, state_lru_h, state_lru_conv, state_ffn_conv,
           w_ada, b_ada, g_mix, w_in, conv_lru_w, conv_lru_b, w_r, b_r, w_i, b_i, lam, w_out, g_ffn,
           w_up_conv, w_up_gate, conv_ffn_w, conv_ffn_b, w_down, g_final):
    f = lambda a: np.ascontiguousarray(np.asarray(a, dtype=np.float32))
    if "nc" not in _NC_CACHE:
        _NC_CACHE["nc"] = build_program()
    nc = _NC_CACHE["nc"]
    clw_fm = np.stack([_fm(f(conv_lru_w)[0, kk]) for kk in range(4)], axis=2).reshape(128, 16)
    cfw_fm = np.stack([_fm(f(conv_ffn_w)[0, kk]) for kk in range(3)], axis=2).reshape(128, 66)
    vfm = np.concatenate([_fm(f(g_mix)[0]), _fm(f(g_ffn)[0]), clw_fm, _fm(f(conv_lru_b)[0]), _fm(f(b_r)[0]),
                          _fm(f(b_i)[0]), _fm(f(lam)[0]), cfw_fm, _fm(f(conv_ffn_b)[0])], axis=1)
    shared = {
        "w_ada": f(w_ada)[0], "b_ada": f(b_ada)[0], "w_in": f(w_in)[0], "w_out": f(w_out)[0],
        "w_upc": f(w_up_conv)[0], "w_upg": f(w_up_gate)[0], "w_dn": f(w_down)[0],
        "wr_bd": _blockdiag(f(w_r)[0]), "wi_bd": _blockdiag(f(w_i)[0]), "vfm": np.ascontiguousarray(vfm),
        "row_g_mix": f(g_mix)[0], "row_g_ffn": f(g_ffn)[0], "row_g_final": f(g_final),
        "row_clw": f(conv_lru_w)[0].reshape(-1), "row_clb": f(conv_lru_b)[0], "row_b_r": f(b_r)[0],
        "row_b_i": f(b_i)[0], "row_lam": f(lam)[0], "row_cfw": f(conv_ffn_w)[0].reshape(-1),
        "row_cfb": f(conv_ffn_b)[0],
    }
    for n in ["ident", "maskT", "cosP", "sinP", "dqT", "dkT", "dkP", "rotS", "sel", "i16"]:
        shared["c_" + n] = CONST[n]
    in_maps = []
    for i in range(N_CORES):
        sl = slice(i * NS, (i + 1) * NS)
        m = dict(shared)
        m["xp"] = f(x_prompt)[i]
        m["xs"] = f(x_sample)[sl, 0, :]
        m["c17"] = np.concatenate([f(c_sample)[sl], f(c_prompt)[i:i + 1]], axis=0)
        m["st_ret"] = f(state_ret)[0, sl]
        m["st_h"] = f(state_lru_h)[0, sl]
        m["st_cl"] = f(state_lru_conv)[0, sl]
        m["st_cf"] = f(state_ffn_conv)[0, sl]
        in_maps.append(m)
    res = run_bass_kernel_spmd(nc, in_maps, core_ids=list(range(N_CORES)))
    R = res.results
    y_prompt = np.stack([R[i]["yp"] for i in range(N_CORES)], axis=0)
    y_sample = np.concatenate([R[i]["ys"] for i in range(N_CORES)], axis=0)[:, None, :]
    ret_p = np.stack([R[i]["o_retp"] for i in range(N_CORES)], axis=0)[None]
    h_p = np.stack([R[i]["o_hp"].reshape(512) for i in range(N_CORES)], axis=0)[None]
    cl_p = np.stack([R[i]["o_clp"] for i in range(N_CORES)], axis=0)[None]
    cf_p = np.stack([R[i]["o_cfp"] for i in range(N_CORES)], axis=0)[None]
    ret_s = np.concatenate([R[i]["o_rets"] for i in range(N_CORES)], axis=0)[None]
    h_s = np.concatenate([R[i]["o_hs"] for i in range(N_CORES)], axis=0)[None]
    cl_s = np.concatenate([R[i]["o_cls"] for i in range(N_CORES)], axis=0)[None]
    cf_s = np.concatenate([R[i]["o_cfs"] for i in range(N_CORES)], axis=0)[None]
    outs = (y_prompt, y_sample, ret_p, h_p, cl_p, cf_p, ret_s, h_s, cl_s, cf_s)
    return tuple(np.ascontiguousarray(o, dtype=np.float32) for o in outs)
```

```python
import math
from contextlib import ExitStack

import numpy as np
import concourse.bass as bass
import concourse.mybir as mybir
from concourse.bass_utils import run_bass_kernel_spmd

F32 = mybir.dt.float32
BF16 = mybir.dt.bfloat16
AF = mybir.ActivationFunctionType
ALU = mybir.AluOpType
AX = mybir.AxisListType

D = 1024
SEQ = 2048
NS = 16
DFF = 2816
NFF = 22
DPROJ = 3072
NT = 16
NST = 4
EPS = 1e-6
PAST_LEN = 16384
N_CORES = 8
NDMASEM = 32


class Op:
    __slots__ = ("eng", "fn", "deps", "dma", "sem", "val", "flag", "idx")


class Sched:
    def __init__(self):
        self.ops = []
        self.lastw = {}
        self.lastr = {}
        self.dma_last = [None] * NDMASEM
        self.dma_cnt = 0
        self.dma_cnt_sw = 0

    def add(self, eng, fn, reads=(), writes=(), dma=False):
        op = Op()
        op.eng, op.fn, op.dma, op.flag = eng, fn, dma, dma
        op.idx = len(self.ops)
        op.sem = None
        op.val = 0
        deps = set()
        for k in reads:
            w = self.lastw.get(k)
            if w is not None:
                deps.add(w)
            if isinstance(k, tuple) and k[0] == "ps":
                for e2, i2 in self.lastr.get(k, {}).items():
                    if e2 != eng:
                        deps.add(i2)
        for k in writes:
            w = self.lastw.get(k)
            if w is not None:
                deps.add(w)
            r = self.lastr.get(k)
            if r:
                deps.update(r.values())
        ek = ("dma", op.idx) if dma else eng
        for k in reads:
            self.lastr.setdefault(k, {})[ek] = op.idx
        for k in writes:
            self.lastw[k] = op.idx
            self.lastr[k] = {}
        if dma:
            half = NDMASEM // 2
            if eng == "pool":
                slot = half + self.dma_cnt_sw % half
                self.dma_cnt_sw += 1
            else:
                slot = self.dma_cnt % half
                self.dma_cnt += 1
            op.sem = ("dma", slot)
            prev = self.dma_last[slot]
            if prev is not None:
                deps.add(prev)
            self.dma_last[slot] = op.idx
        deps.discard(op.idx)
        op.deps = deps
        self.ops.append(op)
        return op.idx

    def emit(self, nc, block, sems, dma_sems):
        ops = self.ops
        for op in ops:
            for d in op.deps:
                dep = ops[d]
                if dep.eng == "pe" and op.eng == "pe" and not dep.dma and not op.dma:
                    continue
                dep.flag = True
        cnt = {e: 0 for e in sems}
        dcnt = [0] * NDMASEM
        for op in ops:
            if op.dma:
                s = op.sem[1]
                dcnt[s] += 16
                op.val = dcnt[s]
            elif op.flag:
                cnt[op.eng] += 1
                op.val = cnt[op.eng]
                op.sem = ("eng", op.eng)
        per_eng = {e: [] for e in sems}
        for op in ops:
            per_eng[op.eng].append(op)

        def run(engname, handle):
            waited = {}
            for op in per_eng[engname]:
                need = {}
                for d in op.deps:
                    dep = ops[d]
                    if dep.eng == "pe" and op.eng == "pe" and not dep.dma and not op.dma:
                        continue
                    if dep.val > need.get(dep.sem, 0):
                        need[dep.sem] = dep.val
                for sk, v in need.items():
                    if waited.get(sk, 0) >= v:
                        continue
                    waited[sk] = v
                    sem = dma_sems[sk[1]] if sk[0] == "dma" else sems[sk[1]]
                    handle.wait_ge(sem, v)
                inst = op.fn(handle)
                if op.dma:
                    inst.then_inc(dma_sems[op.sem[1]], 16)
                elif op.flag:
                    inst.then_inc(sems[op.eng], 1)
            if engname == "sp":
                for s in range(NDMASEM):
                    if dcnt[s] > 0:
                        handle.wait_ge(dma_sems[s], dcnt[s])

        @block.sync
        def _(e):
            run("sp", e)

        @block.tensor
        def _(e):
            run("pe", e)

        @block.scalar
        def _(e):
            run("act", e)

        @block.vector
        def _(e):
            run("dve", e)

        @block.gpsimd
        def _(e):
            run("pool", e)


class K:
    def __init__(self, S):
        self.S = S

    def mm(self, out, lhsT, rhs, start, stop, r, w):
        return self.S.add("pe", lambda e: e.matmul(out, lhsT, rhs, start=start, stop=stop), r, w)

    def tr(self, out, in_, ident, r, w):
        return self.S.add("pe", lambda e: e.transpose(out, in_, ident), r, w)

    def act(self, out, in_, func, r, w, bias=None, scale=None, accum_out=None, eng="act"):
        kw = {}
        if bias is not None:
            kw["bias"] = bias
        if scale is not None:
            kw["scale"] = scale
        if accum_out is not None:
            kw["accum_out"] = accum_out
        return self.S.add(eng, lambda e: e.activation(out, in_, func, **kw), r, w)

    def tt(self, out, in0, in1, op, r, w, eng="dve"):
        return self.S.add(eng, lambda e: e.tensor_tensor(out, in0, in1, op), r, w)

    def ts(self, out, in0, s1, s2, op0, op1, r, w, eng="dve"):
        if op1 is None:
            return self.S.add(eng, lambda e: e.tensor_scalar(out, in0, s1, None, op0), r, w)
        return self.S.add(eng, lambda e: e.tensor_scalar(out, in0, s1, s2, op0, op1), r, w)

    def stt(self, out, in0, scalar, in1, op0, op1, r, w):
        return self.S.add("dve", lambda e: e.scalar_tensor_tensor(out, in0, scalar, in1, op0, op1), r, w)

    def copy(self, out, in_, r, w, eng="dve"):
        if eng == "act":
            return self.S.add(eng, lambda e: e.activation(out, in_, AF.Copy), r, w)
        return self.S.add(eng, lambda e: e.tensor_copy(out, in_), r, w)

    def memset(self, ap, val, w, eng="dve"):
        return self.S.add(eng, lambda e: e.memset(ap, val), (), w)

    def dma(self, out, in_, r, w, eng="sp", **kw):
        return self.S.add(eng, lambda e: e.dma_start(out=out, in_=in_, **kw), r, w, dma=True)


def _consts():
    c = {}
    c["ident"] = np.eye(128, dtype=np.float32)
    j = np.arange(128)[:, None]
    i = np.arange(128)[None, :]
    c["maskT"] = (j <= i).astype(np.float32)
    inv_freq = (10000.0 ** (-np.arange(0, 128, 2, dtype=np.float32) / np.float32(128))).astype(np.float32)
    pos = np.arange(SEQ, dtype=np.float32)
    ang = (pos[:, None] * inv_freq[None, :]).astype(np.float32)
    cosP = np.cos(ang).astype(np.float32).reshape(NT, 128, 64).transpose(1, 0, 2)
    sinP = np.sin(ang).astype(np.float32).reshape(NT, 128, 64).transpose(1, 0, 2)
    c["cosP"] = np.ascontiguousarray(cosP)
    c["sinP"] = np.ascontiguousarray(sinP)
    log_g = np.log(1.0 - 2.0 ** (-5.0 - np.arange(4, dtype=np.float64)))
    ii = np.arange(128, dtype=np.float64)
    dq = np.exp((ii[None, :] + 1.0) * log_g[:, None])
    dk = (128.0 ** -0.5) * np.exp(-(ii[None, :] + 1.0) * log_g[:, None])
    c["dqT"] = np.ascontiguousarray(np.broadcast_to(dq[None], (128, 4, 128))).astype(np.float32)
    c["dkT"] = np.ascontiguousarray(np.broadcast_to(dk[None], (128, 4, 128))).astype(np.float32)
    c["dkP"] = np.ascontiguousarray(dk.T).astype(np.float32)
    c["gC"] = [float(np.exp(128.0 * log_g[h])) for h in range(4)]
    c["g1"] = [float(np.exp(log_g[h])) for h in range(4)]
    angS = (np.float32(PAST_LEN) * inv_freq).astype(np.float32)
    cs = np.cos(angS).astype(np.float32)
    sn = np.sin(angS).astype(np.float32)
    sc = np.float32(128.0 ** -0.5)
    c["rotS"] = np.stack([cs, sn, cs * sc, sn * sc]).astype(np.float32)
    sel = np.zeros((17, 129), np.float32)
    sel[16, :] = 1.0
    c["sel"] = sel
    c["i16"] = np.ascontiguousarray(np.broadcast_to(np.eye(16, dtype=np.float32)[None], (128, 16, 16)))
    return c


CONST = _consts()

VFM = {}
_off = 0
for _n, _w in [("g_mix", 8), ("g_ffn", 8), ("clw", 16), ("clb", 4), ("b_r", 4), ("b_i", 4), ("lam", 4),
               ("cfw", 66), ("cfb", 22)]:
    VFM[_n] = (_off, _w)
    _off += _w
NVFM = _off


import os
KPH = int(os.environ.get("KPHASE", "99"))


def build_program():
    nc = bass.Bass("TRN2", target_bir_lowering=False)
    S = Sched()
    k = K(S)

    def din(name, shape):
        return nc.dram_tensor(name, list(shape), F32, kind="ExternalInput").ap()

    def dout(name, shape):
        return nc.dram_tensor(name, list(shape), F32, kind="ExternalOutput").ap()

    xp = din("xp", [SEQ, D])
    xs = din("xs", [NS, D])
    c17 = din("c17", [17, D])
    st_ret = din("st_ret", [NS, 4, 128, 128])
    st_h = din("st_h", [NS, 512])
    st_cl = din("st_cl", [NS, 3, 512])
    st_cf = din("st_cf", [NS, 2, DFF])
    w_ada = din("w_ada", [D, 6 * D])
    b_ada = din("b_ada", [6 * D])
    w_in = din("w_in", [D, DPROJ])
    w_out = din("w_out", [D, D])
    w_upc = din("w_upc", [D, DFF])
    w_upg = din("w_upg", [D, DFF])
    w_dn = din("w_dn", [DFF, D])
    wr_bd = din("wr_bd", [128, 4, 128])
    wi_bd = din("wi_bd", [128, 4, 128])
    vfm_d = din("vfm", [128, NVFM])
    rows_d = {}
    for n, ln in [("g_mix", D), ("g_ffn", D), ("g_final", D), ("clw", 4 * 512), ("clb", 512), ("b_r", 512),
                  ("b_i", 512), ("lam", 512), ("cfw", 3 * DFF), ("cfb", DFF)]:
        rows_d[n] = din("row_" + n, [ln])
    cd = {}
    for n in ["ident", "maskT", "cosP", "sinP", "dqT", "dkT", "dkP", "rotS", "sel", "i16"]:
        cd[n] = din("c_" + n, CONST[n].shape)

    yp = dout("yp", [SEQ, D])
    ys = dout("ys", [NS, D])
    o_retp = dout("o_retp", [4, 128, 128])
    o_hp = dout("o_hp", [4, 128])
    o_clp = dout("o_clp", [3, 512])
    o_cfp = dout("o_cfp", [2, DFF])
    o_rets = dout("o_rets", [NS, 4, 128, 128])
    o_hs = dout("o_hs", [NS, 512])
    o_cls = dout("o_cls", [NS, 3, 512])
    o_cfs = dout("o_cfs", [NS, 2, DFF])

    x1_scr = nc.dram_tensor("x1_scr", [SEQ, D], F32, kind="Internal").ap()
    x1s_scr = nc.dram_tensor("x1s_scr", [NS, D], F32, kind="Internal").ap()
    modf_scr = nc.dram_tensor("modf_scr", [NS, 3 * D], F32, kind="Internal").ap()
    gatef_scr = nc.dram_tensor("gatef_scr", [1, D], F32, kind="Internal").ap()

    es = ExitStack()
    with es:
        def sb(name, shape, dt=F32):
            return es.enter_context(nc.sbuf_tensor("sb_" + name, list(shape), dt))

        ps = [es.enter_context(nc.psum_tensor("ps%d" % i, [128, 512], F32)) for i in range(8)]
        sems = {e: es.enter_context(nc.semaphore("sem_" + e)) for e in ["pe", "act", "dve", "pool", "sp"]}
        dma_sems = [es.enter_context(nc.semaphore("dsem%d" % i)) for i in range(NDMASEM)]

        ident = sb("ident", [128, 128])
        identb = sb("identb", [128, 128], BF16)
        vfm = sb("vfm", [128, NVFM])
        modfm = sb("modfm", [128, 4, 8])
        gsm = sb("gsm", [128, 8])
        gsf = sb("gsf", [128, 8])
        cf_fm = sb("cf_fm", [128, 4])
        tmp_fm = sb("tmp_fm", [128, 4])

        k.dma(ident[:], cd["ident"], (), ["ident"])
        k.copy(identb[:], ident[:], ["ident"], ["identb"], eng="act")
        k.dma(vfm[:], vfm_d, (), ["vfm"])

        def V(name):
            o, w = VFM[name]
            return vfm[:, o:o + w]

        k.act(tmp_fm[:], V("lam"), AF.Exp, ["vfm"], ["tmp_fm"], scale=-1.0)
        k.act(tmp_fm[:], tmp_fm[:], AF.Ln, ["tmp_fm"], ["tmp_fm"], bias=1.0)
        k.ts(cf_fm[:], tmp_fm[:], -8.0, None, ALU.mult, None, ["tmp_fm"], ["cf_fm"])

        esA = ExitStack()
        with esA:
            def sbA(name, shape, dt=F32):
                return esA.enter_context(nc.sbuf_tensor("sb_" + name, list(shape), dt))

            w_in_sb = sbA("w_in_sb", [128, 8, DPROJ], BF16)
            w_out_sb = sbA("w_out_sb", [128, 8, D], BF16)
            wr_sb = sbA("wr_sb", [128, 4, 128], BF16)
            wi_sb = sbA("wi_sb", [128, 4, 128], BF16)
            gate_m_row = sbA("gate_m_row", [128, D])

            es0 = ExitStack()
            with es0:
                def sb0(name, shape, dt=F32):
                    return es0.enter_context(nc.sbuf_tensor("sb_" + name, list(shape), dt))

                mod = sb0("mod", [17, 6 * D])
                early = dict(
                    x=sb0("sm_x", [NS, D]), cat=sb0("sm_cat", [NS, D]), stc=sb0("sm_stc", [NS, 3, 512]),
                    clb=sb0("sm_clb", [NS, 512]), brr=sb0("sm_brr", [NS, 512]), bir=sb0("sm_bir", [NS, 512]),
                    lamr=sb0("sm_lamr", [NS, 512]), proj=sb0("sm_proj", [NS, DPROJ]), qr=sb0("sm_qr", [NS, 512]),
                    kr=sb0("sm_kr", [NS, 512]))
                k.dma(early["x"][:], xs, (), ["smx"])
                k.dma(early["cat"][:], rows_d["g_mix"].partition_broadcast(NS), (), ["sm_grow"])
                k.dma(early["stc"][:], st_cl, (), ["sm_stc"])
                k.dma(early["clb"][:], rows_d["clb"].partition_broadcast(NS), (), ["sm_clb"])
                k.dma(early["brr"][:], rows_d["b_r"].partition_broadcast(NS), (), ["sm_brr"])
                k.dma(early["bir"][:], rows_d["b_i"].partition_broadcast(NS), (), ["sm_bir"])
                k.dma(early["lamr"][:], rows_d["lam"].partition_broadcast(NS), (), ["sm_lamr"])
                k.act(early["lamr"][:], early["lamr"][:], AF.Exp, ["sm_lamr"], ["sm_lamr"], scale=-1.0)
                k.act(early["lamr"][:], early["lamr"][:], AF.Ln, ["sm_lamr"], ["sm_lamr"], bias=1.0)
                esM = ExitStack()
                with esM:
                    def sbM(name, shape, dt=F32):
                        return esM.enter_context(nc.sbuf_tensor("sb_" + name, list(shape), dt))

                    c_sb = sbM("c_sb", [17, D])
                    sc_sb = sbM("sc_sb", [17, D])
                    scT = sbM("scT", [128, 8, 17], BF16)
                    NWB = 2
                    wada = [sbM("wada%d" % i, [128, 8, 1024], BF16) for i in range(NWB)]
                    bada = [sbM("bada%d" % i, [17, 1024]) for i in range(NWB)]
                    sel = sbM("sel", [17, 129])

                    k.dma(c_sb[:], c17, (), ["c_sb"])
                    k.dma(sel[:], cd["sel"], (), ["sel"])
                    wada_v = w_ada.rearrange("(kc p) n -> p kc n", p=128)

                    def load_wada(cb):
                        k.dma(wada[cb % NWB][:], wada_v[:, :, cb * 1024:(cb + 1) * 1024], (), [("wada", cb % NWB)],
                              eng="pool")
                        k.dma(bada[cb % NWB][:], b_ada[cb * 1024:(cb + 1) * 1024].partition_broadcast(17), (),
                              [("bada", cb % NWB)])

                    win_v = w_in.rearrange("(kc p) n -> p kc n", p=128)
                    wout_v = w_out.rearrange("(kc p) n -> p kc n", p=128)
                    for cb_ in range(NWB):
                        load_wada(cb_)
                    for kc in range(8):
                        k.dma(w_in_sb[:, kc, :].rearrange("p (a b) -> p a b", b=1024),
                              win_v[:, kc, :].rearrange("p (a b) -> p a b", b=1024), (), [("w_in", kc)], eng="pool")

                    k.act(sc_sb[:], c_sb[:], AF.Silu, ["c_sb"], ["sc_sb"])
                    for kc in range(8):
                        k.tr(ps[0][:, kc * 17:(kc + 1) * 17], sc_sb[0:17, kc * 128:(kc + 1) * 128], ident[0:17, 0:17],
                             ["sc_sb", "ident"], [("ps", 0)])
                    k.copy(scT[:].rearrange("p a b -> p (a b)"), ps[0][:, 0:136], [("ps", 0)], ["scT"])

                    for cb in range(6):
                        for hh in range(2):
                            pb = ps[1 + hh]
                            for kc in range(8):
                                k.mm(pb[0:17, :], scT[:, kc, :], wada[cb % NWB][:, kc, hh * 512:(hh + 1) * 512], kc == 0,
                                     kc == 7, ["scT", ("wada", cb % NWB)], [("ps", 1 + hh)])
                            k.tt(mod[:, cb * 1024 + hh * 512:cb * 1024 + (hh + 1) * 512], pb[0:17, :],
                                 bada[cb % NWB][:, hh * 512:(hh + 1) * 512], ALU.add,
                                 [("ps", 1 + hh), ("bada", cb % NWB)], [("mod", 2 * cb + hh)])
                        if cb + NWB < 6:
                            load_wada(cb + NWB)
                        if cb == 2 and KPH >= 1:
                            sample_a(nc, k, sbM, ps, ident, mod, [("mod", i_) for i_ in range(6)], cd, w_in_sb, early)
                        if cb == 6 - NWB - 1:
                            for kc in range(0, 8, 2):
                                k.dma(w_out_sb[:, kc:kc + 2, :], wout_v[:, kc:kc + 2, :], (),
                                      [("w_out", kc), ("w_out", kc + 1)], eng="pool")
                            k.dma(wr_sb[:], wr_bd, (), ["wr_sb"], eng="pool")
                            k.dma(wi_sb[:], wi_bd, (), ["wi_sb"], eng="pool")
                    modkeys = [("mod", cb) for cb in range(12)]

                    for jj, j in enumerate([0, 1, 3, 4]):
                        for c in range(8):
                            col = jj * 8 + c
                            k.mm(ps[3][:, col:col + 1], mod[0:17, j * D + c * 128: j * D + (c + 1) * 128],
                                 sel[0:17, 0:1], True, True, modkeys + ["sel"], [("ps", 3)])
                    k.copy(modfm[:].rearrange("p a b -> p (a b)"), ps[3][:, 0:32], [("ps", 3)], ["modfm"])
                    k.stt(gsm[:], modfm[:, 1, :], 1.0, V("g_mix"), ALU.add, ALU.mult, ["modfm", "vfm"], ["gsm"])
                    k.stt(gsf[:], modfm[:, 3, :], 1.0, V("g_ffn"), ALU.add, ALU.mult, ["modfm", "vfm"], ["gsf"])
                    for gi, (j, dst, key) in enumerate([(2, gate_m_row, "gate_m_row")]):
                        for hf in range(2):
                            pb = ps[4 + hf]
                            k.mm(pb[:, :], sel[0:17, 1:129], mod[0:17, j * D + hf * 512: j * D + (hf + 1) * 512],
                                 True, True, modkeys + ["sel"], [("ps", 4 + hf)])
                            k.copy(dst[:, hf * 512:(hf + 1) * 512], pb[:, :], [("ps", 4 + hf)], [(key, hf)], eng="act")
                    k.dma(modf_scr, mod[0:NS, 3 * D:6 * D], modkeys, ["modf_scr"])
                    k.dma(gatef_scr, mod[16:17, 5 * D:6 * D], modkeys, ["gatef_scr"])
                barrier(S)
                modkeys = []

                if KPH >= 1:
                  sample_mixer(nc, k, es0, ps, ident, xs, mod, modkeys, rows_d, cd, w_in_sb, w_out_sb, wr_sb, wi_sb,
                               st_ret, st_h, st_cl, o_rets, o_hs, o_cls, x1s_scr, early)

            barrier(S)
            es1 = ExitStack()
            with es1:
                if KPH >= 2:
                  prompt_mixer(nc, k, es1, ps, ident, identb, xp, cd, V, gsm, modfm, gate_m_row, cf_fm,
                               w_in_sb, w_out_sb, wr_sb, wi_sb, x1_scr, o_retp, o_hp, o_clp)
            barrier(S)

        esB = ExitStack()
        with esB:
            def sbB(name, shape, dt=F32):
                return esB.enter_context(nc.sbuf_tensor("sb_" + name, list(shape), dt))

            wuc_sb = sbB("wuc_sb", [128, 8, DFF], BF16)
            wug_sb = sbB("wug_sb", [128, 8, DFF], BF16)
            wdn_sb = sbB("wdn_sb", [128, NFF, D], BF16)
            wuc_v = w_upc.rearrange("(kc p) n -> p kc n", p=128)
            wug_v = w_upg.rearrange("(kc p) n -> p kc n", p=128)
            wdn_v = w_dn.rearrange("(m p) n -> p m n", p=128)
            def _wl_up(gi, a, b):
                def f():
                    for h0 in (0, 4):
                        k.dma(wuc_sb[:, h0:h0 + 4, a:b], wuc_v[:, h0:h0 + 4, a:b], (),
                              [("wuc", kc, gi) for kc in range(h0, h0 + 4)], eng="pool")
                        k.dma(wug_sb[:, h0:h0 + 4, a:b], wug_v[:, h0:h0 + 4, a:b], (),
                              [("wug", kc, gi) for kc in range(h0, h0 + 4)], eng="pool")
                return f

            def _wl_dn():
                for m0 in range(0, NFF, 4):
                    m1 = min(NFF, m0 + 4)
                    k.dma(wdn_sb[:, m0:m1, :], wdn_v[:, m0:m1, :], (), [("wdn", m) for m in range(m0, m1)],
                          eng="pool")

            wload = [_wl_up(0, 0, 1024), _wl_up(1, 1024, 2048), _wl_up(2, 2048, DFF), _wl_dn]

            es3 = ExitStack()
            with es3:
                if KPH >= 4:
                  prompt_ffn(nc, k, es3, ps, ident, V, gsf, modfm, gatef_scr, rows_d, x1_scr, wuc_sb, wug_sb, wdn_sb,
                             yp, o_cfp, wload)
                else:
                  for f_ in wload:
                      f_()
            barrier(S)
            es2 = ExitStack()
            with es2:
                if KPH >= 3:
                  sample_ffn(nc, k, es2, ps, ident, x1s_scr, modf_scr, rows_d, wuc_sb, wug_sb, wdn_sb, st_cf, o_cfs, ys)

        with nc.Block() as block:
            S.emit(nc, block, sems, dma_sems)
    return nc


ALLK = "__all__"


def barrier(S, keep=None):
    keep = keep or (lambda key: False)
    allk = [kk for kk in (set(S.lastw.keys()) | set(S.lastr.keys())) if not keep(kk)]
    order = ["act", "dve", "pool", "sp", "pe"]
    for e in order:
        S.add(e, _drain, (), allk + [("barrier", e)])
    for e in order:
        S.add(e, _drain, [("barrier", x) for x in order], [("barrier2", e)])
    kw = {kk: v for kk, v in S.lastw.items() if keep(kk)}
    kr = {kk: v for kk, v in S.lastr.items() if keep(kk)}
    S.lastw = {("barrier2", e): S.lastw[("barrier2", e)] for e in order}
    S.lastw.update(kw)
    S.lastr = kr


def _drain(e):
    return e.drain()


def prompt_mixer(nc, k, es, ps, ident, identb, xp, cd, V, gsm, modfm, gate_m_row, cf_fm,
                 w_in_sb, w_out_sb, wr_sb, wi_sb, x1_scr, o_retp, o_hp, o_clp):
    S = k.S

    def sb(name, shape, dt=F32):
        return es.enter_context(nc.sbuf_tensor("sb_" + name, list(shape), dt))

    cosP = sb("cosP", [128, NT, 64]); sinP = sb("sinP", [128, NT, 64])
    maskT = sb("maskT", [128, 128]); dqT = sb("dqT", [128, 4, 128]); dkT = sb("dkT", [128, 4, 128])
    dkP = sb("dkP", [128, 4])
    for t_, n in [(cosP, "cosP"), (sinP, "sinP"), (maskT, "maskT"), (dqT, "dqT"), (dkT, "dkT"), (dkP, "dkP")]:
        k.dma(t_[:], cd[n], (), [n])
    neg_half = sb("neg_half", [128, 4])
    k.memset(neg_half[:], -0.5, ["neg_half"])
    hcf = sb("hcf", [128, 4]); hb_r = sb("hb_r", [128, 4]); hb_i = sb("hb_i", [128, 4])
    k.ts(hcf[:], cf_fm[:], 0.5, None, ALU.mult, None, ["cf_fm"], ["hcf"])
    k.ts(hb_r[:], V("b_r"), 0.5, None, ALU.mult, None, ["vfm"], ["hb_r"])
    k.ts(hb_i[:], V("b_i"), 0.5, None, ALU.mult, None, ["vfm"], ["hb_i"])

    xt = [sb("xt%d" % i, [128, D]) for i in range(2)]
    xr = sb("xr0", [128, D]); x1t = sb("x1t0", [128, D])
    ssb = sb("ssb", [128, 4]); msb = sb("msb", [128, 4]); rstd = sb("rstd", [128, 4])
    xn = sb("xn0", [128, D])
    hT = [sb("hT%d" % i, [128, 8, 512], BF16) for i in range(2)]
    catT = [sb("catT%d" % i, [128, 8, 512], BF16) for i in range(2)]
    xc = sb("xc", [128, 2, 512]); rr = sb("rr", [128, 2, 512]); aa = sb("aa", [128, 2, 512])
    ig = sb("ig", [128, 2, 512]); gl = sb("gl", [128, 2, 512]); xcb = sb("xcb", [128, 2, 512], BF16)
    xcar = sb("xcar", [128, 3, 4]); hcar = sb("hcar", [128, 4])
    st12 = sb("st12", [12, 128]); st4 = sb("st4", [4, 128])
    qrot = sb("qrot", [128, 512]); krot = sb("krot", [128, 512])
    mq = [sb("mq%d" % i, [128, 256]) for i in range(4)]
    ktok = [sb("ktok%d" % i, [128, 512], BF16) for i in range(2)]
    v_bf = [sb("v_bf%d" % i, [128, 512], BF16) for i in range(2)]
    sg = [sb("sg%d" % i, [128, 512]) for i in range(2)]
    qkT = [sb("qkT%d" % i, [128, 8, 128], BF16) for i in range(2)]
    PT = sb("PT", [128, 512], BF16)
    Z = sb("Z", [128, 512]); S_bf = sb("S_bf", [128, 512], BF16)
    stats = sb("stats", [128, 4, 6]); mv = sb("mv", [128, 4, 2]); vpe = sb("vpe", [128, 4]); rs = sb("rs", [128, 4])
    sgr = sb("sgr", [128, 512]); ret = [sb("ret%d" % i, [128, 512], BF16) for i in range(2)]
    Sl = sgr

    clw = V("clw"); clb = V("clb"); b_r = V("b_r"); b_i = V("b_i")
    shift_m = modfm[:, 0, :]
    gC = CONST["gC"]
    BK_L, BK_A0, BK_A1, BK_T, BK_KV, BK_O = 2, 3, 4, 5, 6, 7

    def h3(ap):
        return ap.rearrange("p (h d) -> p h d", h=4)

    def hTk(s):
        return [("hT", s % 2, c, t) for c in range(8) for t in range(4)]

    junkb = sb("junkb", [128, D], BF16)

    def N1(T):
        t = T % 4
        a = T % 2
        k.dma(xt[a][:], xp[T * 128:(T + 1) * 128, :], (), [("xt", a)])
        k.act(junkb[:], xt[a][:], AF.Square, [("xt", a)], ["junkb", ("ssb", t)], accum_out=ssb[:, t:t + 1])
        k.ts(msb[:, t:t + 1], ssb[:, t:t + 1], 1.0 / D, EPS, ALU.mult, ALU.add, [("ssb", t)], [("msb", t)])
        k.tt(rstd[:, t:t + 1], msb[:, t:t + 1], neg_half[:, 0:1], ALU.pow, [("msb", t), "neg_half"],
             [("rstd", t)], eng="pool")

    def N2(T):
        s, t = divmod(T, 4)
        a = T % 2
        tok = slice(t * 128, (t + 1) * 128)
        hTs = hT[s % 2]
        k.ts(xn[:], xt[a][:], rstd[:, t:t + 1], 0.0, ALU.mult, ALU.add, [("xt", a), ("rstd", t)], ["xn"], eng="pool")
        for half in range(2):
            b = half
            for c4 in range(4):
                c = half * 4 + c4
                k.tr(ps[b][:, c4 * 128:(c4 + 1) * 128], xn[:, c * 128:(c + 1) * 128], ident[:],
                     ["xn", "ident"], [("ps", b)])
            for c4 in range(4):
                c = half * 4 + c4
                if half == 0:
                    k.ts(hTs[:, c, tok], ps[b][:, c4 * 128:(c4 + 1) * 128], gsm[:, c:c + 1], shift_m[:, c:c + 1],
                         ALU.mult, ALU.add, [("ps", b), "gsm", "modfm"], [("hT", s % 2, c, t)])
                else:
                    k.act(hTs[:, c, tok], ps[b][:, c4 * 128:(c4 + 1) * 128], AF.Identity,
                          [("ps", b), "gsm", "modfm"], [("hT", s % 2, c, t)], scale=gsm[:, c:c + 1],
                          bias=shift_m[:, c:c + 1])

    def L1(s, hf):
        hTs = hT[s % 2]
        for ci in range(2):
            c = 2 * hf + ci
            bkx = ci
            for kc in range(8):
                k.mm(ps[bkx][:, :], w_in_sb[:, kc, 2048 + c * 128: 2048 + (c + 1) * 128], hTs[:, kc, :], kc == 0,
                     kc == 7, hTk(s) + [("w_in", kc)], [("ps", bkx)])
            for kc in range(8):
                k.mm(ps[BK_L][:, :], w_in_sb[:, kc, 2560 + c * 128: 2560 + (c + 1) * 128], hTs[:, kc, :], kc == 0,
                     kc == 7, hTk(s) + [("w_in", kc)], [("ps", BK_L)])
            k.act(xc[:, ci, :], ps[bkx][:, :], AF.Identity, [("ps", bkx), "vfm"], [("xc", ci)],
                  scale=clw[:, c * 4 + 3:c * 4 + 4], bias=clb[:, c:c + 1])
            k.act(gl[:, ci, :], ps[BK_L][:, :], AF.Gelu_apprx_tanh, [("ps", BK_L)], [("gl", ci)])
            for sh in (1, 2, 3):
                kk = 3 - sh
                wcol = clw[:, c * 4 + kk:c * 4 + kk + 1]
                k.stt(xc[:, ci, sh:512], ps[bkx][:, 0:512 - sh], wcol, xc[:, ci, sh:512], ALU.mult, ALU.add,
                      [("ps", bkx), ("xc", ci), "vfm"], [("xc", ci)])
                if s > 0:
                    k.stt(xc[:, ci, 0:sh], xcar[:, 3 - sh:3, c], wcol, xc[:, ci, 0:sh], ALU.mult, ALU.add,
                          [("xcar", c), ("xc", ci), "vfm"], [("xc", ci)])
            k.copy(xcar[:, :, c], ps[bkx][:, 509:512], [("ps", bkx)], [("xcar", c)])

    def L1c(s, hf):
        for ci in range(2):
            k.copy(xcb[:, ci, :], xc[:, ci, :], [("xc", ci)], [("xcb", ci)], eng="pool")

    def L2(s, hf):
        cts = catT[s % 2]
        for ci in range(2):
            c = 2 * hf + ci
            k.mm(ps[BK_L][:, :], wr_sb[:, c, :], xcb[:, ci, :], True, True, [("xcb", ci), "wr_sb"], [("ps", BK_L)])
            k.act(rr[:, ci, :], ps[BK_L][:, :], AF.Tanh, [("ps", BK_L), "hb_r"], [("rr", ci)], bias=hb_r[:, c:c + 1],
                  scale=0.5)
            k.mm(ps[ci][:, :], wi_sb[:, c, :], xcb[:, ci, :], True, True, [("xcb", ci), "wi_sb"], [("ps", ci)])
            k.act(ig[:, ci, :], ps[ci][:, :], AF.Tanh, [("ps", ci), "hb_i"], [("ig", ci)], bias=hb_i[:, c:c + 1],
                  scale=0.5)
        for ci in range(2):
            c = 2 * hf + ci
            k.act(aa[:, ci, :], rr[:, ci, :], AF.Exp, [("rr", ci), "hcf"], [("aa", ci)], scale=hcf[:, c:c + 1],
                  bias=hcf[:, c:c + 1])
            k.act(rr[:, ci, :], rr[:, ci, :], AF.Exp, [("rr", ci), "cf_fm"], [("rr", ci)], scale=cf_fm[:, c:c + 1],
                  bias=cf_fm[:, c:c + 1])
        for ci in range(2):
            k.act(rr[:, ci, :], rr[:, ci, :], AF.Sqrt, [("rr", ci)], [("rr", ci)], scale=-1.0, bias=1.0)

    def L2b(s, hf):
        cts = catT[s % 2]
        for ci in range(2):
            c = 2 * hf + ci
            if s == 0:
                k.memset(rr[:, ci, 0:1], 1.0, [("rr", ci)])
            k.stt(ig[:, ci, :], ig[:, ci, :], 1.0, xc[:, ci, :], ALU.add, ALU.mult, [("ig", ci), ("xc", ci)], [("ig", ci)])
            k.stt(ig[:, ci, :], ig[:, ci, :], 0.5, rr[:, ci, :], ALU.mult, ALU.mult, [("ig", ci), ("rr", ci)], [("ig", ci)])
            init = 0.0 if s == 0 else hcar[:, c:c + 1]
            S.add("dve", (lambda ci=ci, init=init: (lambda e: e.tensor_tensor_scan(
                xc[:, ci, :], aa[:, ci, :], ig[:, ci, :], init, ALU.mult, ALU.add)))(),
                [("aa", ci), ("ig", ci), ("hcar", c), ("xc", ci)], [("xc", ci)])
            k.copy(hcar[:, c:c + 1], xc[:, ci, 511:512], [("xc", ci)], [("hcar", c)])
            k.tt(cts[:, 4 + c, :], xc[:, ci, :], gl[:, ci, :], ALU.mult, [("xc", ci), ("gl", ci)],
                 [("catT", s % 2, 4 + c)])

    def A(T, js=(0, 1, 2, 3)):
        s, t = divmod(T, 4)
        a = T % 2
        tok = slice(t * 128, (t + 1) * 128)
        hTs = hT[s % 2]
        cosb = cosP[:, T:T + 1, :].broadcast_to([128, 4, 64])
        sinb = sinP[:, T:T + 1, :].broadcast_to([128, 4, 64])
        for j in js:
            bk = BK_A0 + j % 2
            for kc in range(8):
                k.mm(ps[bk][:, :], hTs[:, kc, tok], w_in_sb[:, kc, j * 512:(j + 1) * 512], kc == 0, kc == 7,
                     [("hT", s % 2, c_, t) for c_ in range(8)] + [("w_in", kc)], [("ps", bk)])
            p3 = h3(ps[bk][:, :])
            if j < 2:
                dst = qrot if j == 0 else krot
                nm = "q" if j == 0 else "k"
                m3 = [x[:, :].rearrange("p (h d) -> p h d", h=4) for x in mq]
                k.tt(m3[0], p3[:, :, 0:64], cosb, ALU.mult, [("ps", bk), "cosP"], [("qkm", 0)])
                k.tt(m3[1], p3[:, :, 64:128], sinb, ALU.mult, [("ps", bk), "sinP"], [("qkm", 1)])
                k.tt(m3[2], p3[:, :, 0:64], sinb, ALU.mult, [("ps", bk), "sinP"], [("qkm", 2)])
                k.tt(m3[3], p3[:, :, 64:128], cosb, ALU.mult, [("ps", bk), "cosP"], [("qkm", 3)])
                d3 = h3(dst[:, :])
                k.tt(d3[:, :, 0:64], m3[0], m3[1], ALU.subtract, [("qkm", 0), ("qkm", 1)], [(nm + "rot", 0)])
                k.tt(d3[:, :, 64:128], m3[2], m3[3], ALU.add, [("qkm", 2), ("qkm", 3)], [(nm + "rot", 1)])
                if j == 1:
                    for h in range(4):
                        k.ts(ktok[a][:, h * 128:(h + 1) * 128], krot[:, h * 128:(h + 1) * 128], dkP[:, h:h + 1], 0.0,
                             ALU.mult, ALU.add, [("krot", 0), ("krot", 1), "dkP"], [("ktok", a, h)], eng="pool")
            elif j == 2:
                k.copy(v_bf[a][:, :], ps[bk][:, :], [("ps", bk)], [("v_bf", a)], eng="act")

    def Ag_ev(T):
        a = T % 2
        bk = BK_A0 + 1
        k.act(sg[a][:, :], ps[bk][:, :], AF.Tanh, [("ps", bk)], [("sg", a)], scale=0.5)
        k.stt(sg[a][:, :], sg[a][:, :], 1.0, ps[bk][:, :], ALU.add, ALU.mult, [("ps", bk), ("sg", a)], [("sg", a)])

    def A2(T):
        for h in range(4):
            k.tr(ps[BK_T][:, h * 128:(h + 1) * 128], qrot[:, h * 128:(h + 1) * 128], ident[:],
                 [("qrot", 0), ("qrot", 1), "ident"], [("ps", BK_T)])
        for h in range(4):
            k.tr(ps[BK_KV][:, h * 128:(h + 1) * 128], krot[:, h * 128:(h + 1) * 128], ident[:],
                 [("krot", 0), ("krot", 1), "ident"], [("ps", BK_KV)])

    def A2ev(T):
        a = T % 2
        k.tt(qkT[a][:, 0:4, :], h3(ps[BK_T][:, :]), dqT[:], ALU.mult, [("ps", BK_T), "dqT"], [("qT", a)])
        k.tt(qkT[a][:, 4:8, :], h3(ps[BK_KV][:, :]), dkT[:], ALU.mult, [("ps", BK_KV), "dkT"], [("kT", a)])

    def B1(T):
        a = T % 2
        qk = qkT[a]
        for h in range(4):
            k.mm(ps[BK_T][:, h * 128:(h + 1) * 128], qk[:, 4 + h, :], qk[:, h, :], True, True,
                 [("qT", a), ("kT", a)], [("ps", BK_T)])
        k.tt(h3(PT[:, :]), h3(ps[BK_T][:, :]), maskT[:, :].unsqueeze(1).broadcast_to([128, 4, 128]), ALU.mult,
             [("ps", BK_T), "maskT"], ["PT"])
        for h in range(4):
            hs = slice(h * 128, (h + 1) * 128)
            k.mm(ps[BK_KV][:, hs], ktok[a][:, hs], v_bf[a][:, hs], True, True, [("ktok", a, h), ("v_bf", a)],
                 [("ps", BK_KV)])
        for h in range(4):
            hs = slice(h * 128, (h + 1) * 128)
            k.mm(ps[BK_O][:, hs], PT[:, hs], v_bf[a][:, hs], True, T == 0, ["PT", ("v_bf", a)], [("ps", BK_O)])
            if T > 0:
                k.mm(ps[BK_O][:, hs], qk[:, h, :], S_bf[:, hs], False, True, [("qT", a), ("S_bf", h)], [("ps", BK_O)])
        for h in range(4):
            hs = slice(h * 128, (h + 1) * 128)
            if T == 0:
                k.copy(Z[:, hs], ps[BK_KV][:, hs], [("ps", BK_KV)], [("Z", h)])
            else:
                k.stt(Z[:, hs], Z[:, hs], gC[h], ps[BK_KV][:, hs], ALU.mult, ALU.add, [("ps", BK_KV), ("Z", h)],
                      [("Z", h)])
            if T < NT - 1:
                k.ts(S_bf[:, hs], Z[:, hs], gC[h], None, ALU.mult, None, [("Z", h)], [("S_bf", h)])

    def B1b(T):
        a = T % 2
        for h in range(4):
            hs = slice(h * 128, (h + 1) * 128)
            S.add("dve", (lambda h=h, hs=hs: (lambda e: e.bn_stats(stats[:, h, :], ps[BK_O][:, hs])))(),
                  [("ps", BK_O)], [("stats", h)])
            S.add("dve", (lambda h=h: (lambda e: e.bn_aggr(mv[:, h, :], stats[:, h, :])))(),
                  [("stats", h)], [("mv", h)])
        mvk = [("mv", h) for h in range(4)]
        k.ts(vpe[:, :], mv[:, :, 1], EPS, 4.0, ALU.add, ALU.mult, mvk, ["vpe"])
        k.tt(rs[:, :], vpe[:, :], neg_half[:, :], ALU.pow, ["vpe", "neg_half"], ["rs"], eng="pool")

    def B1c(T):
        a = T % 2
        for h in range(4):
            hs = slice(h * 128, (h + 1) * 128)
            k.act(sgr[:, hs], sg[a][:, hs], AF.Copy, [("sg", a), "rs"], [("sgr", h)], scale=rs[:, h:h + 1])
            k.stt(ret[a][:, hs], ps[BK_O][:, hs], mv[:, h, 0:1], sgr[:, hs], ALU.subtract, ALU.mult,
                  [("ps", BK_O), ("mv", h), ("sgr", h)], [("ret", a, h)])
        if T == NT - 1:
            for h in range(4):
                hs = slice(h * 128, (h + 1) * 128)
                k.act(Sl[:, hs], Z[:, hs], AF.Copy, [("Z", h)], [("sgr", h)], scale=gC[h])
            k.dma(o_retp.rearrange("h k v -> k h v"), h3(Sl[:, :]), [("sgr", h_) for h_ in range(4)], ["o_retp"])

    def B2(T):
        s, t = divmod(T, 4)
        a = T % 2
        tok = slice(t * 128, (t + 1) * 128)
        pbf = ps[BK_KV][:, :].bitcast(BF16)
        for h in range(4):
            k.tr(pbf[:, h * 128:(h + 1) * 128], ret[a][:, h * 128:(h + 1) * 128], identb[:],
                 [("ret", a, h), "identb"], [("ps", BK_KV)])
        k.copy(catT[s % 2][:, 0:4, tok], pbf[:, 0:512].rearrange("p (h d) -> p h d", h=4), [("ps", BK_KV)],
               [("catTr", s % 2, t)])

    def O(T):
        s, t = divmod(T, 4)
        tok = slice(t * 128, (t + 1) * 128)
        cts = catT[s % 2]
        catk = [("catT", s % 2, 4 + c) for c in range(4)]
        k.dma(xr[:], xp[T * 128:(T + 1) * 128, :], (), ["xr"])
        for cb in range(2):
            for kc in range(8):
                k.mm(ps[cb][:, :], cts[:, kc, tok], w_out_sb[:, kc, cb * 512:(cb + 1) * 512], kc == 0, kc == 7,
                     catk + [("catTr", s % 2, t), ("w_out", kc)], [("ps", cb)])
            k.tt(x1t[:, cb * 512:(cb + 1) * 512], ps[cb][:, :], gate_m_row[:, cb * 512:(cb + 1) * 512],
                 ALU.mult, [("ps", cb), ("gate_m_row", 0), ("gate_m_row", 1)], [("x1t", cb)])

    def O2(T):
        k.tt(x1t[:, :], x1t[:, :], xr[:, :], ALU.add, [("x1t", 0), ("x1t", 1), "xr"], [("x1t", 0), ("x1t", 1)],
             eng="pool")
        k.dma(x1_scr[T * 128:(T + 1) * 128, :], x1t[:, :], [("x1t", 0), ("x1t", 1)], [("x1_scr", T)])

    def ok(T):
        return 0 <= T < NT

    def Lpiece(kind, idx):
        if not (0 <= idx < 2 * NST):
            return
        s, hf = divmod(idx, 2)
        {"L1": L1, "L2a": L2, "L2b": L2b, "L1c": L1c}[kind](s, hf)

    N1(0)
    N1(1)
    N2(0)
    N1(2)
    N2(1)
    N1(3)
    N2(2)
    N2(3)
    N1(4)
    for tau in range(NT + 6):
        if ok(tau):
            A(tau, (0, 1))
        if ok(tau - 1):
            A2ev(tau - 1)
        if ok(tau - 2):
            B1c(tau - 2)
        if ok(tau):
            A(tau, (2, 3))
        if ok(tau - 1):
            B1(tau - 1)
        if ok(tau):
            Ag_ev(tau)
        if ok(tau + 4) and tau + 4 >= 4:
            N2(tau + 4)
        if tau % 2 == 0:
            Lpiece("L2b", tau // 2 - 1)
            Lpiece("L1", tau // 2)
        else:
            Lpiece("L2a", tau // 2)
        if tau == 4 * NST:
            k.tr(ps[0][0:12, 0:128], xcar[:].rearrange("p k c -> p (k c)"), ident[:],
                 [("xcar", c) for c in range(4)] + ["ident"], [("ps", 0)])
            k.copy(st12[:, :], ps[0][0:12, 0:128], [("ps", 0)], ["st12"])
            k.dma(o_clp.rearrange("k (c p) -> (k c) p", p=128), st12[:, :], ["st12"], ["o_clp"])
            k.tr(ps[1][0:4, 0:128], hcar[:, :], ident[:], [("hcar", c) for c in range(4)] + ["ident"], [("ps", 1)])
            k.copy(st4[:, :], ps[1][0:4, 0:128], [("ps", 1)], ["st4"])
            k.dma(o_hp, st4[:, :], ["st4"], ["o_hp"])
        if ok(tau - 2):
            B2(tau - 2)
        if ok(tau - 5):
            O(tau - 5)
        if ok(tau - 1):
            B1b(tau - 1)
        if ok(tau + 5):
            N1(tau + 5)
        if ok(tau - 5):
            O2(tau - 5)
        if ok(tau):
            A2(tau)
        if tau % 2 == 0:
            Lpiece("L1c", tau // 2)


def prompt_ffn(nc, k, es, ps, ident, V, gsf, modfm, gatef_scr, rows_d, x1_scr, wuc_sb, wug_sb, wdn_sb, yp, o_cfp,
               wload):
    S = k.S

    def sb(name, shape, dt=F32):
        return es.enter_context(nc.sbuf_tensor("sb_" + name, list(shape), dt))

    gfin = sb("gfin", [128, D])
    k.dma(gfin[:], rows_d["g_final"].partition_broadcast(128), (), ["gfin"])
    gate_f_row = sb("gate_f_row", [128, D])
    k.dma(gate_f_row[:], gatef_scr[0].partition_broadcast(128), ["gatef_scr"], [("gate_f_row", 0), ("gate_f_row", 1)])
    neg_half = sb("neg_half2", [128, 1])
    k.memset(neg_half[:], -0.5, ["neg_half2"])
    xa = [sb("xa%d" % i, [128, D]) for i in range(2)]
    xb = [sb("xb0", [128, D])] * 2
    xn2 = [sb("xn2_0", [128, D])] * 2
    yt = sb("yt", [128, D]); yo = sb("yo", [128, D])
    ss = sb("ss2", [128, 4]); ms = sb("ms2", [128, 4]); rstd = sb("rstd2", [128, 4])
    ss3 = sb("ss3", [128, 2]); ms3 = sb("ms3", [128, 2]); rstd3 = sb("rstd3", [128, 2])
    h2T = sb("h2T", [128, 8, 512], BF16)
    aT = sb("aT", [128, NFF, 512], BF16)
    acc = [sb("acc%d" % i, [128, 512]) for i in range(2)]
    ucar = sb("ucar", [128, 2, NFF])
    st44 = sb("st44", [44, 128])
    junkb = sb("junkb2", [128, D], BF16)
    cfw = V("cfw"); cfb = V("cfb")
    shift_f = modfm[:, 2, :]

    def N2load(s, t):
        T = 4 * s + t
        a = T % 2
        k.dma(xa[a][:], x1_scr[T * 128:(T + 1) * 128, :], [("x1_scr", T)], [("xa", a)])

    def N2pre(s, t, load=True, part="all"):
        T = 4 * s + t
        a = T % 2
        if load:
            N2load(s, t)
        if part == "copy":
            k.act(xn2[a][:], xa[a][:], AF.Copy, [("xa", a), ("rstd2", t)], [("xn2", 0)], scale=rstd[:, t:t + 1])
            return
        k.act(junkb[:], xa[a][:], AF.Square, [("xa", a)], ["junkb2", ("ss2", t)], accum_out=ss[:, t:t + 1])
        k.ts(ms[:, t:t + 1], ss[:, t:t + 1], 1.0 / D, EPS, ALU.mult, ALU.add, [("ss2", t)], [("ms2", t)])
        k.tt(rstd[:, t:t + 1], ms[:, t:t + 1], neg_half[:, 0:1], ALU.pow, [("ms2", t), "neg_half2"],
             [("rstd2", t)], eng="pool")
        if part == "stats":
            return
        k.act(xn2[a][:], xa[a][:], AF.Copy, [("xa", a), ("rstd2", t)], [("xn2", 0)], scale=rstd[:, t:t + 1])

    def N2tr(s, t):
        T = 4 * s + t
        a = T % 2
        tok = slice(t * 128, (t + 1) * 128)
        for half in range(2):
            b = half
            for c4 in range(4):
                c = half * 4 + c4
                k.tr(ps[b][:, c4 * 128:(c4 + 1) * 128], xn2[a][:, c * 128:(c + 1) * 128], ident[:],
                     [("xn2", 0), "ident"], [("ps", b)])
            for c4 in range(4):
                c = half * 4 + c4
                if half == 0:
                    k.ts(h2T[:, c, tok], ps[b][:, c4 * 128:(c4 + 1) * 128], gsf[:, c:c + 1], shift_f[:, c:c + 1],
                         ALU.mult, ALU.add, [("ps", b), "gsf", "modfm"], [("h2T", c, t)])
                else:
                    k.act(h2T[:, c, tok], ps[b][:, c4 * 128:(c4 + 1) * 128], AF.Identity,
                          [("ps", b), "gsf", "modfm"], [("h2T", c, t)], scale=gsf[:, c:c + 1],
                          bias=shift_f[:, c:c + 1])

    h2k = [("h2T", c, t) for c in range(8) for t in range(4)]
    aTk = [("aT", m) for m in range(NFF)]

    def UP(s):
        for m in range(NFF):
            bu = 2 + 2 * (m % 2)
            bg = bu + 1
            ms_ = slice(m * 128, (m + 1) * 128)
            for kc in range(8):
                k.mm(ps[bu][:, :], wuc_sb[:, kc, ms_], h2T[:, kc, :], kc == 0, kc == 7, h2k + [("wuc", kc, m // 8)], [("ps", bu)])
            for kc in range(8):
                k.mm(ps[bg][:, :], wug_sb[:, kc, ms_], h2T[:, kc, :], kc == 0, kc == 7, h2k + [("wug", kc, m // 8)], [("ps", bg)])
            ac = acc[m % 2]
            ak = ("acc", m % 2)
            k.act(ac[:, :], ps[bu][:, :], AF.Identity, [("ps", bu), "vfm"], [ak], scale=cfw[:, m * 3 + 2:m * 3 + 3],
                  bias=cfb[:, m:m + 1])
            for sh in (1, 2):
                kk = 2 - sh
                wcol = cfw[:, m * 3 + kk:m * 3 + kk + 1]
                k.stt(ac[:, sh:512], ps[bu][:, 0:512 - sh], wcol, ac[:, sh:512], ALU.mult, ALU.add,
                      [("ps", bu), ak, "vfm"], [ak])
                if s > 0:
                    k.stt(ac[:, 0:sh], ucar[:, 2 - sh:2, m], wcol, ac[:, 0:sh], ALU.mult, ALU.add,
                          [("ucar", m), ak, "vfm"], [ak])
            k.copy(ucar[:, :, m], ps[bu][:, 510:512], [("ps", bu)], [("ucar", m)])
            k.act(ac[:, :], ac[:, :], AF.Gelu_apprx_tanh, [ak], [ak])
            k.tt(aT[:, m, :], ac[:, :], ps[bg][:, :], ALU.mult, [ak, ("ps", bg)], [("aT", m)])

    def DOWN(s, t):
        if True:
            T = 4 * s + t
            a = T % 2
            tok = slice(t * 128, (t + 1) * 128)
            k.dma(xb[a][:], x1_scr[T * 128:(T + 1) * 128, :], [("x1_scr", T)], [("xb", 0)])
            for cb in range(2):
                for m in range(NFF):
                    k.mm(ps[6 + cb][:, :], aT[:, m, tok], wdn_sb[:, m, cb * 512:(cb + 1) * 512], m == 0, m == NFF - 1,
                         aTk + [("wdn", m)], [("ps", 6 + cb)])
                k.tt(yt[:, cb * 512:(cb + 1) * 512], ps[6 + cb][:, :], gate_f_row[:, cb * 512:(cb + 1) * 512], ALU.mult,
                     [("ps", 6 + cb), ("gate_f_row", 0), ("gate_f_row", 1)], [("yt", cb)])

    def DOWN2(s, t):
        if True:
            T = 4 * s + t
            a = T % 2
            k.tt(yt[:, :], yt[:, :], xb[a][:, :], ALU.add, [("yt", 0), ("yt", 1), ("xb", 0)], [("yt", 0), ("yt", 1)],
                 eng="pool")
            k.act(yo[:, :], yt[:, :], AF.Square, [("yt", 0), ("yt", 1)], ["yo", ("ss3", a)], accum_out=ss3[:, a:a + 1])
            k.ts(ms3[:, a:a + 1], ss3[:, a:a + 1], 1.0 / D, EPS, ALU.mult, ALU.add, [("ss3", a)], [("ms3", a)])
            k.tt(rstd3[:, a:a + 1], ms3[:, a:a + 1], neg_half[:, 0:1], ALU.pow, [("ms3", a), "neg_half2"],
                 [("rstd3", a)], eng="pool")
            k.stt(yo[:, :], yt[:, :], rstd3[:, a:a + 1], gfin[:, :], ALU.mult, ALU.mult,
                  [("yt", 0), ("yt", 1), ("rstd3", a), "gfin"], ["yo"])
            k.dma(yp[T * 128:(T + 1) * 128, :], yo[:, :], ["yo"], [("yp", T)])

    N2load(0, 0)
    N2load(0, 1)
    wload[0]()
    N2pre(0, 0, load=False, part="stats")
    N2pre(0, 1, load=False, part="stats")
    for t in range(4):
        N2pre(0, t, load=False, part="copy")
        N2tr(0, t)
        if t + 2 < 4:
            N2load(0, t + 2)
            N2pre(0, t + 2, load=False, part="stats")
    wload[1]()
    wload[2]()
    wload[3]()
    for s in range(NST):
        UP(s)
        if s == NST - 1:
            k.tr(ps[0][0:44, 0:128], ucar[:].rearrange("p k m -> p (k m)"), ident[:],
                 [("ucar", m) for m in range(NFF)] + ["ident"], [("ps", 0)])
            k.copy(st44[:, :], ps[0][0:44, 0:128], [("ps", 0)], ["st44"])
            k.dma(o_cfp.rearrange("k (m p) -> (k m) p", p=128), st44[:, :], ["st44"], ["o_cfp"])
        for t in range(4):
            if s + 1 < NST:
                N2pre(s + 1, t)
            DOWN(s, t)
            if s + 1 < NST:
                N2tr(s + 1, t)
            DOWN2(s, t)


def _rms_rstd(k, sbf, pfx, x, junk):
    ss = sbf(pfx + "_ss", [NS, 1]); ms = sbf(pfx + "_ms", [NS, 1]); rstd = sbf(pfx + "_rstd", [NS, 1])
    nh = sbf(pfx + "_nh", [NS, 1])
    k.memset(nh[:], -0.5, [pfx + "nh"])
    k.act(junk, x, AF.Square, [pfx + "x"], [pfx + "junk", pfx + "ss"], accum_out=ss[:, 0:1])
    k.ts(ms[:], ss[:], 1.0 / D, EPS, ALU.mult, ALU.add, [pfx + "ss"], [pfx + "ms"])
    k.tt(rstd[:], ms[:], nh[:], ALU.pow, [pfx + "ms", pfx + "nh"], [pfx + "rstd"], eng="pool")
    return rstd


def _to_fm(k, ps_bank, bank_id, src, nchunk, dst, ident, rkeys, wkey):
    for c in range(nchunk):
        k.tr(ps_bank[:, c * NS:(c + 1) * NS], src[0:NS, c * 128:(c + 1) * 128], ident[0:NS, 0:NS],
             rkeys + ["ident"], [("ps", bank_id)])
    k.copy(dst[:].rearrange("p a b -> p (a b)"), ps_bank[:, 0:nchunk * NS], [("ps", bank_id)], [wkey])


def sample_a(nc, k, sbf, ps, ident, mod, modkeys, cd, w_in_sb, early):
    x = early["x"]; grow = early["cat"]; proj = early["proj"]; qr = early["qr"]; kr = early["kr"]
    xn = sbf("sm_xn", [NS, D]); gs = sbf("sm_gs", [NS, D])
    rstd = _rms_rstd(k, sbf, "sm", x[:], gs[:])
    k.act(xn[:], x[:], AF.Copy, ["smx", "smrstd"], ["sm_xn"], scale=rstd[:, 0:1])
    k.stt(gs[:], mod[0:NS, D:2 * D], 1.0, grow[:], ALU.add, ALU.mult, modkeys + ["sm_grow", "smjunk"], ["sm_gs", "smjunk"])
    k.tt(xn[:], xn[:], gs[:], ALU.mult, ["sm_xn", "sm_gs"], ["sm_xn"])
    k.tt(xn[:], xn[:], mod[0:NS, 0:D], ALU.add, ["sm_xn"] + modkeys, ["sm_xn"])
    hT = sbf("sm_hT", [128, 8, NS], BF16)
    _to_fm(k, ps[0], 0, xn, 8, hT, ident, ["sm_xn"], "sm_hT")
    for cb in range(6):
        b = 1 + cb % 2
        for kc in range(8):
            k.mm(ps[b][0:NS, :], hT[:, kc, :], w_in_sb[:, kc, cb * 512:(cb + 1) * 512], kc == 0, kc == 7,
                 ["sm_hT", ("w_in", kc)], [("ps", b)])
        k.copy(proj[:, cb * 512:(cb + 1) * 512], ps[b][0:NS, :], [("ps", b)], [("sm_proj", cb)], eng="act")

    def p3(cb):
        return proj[:, cb * 512:(cb + 1) * 512].rearrange("p (h d) -> p h d", h=4)

    rot = sbf("sm_rot", [NS, 4, 64])
    k.dma(rot[:], cd["rotS"].partition_broadcast(NS), (), ["sm_rot"])
    mt = [sbf("sm_m%d" % i, [NS, 4, 64]) for i in range(4)]
    for j, dst in enumerate([qr, kr]):
        src3 = p3(j)
        cosb = rot[:, 2 * j:2 * j + 1, :].broadcast_to([NS, 4, 64])
        sinb = rot[:, 2 * j + 1:2 * j + 2, :].broadcast_to([NS, 4, 64])
        d3 = dst[:, :].rearrange("p (h d) -> p h d", h=4)
        k.tt(mt[0][:], src3[:, :, 0:64], cosb, ALU.mult, [("sm_proj", j), "sm_rot"], [("sm_m", 0)])
        k.tt(mt[1][:], src3[:, :, 64:128], sinb, ALU.mult, [("sm_proj", j), "sm_rot"], [("sm_m", 1)])
        k.tt(mt[2][:], src3[:, :, 0:64], sinb, ALU.mult, [("sm_proj", j), "sm_rot"], [("sm_m", 2)])
        k.tt(mt[3][:], src3[:, :, 64:128], cosb, ALU.mult, [("sm_proj", j), "sm_rot"], [("sm_m", 3)])
        k.tt(d3[:, :, 0:64], mt[0][:], mt[1][:], ALU.subtract, [("sm_m", 0), ("sm_m", 1)], [("sm_rot_o", j, 0)])
        k.tt(d3[:, :, 64:128], mt[2][:], mt[3][:], ALU.add, [("sm_m", 2), ("sm_m", 3)], [("sm_rot_o", j, 1)])


def sample_mixer(nc, k, es, ps, ident, xs, mod, modkeys, rows_d, cd, w_in_sb, w_out_sb, wr_sb, wi_sb,
                 st_ret, st_h, st_cl, o_rets, o_hs, o_cls, x1s_scr, early):
    S = k.S

    def sb(name, shape, dt=F32):
        return es.enter_context(nc.sbuf_tensor("sb_" + name, list(shape), dt))

    g1 = CONST["g1"]
    x = early["x"]; cat = early["cat"]; proj = early["proj"]; qr = early["qr"]; kr = early["kr"]
    S_s = sb("sm_S", [128, NS, 4, 128])
    sv = st_ret.rearrange("b h k v -> k b h v")
    for h_ in range(4):
        k.dma(S_s[:, :, h_, :], sv[:, :, h_, :], (), [("sm_S_in", h_)])
    es = ExitStack()
    es.__enter__()

    def p3(cb):
        return proj[:, cb * 512:(cb + 1) * 512].rearrange("p (h d) -> p h d", h=4)

    qrk = []
    krk = []
    stc = early["stc"]; clb = early["clb"]; brr = early["brr"]; bir = early["bir"]; lamr = early["lamr"]
    clw = sb("sm_clw", [NS, 4, 512]); h0 = sb("sm_h0", [NS, 512])
    k.dma(clw[:], rows_d["clw"].partition_broadcast(NS), (), ["sm_clw"])
    k.dma(h0[:], st_h, (), ["sm_h0"])
    xl = proj[:, 2048:2560]
    k.dma(o_cls[:, 0:2, :], stc[:, 1:3, :], ["sm_stc"], ["o_cls01"])
    k.dma(o_cls[:, 2, :], xl, [("sm_proj", 4)], ["o_cls2"])
    xcs = sb("sm_xcs", [NS, 512]); t2 = sb("sm_t2", [NS, 512])
    k.tt(xcs[:], xl, clw[:, 3, :], ALU.mult, [("sm_proj", 4), "sm_clw"], ["sm_xcs"])
    k.tt(xcs[:], xcs[:], clb[:], ALU.add, ["sm_xcs", "sm_clb"], ["sm_xcs"])
    for kk in range(3):
        k.tt(t2[:], stc[:, kk, :], clw[:, kk, :], ALU.mult, ["sm_stc", "sm_clw"], ["sm_t2"])
        k.tt(xcs[:], xcs[:], t2[:], ALU.add, ["sm_xcs", "sm_t2"], ["sm_xcs"])
    xcT = sb("sm_xcT", [128, 4, NS], BF16)
    _to_fm(k, ps[3], 3, xcs, 4, xcT, ident, ["sm_xcs"], "sm_xcT")
    for c in range(4):
        cs_ = slice(c * 128, (c + 1) * 128)
        k.mm(ps[1][0:NS, cs_], xcT[:, c, :], wr_sb[:, c, :], True, True, ["sm_xcT", "wr_sb"], [("ps", 1)])
        k.mm(ps[2][0:NS, cs_], xcT[:, c, :], wi_sb[:, c, :], True, True, ["sm_xcT", "wi_sb"], [("ps", 2)])
    rg = sb("sm_rg", [NS, 512]); igs = sb("sm_ig", [NS, 512]); cfr = lamr; av = sb("sm_a", [NS, 512])
    k.tt(rg[:], ps[1][0:NS, :], brr[:], ALU.add, [("ps", 1), "sm_brr"], ["sm_rg"])
    k.tt(igs[:], ps[2][0:NS, :], bir[:], ALU.add, [("ps", 2), "sm_bir"], ["sm_ig"])
    k.act(rg[:], rg[:], AF.Sigmoid, ["sm_rg"], ["sm_rg"])
    k.act(igs[:], igs[:], AF.Sigmoid, ["sm_ig"], ["sm_ig"])
    k.stt(rg[:], cfr[:], -8.0, rg[:], ALU.mult, ALU.mult, ["sm_rg"], ["sm_rg"])
    k.act(av[:], rg[:], AF.Exp, ["sm_rg"], ["sm_a"])
    k.act(rg[:], rg[:], AF.Exp, ["sm_rg"], ["sm_rg"], scale=2.0)
    k.act(rg[:], rg[:], AF.Sqrt, ["sm_rg"], ["sm_rg"], scale=-1.0, bias=1.0)
    k.tt(igs[:], igs[:], xcs[:], ALU.mult, ["sm_ig", "sm_xcs"], ["sm_ig"])
    k.tt(igs[:], igs[:], rg[:], ALU.mult, ["sm_ig", "sm_rg"], ["sm_ig"])
    k.tt(av[:], av[:], h0[:], ALU.mult, ["sm_a", "sm_h0"], ["sm_a"])
    k.tt(av[:], av[:], igs[:], ALU.add, ["sm_a", "sm_ig"], ["sm_a"])
    k.dma(o_hs, av[:], ["sm_a"], ["o_hs"])
    k.act(t2[:], proj[:, 2560:3072], AF.Gelu_apprx_tanh, [("sm_proj", 5), "sm_t2"], ["sm_t2"])
    k.tt(cat[:, 512:1024], av[:], t2[:], ALU.mult, ["sm_a", "sm_t2"], [("sm_cat", 1)])
    barrier(S, keep=lambda kk: isinstance(kk, tuple) and kk[0] in ("o_rets", "sm_S_out", "sm_S_in"))
    es.__exit__(None, None, None)
    es = ExitStack()
    es.__enter__()
    tmp = sb("sm_tmp", [NS, 512]); qk = sb("sm_qk", [NS, 4]); o1 = sb("sm_o1", [NS, 512]); osb = sb("sm_o", [NS, 512])
    k.tt(tmp[:], qr[:], kr[:], ALU.mult, qrk + krk, ["sm_tmp"])
    S.add("dve", lambda e: e.tensor_reduce(qk[:], tmp[:, :].rearrange("p (h d) -> p h d", h=4), AX.X, ALU.add),
          ["sm_tmp"], ["sm_qk"])
    k.tt(o1[:, :].rearrange("p (h d) -> p h d", h=4), p3(2), qk[:, :].unsqueeze(2).broadcast_to([NS, 4, 128]),
         ALU.mult, [("sm_proj", 2), "sm_qk"], ["sm_o1"])
    Sk = [("sm_S_in", g) for g in range(4)]
    i16 = sb("sm_i16", [128, NS, NS])
    k.dma(i16[:], cd["i16"], (), ["sm_i16"])
    qT = sb("sm_qT", [128, 4, NS])
    _to_fm(k, ps[3], 3, qr, 4, qT, ident, qrk, "sm_qT")
    QM = sb("sm_QM", [128, 4, NS, NS])
    for h in range(4):
        k.tt(QM[:, h, :, :], qT[:, h, :].unsqueeze(2).broadcast_to([128, NS, NS]), i16[:], ALU.mult,
             ["sm_qT", "sm_i16"], [("sm_QM", h)])
    Vm = [sb("sm_Vm%d" % i, [NS, NS, 128], BF16) for i in range(4)]
    kr_bf = sb("sm_kr_bf", [NS, 512], BF16)
    k.copy(kr_bf[:], kr[:], krk, ["sm_kr_bf"], eng="act")
    for h in range(4):
        k.tt(Vm[h][:], p3(2)[:, h, :].unsqueeze(1).broadcast_to([NS, NS, 128]),
             ident[0:NS, 0:NS].unsqueeze(2).broadcast_to([NS, NS, 128]), ALU.mult,
             [("sm_proj", 2), "ident"], [("sm_Vm", h)])
    for h in range(4):
        for b in range(NS):
            k.mm(ps[4][0:NS, h * 128:(h + 1) * 128], QM[:, h, b, :], S_s[:, b, h, :], b == 0, b == NS - 1,
                 [("sm_QM", h), ("sm_S_in", h)], [("ps", 4)])
    for h in range(4):
        hs = slice(h * 128, (h + 1) * 128)
        k.stt(osb[:, hs], ps[4][0:NS, hs], g1[h], o1[:, hs], ALU.mult, ALU.add, [("ps", 4), "sm_o1"], [("sm_o", h)])
    v3 = p3(2)
    for h in range(4):
        vm = Vm[h]
        for g in range(4):
            bk = 4 + g if h % 2 == 0 else g
            k.mm(ps[bk][:, :], kr_bf[0:NS, h * 128:(h + 1) * 128], vm[0:NS, 4 * g:4 * g + 4, :], True, True,
                 ["sm_kr_bf", ("sm_Vm", h)], [("ps", bk)])
            k.stt(S_s[:, 4 * g:4 * g + 4, h, :], S_s[:, 4 * g:4 * g + 4, h, :], g1[h],
                  ps[bk][:, :].rearrange("p (b v) -> p b v", b=4), ALU.mult, ALU.add,
                  [("ps", bk)] + Sk, [("sm_S_out", g, h)])
    stats = sb("sm_stats", [NS, 4, 6]); mv = sb("sm_mv", [NS, 4, 2]); vpe = sb("sm_vpe", [NS, 4]); rs = sb("sm_rs", [NS, 4])
    nh4 = sb("sm_nh4", [NS, 4])
    k.memset(nh4[:], -0.5, ["sm_nh4"])
    for h in range(4):
        hs = slice(h * 128, (h + 1) * 128)
        S.add("dve", (lambda h=h, hs=hs: (lambda e: e.bn_stats(stats[:, h, :], osb[:, hs])))(), [("sm_o", h)],
              [("sm_stats", h)])
        S.add("dve", (lambda h=h: (lambda e: e.bn_aggr(mv[:, h, :], stats[:, h, :])))(), [("sm_stats", h)],
              [("sm_mv", h)])
    mvk = [("sm_mv", h) for h in range(4)]
    k.ts(vpe[:], mv[:, :, 1], EPS, None, ALU.add, None, mvk, ["sm_vpe"])
    k.tt(rs[:], vpe[:], nh4[:], ALU.pow, ["sm_vpe", "sm_nh4"], ["sm_rs"], eng="pool")
    sgs = sb("sm_sgs", [NS, 512])
    k.act(sgs[:], proj[:, 1536:2048], AF.Silu, [("sm_proj", 3)], ["sm_sgs"])
    for h in range(4):
        hs = slice(h * 128, (h + 1) * 128)
        k.ts(osb[:, hs], osb[:, hs], mv[:, h, 0:1], rs[:, h:h + 1], ALU.subtract, ALU.mult,
             [("sm_o", h), ("sm_mv", h), "sm_rs"], [("sm_o", h)])
    k.tt(cat[:, 0:512], osb[:], sgs[:], ALU.mult, [("sm_o", h) for h in range(4)] + ["sm_sgs"], [("sm_cat", 0)])
    ov = o_rets.rearrange("b h k v -> k b h v")
    for g in range(4):
        k.dma(ov[:, 4 * g:4 * g + 4, :, :], S_s[:, 4 * g:4 * g + 4, :, :], [("sm_S_out", g, h) for h in range(4)],
              [("o_rets", g)])
    catT = sb("sm_catT", [128, 8, NS], BF16)
    _to_fm(k, ps[0], 0, cat, 8, catT, ident, [("sm_cat", 0), ("sm_cat", 1)], "sm_catT")
    x1 = sb("sm_x1", [NS, D])
    for cb in range(2):
        b = 1 + cb
        for kc in range(8):
            k.mm(ps[b][0:NS, :], catT[:, kc, :], w_out_sb[:, kc, cb * 512:(cb + 1) * 512], kc == 0, kc == 7,
                 ["sm_catT", ("w_out", kc)], [("ps", b)])
        k.tt(x1[:, cb * 512:(cb + 1) * 512], ps[b][0:NS, :], mod[0:NS, 2 * D + cb * 512:2 * D + (cb + 1) * 512], ALU.mult,
             [("ps", b)] + modkeys, [("sm_x1", cb)])
    k.tt(x1[:], x1[:], x[:], ALU.add, [("sm_x1", 0), ("sm_x1", 1), "smx"], [("sm_x1", 0), ("sm_x1", 1)])
    k.dma(x1s_scr, x1[:], [("sm_x1", 0), ("sm_x1", 1)], ["x1s_scr"])
    es.__exit__(None, None, None)


def sample_ffn(nc, k, es, ps, ident, x1s_scr, modf_scr, rows_d, wuc_sb, wug_sb, wdn_sb, st_cf, o_cfs, ys):
    S = k.S

    def sb(name, shape, dt=F32):
        return es.enter_context(nc.sbuf_tensor("sb_" + name, list(shape), dt))

    x1 = sb("sf_x1", [NS, D]); modf = sb("sf_modf", [NS, 3 * D]); xn = sb("sf_xn", [NS, D]); junk = xn
    grow = sb("sf_grow", [NS, D]); gs = grow; gfin = sb("sf_gfin", [NS, D])
    k.dma(x1[:], x1s_scr, ["x1s_scr"], ["sfx"])
    k.dma(modf[:], modf_scr, ["modf_scr"], ["sf_modf"])
    k.dma(grow[:], rows_d["g_ffn"].partition_broadcast(NS), (), ["sf_grow"])
    k.dma(gfin[:], rows_d["g_final"].partition_broadcast(NS), (), ["sf_gfin"])
    rstd = _rms_rstd(k, sb, "sf", x1[:], junk[:])
    k.act(xn[:], x1[:], AF.Copy, ["sfx", "sfrstd"], ["sf_xn", "sfjunk"], scale=rstd[:, 0:1])
    k.stt(gs[:], modf[:, D:2 * D], 1.0, grow[:], ALU.add, ALU.mult, ["sf_modf", "sf_grow"], ["sf_grow"])
    k.tt(xn[:], xn[:], gs[:], ALU.mult, ["sf_xn", "sf_grow"], ["sf_xn"])
    k.tt(xn[:], xn[:], modf[:, 0:D], ALU.add, ["sf_xn", "sf_modf"], ["sf_xn"])
    hT = sb("sf_hT", [128, 8, NS], BF16)
    _to_fm(k, ps[0], 0, xn, 8, hT, ident, ["sf_xn"], "sf_hT")
    aT = sb("sf_aT", [128, NFF, NS], BF16)
    cfw_v = rows_d["cfw"].rearrange("(k f) -> k f", k=3)
    blocks = [(0, 512), (512, 512), (1024, 512), (1536, 512), (2048, 512), (2560, 256)]
    bufs = {}
    for a in range(2):
        bufs[a] = dict(
            cfw=sb("sf_cfw%d" % a, [NS, 3, 512]), cfb=sb("sf_cfb%d" % a, [NS, 512]), stf=sb("sf_stf%d" % a, [NS, 2, 512]),
            u=sb("sf_u%d" % a, [NS, 512]), uc=sb("sf_uc%d" % a, [NS, 512]), t=sb("sf_t%d" % a, [NS, 512]))

    def stage1(bi):
        c0, wd = blocks[bi]
        a = bi % 2
        B = bufs[a]
        cs_ = slice(c0, c0 + wd)
        k.dma(B["cfw"][:, :, 0:wd], cfw_v[:, cs_].partition_broadcast(NS), (), [("sf_cfw", a)])
        k.dma(B["cfb"][:, 0:wd], rows_d["cfb"][cs_].partition_broadcast(NS), (), [("sf_cfb", a)])
        k.dma(B["stf"][:, :, 0:wd], st_cf[:, :, cs_], (), [("sf_stf", a)])
        bu, bg = 1 + 2 * a, 2 + 2 * a
        for kc in range(8):
            k.mm(ps[bu][0:NS, 0:wd], hT[:, kc, :], wuc_sb[:, kc, cs_], kc == 0, kc == 7,
                 ["sf_hT", ("wuc", kc, c0 // 1024)], [("ps", bu)])
        for kc in range(8):
            k.mm(ps[bg][0:NS, 0:wd], hT[:, kc, :], wug_sb[:, kc, cs_], kc == 0, kc == 7,
                 ["sf_hT", ("wug", kc, c0 // 1024)], [("ps", bg)])
        tt_ = B["t"]
        for kk in range(2):
            k.tt(tt_[:, 0:wd], B["stf"][:, kk, 0:wd], B["cfw"][:, kk, 0:wd], ALU.mult, [("sf_stf", a), ("sf_cfw", a)],
                 [("sf_t", a)])
            k.tt(B["cfb"][:, 0:wd], B["cfb"][:, 0:wd], tt_[:, 0:wd], ALU.add, [("sf_cfb", a), ("sf_t", a)],
                 [("sf_cfb", a)])
        k.dma(o_cfs[:, 0, cs_], B["stf"][:, 1, 0:wd], [("sf_stf", a)], [("o_cfs0", bi)], eng="act")

    def stage2(bi):
        c0, wd = blocks[bi]
        a = bi % 2
        B = bufs[a]
        cs_ = slice(c0, c0 + wd)
        bu, bg = 1 + 2 * a, 2 + 2 * a
        u = B["u"]; uc = B["uc"]
        k.tt(uc[:, 0:wd], ps[bu][0:NS, 0:wd], B["cfw"][:, 2, 0:wd], ALU.mult, [("ps", bu), ("sf_cfw", a)], [("sf_uc", a)])
        k.copy(u[:, 0:wd], ps[bu][0:NS, 0:wd], [("ps", bu)], [("sf_u", a)], eng="act")
        k.tt(uc[:, 0:wd], uc[:, 0:wd], B["cfb"][:, 0:wd], ALU.add, [("sf_uc", a), ("sf_cfb", a)], [("sf_uc", a)])
        k.act(uc[:, 0:wd], uc[:, 0:wd], AF.Gelu_apprx_tanh, [("sf_uc", a)], [("sf_uc", a)])
        k.tt(uc[:, 0:wd], uc[:, 0:wd], ps[bg][0:NS, 0:wd], ALU.mult, [("sf_uc", a), ("ps", bg)], [("sf_uc", a)])
        k.dma(o_cfs[:, 1, cs_], u[:, 0:wd], [("sf_u", a)], [("o_cfs1", bi)], eng="act")
        nchk = wd // 128
        for c in range(nchk):
            k.tr(ps[5 + a][:, c * NS:(c + 1) * NS], uc[0:NS, c * 128:(c + 1) * 128], ident[0:NS, 0:NS],
                 [("sf_uc", a), "ident"], [("ps", 5 + a)])
        m0 = c0 // 128
        k.copy(aT[:, m0:m0 + nchk, :].rearrange("p a b -> p (a b)"), ps[5 + a][:, 0:nchk * NS], [("ps", 5 + a)],
               [("sf_aT", bi)])

    stage1(0)
    for bi in range(6):
        if bi + 1 < 6:
            stage1(bi + 1)
        stage2(bi)
    aTk = [("sf_aT", bi) for bi in range(6)]
    y = xn
    for cb in range(2):
        b = 1 + cb
        for m in range(NFF):
            k.mm(ps[b][0:NS, :], aT[:, m, :], wdn_sb[:, m, cb * 512:(cb + 1) * 512], m == 0, m == NFF - 1,
                 aTk + [("wdn", m)], [("ps", b)])
        k.tt(y[:, cb * 512:(cb + 1) * 512], ps[b][0:NS, :], modf[:, 2 * D + cb * 512:2 * D + (cb + 1) * 512], ALU.mult,
             [("ps", b), "sf_modf"], [("sf_y", cb), "sf_xn"])
    yk = [("sf_y", 0), ("sf_y", 1)]
    k.tt(y[:], y[:], x1[:], ALU.add, yk + ["sfx"], yk)
    ss = sb("sf_ss2", [NS, 1]); ms = sb("sf_ms2", [NS, 1]); r2 = sb("sf_r2", [NS, 1]); nh = sb("sf_nh2", [NS, 1])
    k.memset(nh[:], -0.5, ["sf_nh2"])
    k.act(grow[:], y[:], AF.Square, yk + ["sf_grow"], ["sf_grow", "sf_ss2"], accum_out=ss[:, 0:1])
    k.ts(ms[:], ss[:], 1.0 / D, EPS, ALU.mult, ALU.add, ["sf_ss2"], ["sf_ms2"])
    k.tt(r2[:], ms[:], nh[:], ALU.pow, ["sf_ms2", "sf_nh2"], ["sf_r2"], eng="pool")
    k.stt(y[:], y[:], r2[:, 0:1], gfin[:], ALU.mult, ALU.mult, yk + ["sf_r2", "sf_gfin"], yk)
    k.dma(ys, y[:], yk, ["ys"])


_NC_CACHE = {}


def _blockdiag(w):
    out = np.zeros((128, 4, 128), np.float32)
    for n in range(8):
        c, hh = n // 2, n % 2
        out[hh * 64:(hh + 1) * 64, c, hh * 64:(hh + 1) * 64] = w[n]
    return out


def _fm(v):
    return np.ascontiguousarray(v.reshape(-1, 128).T)


def kernel(x_prompt, x_sample, c_prompt, c_sample, state_ret, state_lru_h, state_lru_conv, state_ffn_conv,
           w_ada, b_ada, g_mix, w_in, conv_lru_w, conv_lru_b, w_r, b_r, w_i, b_i, lam, w_out, g_ffn,
           w_up_conv, w_up_gate, conv_ffn_w, conv_ffn_b, w_down, g_final):
    f = lambda a: np.ascontiguousarray(np.asarray(a, dtype=np.float32))
    if "nc" not in _NC_CACHE:
        _NC_CACHE["nc"] = build_program()
    nc = _NC_CACHE["nc"]
    clw_fm = np.stack([_fm(f(conv_lru_w)[0, kk]) for kk in range(4)], axis=2).reshape(128, 16)
    cfw_fm = np.stack([_fm(f(conv_ffn_w)[0, kk]) for kk in range(3)], axis=2).reshape(128, 66)
    vfm = np.concatenate([_fm(f(g_mix)[0]), _fm(f(g_ffn)[0]), clw_fm, _fm(f(conv_lru_b)[0]), _fm(f(b_r)[0]),
                          _fm(f(b_i)[0]), _fm(f(lam)[0]), cfw_fm, _fm(f(conv_ffn_b)[0])], axis=1)
    shared = {
        "w_ada": f(w_ada)[0], "b_ada": f(b_ada)[0], "w_in": f(w_in)[0], "w_out": f(w_out)[0],
        "w_upc": f(w_up_conv)[0], "w_upg": f(w_up_gate)[0], "w_dn": f(w_down)[0],
        "wr_bd": _blockdiag(f(w_r)[0]), "wi_bd": _blockdiag(f(w_i)[0]), "vfm": np.ascontiguousarray(vfm),
        "row_g_mix": f(g_mix)[0], "row_g_ffn": f(g_ffn)[0], "row_g_final": f(g_final),
        "row_clw": f(conv_lru_w)[0].reshape(-1), "row_clb": f(conv_lru_b)[0], "row_b_r": f(b_r)[0],
        "row_b_i": f(b_i)[0], "row_lam": f(lam)[0], "row_cfw": f(conv_ffn_w)[0].reshape(-1),
        "row_cfb": f(conv_ffn_b)[0],
    }
    for n in ["ident", "maskT", "cosP", "sinP", "dqT", "dkT", "dkP", "rotS", "sel", "i16"]:
        shared["c_" + n] = CONST[n]
    in_maps = []
    for i in range(N_CORES):
        sl = slice(i * NS, (i + 1) * NS)
        m = dict(shared)
        m["xp"] = f(x_prompt)[i]
        m["xs"] = f(x_sample)[sl, 0, :]
        m["c17"] = np.concatenate([f(c_sample)[sl], f(c_prompt)[i:i + 1]], axis=0)
        m["st_ret"] = f(state_ret)[0, sl]
        m["st_h"] = f(state_lru_h)[0, sl]
        m["st_cl"] = f(state_lru_conv)[0, sl]
        m["st_cf"] = f(state_ffn_conv)[0, sl]
        in_maps.append(m)
    res = run_bass_kernel_spmd(nc, in_maps, core_ids=list(range(N_CORES)))
    R = res.results
    y_prompt = np.stack([R[i]["yp"] for i in range(N_CORES)], axis=0)
    y_sample = np.concatenate([R[i]["ys"] for i in range(N_CORES)], axis=0)[:, None, :]
    ret_p = np.stack([R[i]["o_retp"] for i in range(N_CORES)], axis=0)[None]
    h_p = np.stack([R[i]["o_hp"].reshape(512) for i in range(N_CORES)], axis=0)[None]
    cl_p = np.stack([R[i]["o_clp"] for i in range(N_CORES)], axis=0)[None]
    cf_p = np.stack([R[i]["o_cfp"] for i in range(N_CORES)], axis=0)[None]
    ret_s = np.concatenate([R[i]["o_rets"] for i in range(N_CORES)], axis=0)[None]
    h_s = np.concatenate([R[i]["o_hs"] for i in range(N_CORES)], axis=0)[None]
    cl_s = np.concatenate([R[i]["o_cls"] for i in range(N_CORES)], axis=0)[None]
    cf_s = np.concatenate([R[i]["o_cfs"] for i in range(N_CORES)], axis=0)[None]
    outs = (y_prompt, y_sample, ret_p, h_p, cl_p, cf_p, ret_s, h_s, cl_s, cf_s)
    return tuple(np.ascontiguousarray(o, dtype=np.float32) for o in outs)
```

```python
import math
from contextlib import ExitStack

import numpy as np
import concourse.bass as bass
import concourse.mybir as mybir
from concourse.bass_utils import run_bass_kernel_spmd

F32 = mybir.dt.float32
BF16 = mybir.dt.bfloat16
AF = mybir.ActivationFunctionType
ALU = mybir.AluOpType
AX = mybir.AxisListType

D = 1024
SEQ = 2048
NS = 16
DFF = 2816
NFF = 22
DPROJ = 3072
NT = 16
NST = 4
EPS = 1e-6
PAST_LEN = 16384
N_CORES = 8
NDMASEM = 32


class Op:
    __slots__ = ("eng", "fn", "deps", "dma", "sem", "val", "flag", "idx")


class Sched:
    def __init__(self):
        self.ops = []
        self.lastw = {}
        self.lastr = {}
        self.dma_last = [None] * NDMASEM
        self.dma_cnt = 0
        self.dma_cnt_sw = 0

    def add(self, eng, fn, reads=(), writes=(), dma=False):
        op = Op()
        op.eng, op.fn, op.dma, op.flag = eng, fn, dma, dma
        op.idx = len(self.ops)
        op.sem = None
        op.val = 0
        deps = set()
        for k in reads:
            w = self.lastw.get(k)
            if w is not None:
                deps.add(w)
            if isinstance(k, tuple) and k[0] == "ps":
                for e2, i2 in self.lastr.get(k, {}).items():
                    if e2 != eng:
                        deps.add(i2)
        for k in writes:
            w = self.lastw.get(k)
            if w is not None:
                deps.add(w)
            r = self.lastr.get(k)
            if r:
                deps.update(r.values())
        ek = ("dma", op.idx) if dma else eng
        for k in reads:
            self.lastr.setdefault(k, {})[ek] = op.idx
        for k in writes:
            self.lastw[k] = op.idx
            self.lastr[k] = {}
        if dma:
            half = NDMASEM // 2
            if eng == "pool":
                slot = half + self.dma_cnt_sw % half
                self.dma_cnt_sw += 1
            else:
                slot = self.dma_cnt % half
                self.dma_cnt += 1
            op.sem = ("dma", slot)
            prev = self.dma_last[slot]
            if prev is not None:
                deps.add(prev)
            self.dma_last[slot] = op.idx
        deps.discard(op.idx)
        op.deps = deps
        self.ops.append(op)
        return op.idx

    def emit(self, nc, block, sems, dma_sems):
        ops = self.ops
        for op in ops:
            for d in op.deps:
                dep = ops[d]
                if dep.eng == "pe" and op.eng == "pe" and not dep.dma and not op.dma:
                    continue
                dep.flag = True
        cnt = {e: 0 for e in sems}
        dcnt = [0] * NDMASEM
        for op in ops:
            if op.dma:
                s = op.sem[1]
                dcnt[s] += 16
                op.val = dcnt[s]
            elif op.flag:
                cnt[op.eng] += 1
                op.val = cnt[op.eng]
                op.sem = ("eng", op.eng)
        per_eng = {e: [] for e in sems}
        for op in ops:
            per_eng[op.eng].append(op)

        def run(engname, handle):
            waited = {}
            for op in per_eng[engname]:
                need = {}
                for d in op.deps:
                    dep = ops[d]
                    if dep.eng == "pe" and op.eng == "pe" and not dep.dma and not op.dma:
                        continue
                    if dep.val > need.get(dep.sem, 0):
                        need[dep.sem] = dep.val
                for sk, v in need.items():
                    if waited.get(sk, 0) >= v:
                        continue
                    waited[sk] = v
                    sem = dma_sems[sk[1]] if sk[0] == "dma" else sems[sk[1]]
                    handle.wait_ge(sem, v)
                inst = op.fn(handle)
                if op.dma:
                    inst.then_inc(dma_sems[op.sem[1]], 16)
                elif op.flag:
                    inst.then_inc(sems[op.eng], 1)
            if engname == "sp":
                for s in range(NDMASEM):
                    if dcnt[s] > 0:
                        handle.wait_ge(dma_sems[s], dcnt[s])

        @block.sync
        def _(e):
            run("sp", e)

        @block.tensor
        def _(e):
            run("pe", e)

        @block.scalar
        def _(e):
            run("act", e)

        @block.vector
        def _(e):
            run("dve", e)

        @block.gpsimd
        def _(e):
            run("pool", e)


class K:
    def __init__(self, S):
        self.S = S

    def mm(self, out, lhsT, rhs, start, stop, r, w):
        return self.S.add("pe", lambda e: e.matmul(out, lhsT, rhs, start=start, stop=stop), r, w)

    def tr(self, out, in_, ident, r, w):
        return self.S.add("pe", lambda e: e.transpose(out, in_, ident), r, w)

    def act(self, out, in_, func, r, w, bias=None, scale=None, accum_out=None, eng="act"):
        kw = {}
        if bias is not None:
            kw["bias"] = bias
        if scale is not None:
            kw["scale"] = scale
        if accum_out is not None:
            kw["accum_out"] = accum_out
        return self.S.add(eng, lambda e: e.activation(out, in_, func, **kw), r, w)

    def tt(self, out, in0, in1, op, r, w, eng="dve"):
        return self.S.add(eng, lambda e: e.tensor_tensor(out, in0, in1, op), r, w)

    def ts(self, out, in0, s1, s2, op0, op1, r, w, eng="dve"):
        if op1 is None:
            return self.S.add(eng, lambda e: e.tensor_scalar(out, in0, s1, None, op0), r, w)
        return self.S.add(eng, lambda e: e.tensor_scalar(out, in0, s1, s2, op0, op1), r, w)

    def stt(self, out, in0, scalar, in1, op0, op1, r, w):
        return self.S.add("dve", lambda e: e.scalar_tensor_tensor(out, in0, scalar, in1, op0, op1), r, w)

    def copy(self, out, in_, r, w, eng="dve"):
        if eng == "act":
            return self.S.add(eng, lambda e: e.activation(out, in_, AF.Copy), r, w)
        return self.S.add(eng, lambda e: e.tensor_copy(out, in_), r, w)

    def memset(self, ap, val, w, eng="dve"):
        return self.S.add(eng, lambda e: e.memset(ap, val), (), w)

    def dma(self, out, in_, r, w, eng="sp", **kw):
        return self.S.add(eng, lambda e: e.dma_start(out=out, in_=in_, **kw), r, w, dma=True)


def _consts():
    c = {}
    c["ident"] = np.eye(128, dtype=np.float32)
    j = np.arange(128)[:, None]
    i = np.arange(128)[None, :]
    c["maskT"] = (j <= i).astype(np.float32)
    inv_freq = (10000.0 ** (-np.arange(0, 128, 2, dtype=np.float32) / np.float32(128))).astype(np.float32)
    pos = np.arange(SEQ, dtype=np.float32)
    ang = (pos[:, None] * inv_freq[None, :]).astype(np.float32)
    cosP = np.cos(ang).astype(np.float32).reshape(NT, 128, 64).transpose(1, 0, 2)
    sinP = np.sin(ang).astype(np.float32).reshape(NT, 128, 64).transpose(1, 0, 2)
    c["cosP"] = np.ascontiguousarray(cosP)
    c["sinP"] = np.ascontiguousarray(sinP)
    log_g = np.log(1.0 - 2.0 ** (-5.0 - np.arange(4, dtype=np.float64)))
    ii = np.arange(128, dtype=np.float64)
    dq = np.exp((ii[None, :] + 1.0) * log_g[:, None])
    dk = (128.0 ** -0.5) * np.exp(-(ii[None, :] + 1.0) * log_g[:, None])
    c["dqT"] = np.ascontiguousarray(np.broadcast_to(dq[None], (128, 4, 128))).astype(np.float32)
    c["dkT"] = np.ascontiguousarray(np.broadcast_to(dk[None], (128, 4, 128))).astype(np.float32)
    c["dkP"] = np.ascontiguousarray(dk.T).astype(np.float32)
    c["gC"] = [float(np.exp(128.0 * log_g[h])) for h in range(4)]
    c["g1"] = [float(np.exp(log_g[h])) for h in range(4)]
    angS = (np.float32(PAST_LEN) * inv_freq).astype(np.float32)
    cs = np.cos(angS).astype(np.float32)
    sn = np.sin(angS).astype(np.float32)
    sc = np.float32(128.0 ** -0.5)
    c["rotS"] = np.stack([cs, sn, cs * sc, sn * sc]).astype(np.float32)
    sel = np.zeros((17, 129), np.float32)
    sel[16, :] = 1.0
    c["sel"] = sel
    c["i16"] = np.ascontiguousarray(np.broadcast_to(np.eye(16, dtype=np.float32)[None], (128, 16, 16)))
    return c


CONST = _consts()

VFM = {}
_off = 0
for _n, _w in [("g_mix", 8), ("g_ffn", 8), ("clw", 16), ("clb", 4), ("b_r", 4), ("b_i", 4), ("lam", 4),
               ("cfw", 66), ("cfb", 22)]:
    VFM[_n] = (_off, _w)
    _off += _w
NVFM = _off


import os
KPH = int(os.environ.get("KPHASE", "99"))


def build_program():
    nc = bass.Bass("TRN2", target_bir_lowering=False)
    S = Sched()
    k = K(S)

    def din(name, shape):
        return nc.dram_tensor(name, list(shape), F32, kind="ExternalInput").ap()

    def dout(name, shape):
        return nc.dram_tensor(name, list(shape), F32, kind="ExternalOutput").ap()

    xp = din("xp", [SEQ, D])
    xs = din("xs", [NS, D])
    c17 = din("c17", [17, D])
    st_ret = din("st_ret", [NS, 4, 128, 128])
    st_h = din("st_h", [NS, 512])
    st_cl = din("st_cl", [NS, 3, 512])
    st_cf = din("st_cf", [NS, 2, DFF])
    w_ada = din("w_ada", [D, 6 * D])
    b_ada = din("b_ada", [6 * D])
    w_in = din("w_in", [D, DPROJ])
    w_out = din("w_out", [D, D])
    w_upc = din("w_upc", [D, DFF])
    w_upg = din("w_upg", [D, DFF])
    w_dn = din("w_dn", [DFF, D])
    wr_bd = din("wr_bd", [128, 4, 128])
    wi_bd = din("wi_bd", [128, 4, 128])
    vfm_d = din("vfm", [128, NVFM])
    rows_d = {}
    for n, ln in [("g_mix", D), ("g_ffn", D), ("g_final", D), ("clw", 4 * 512), ("clb", 512), ("b_r", 512),
                  ("b_i", 512), ("lam", 512), ("cfw", 3 * DFF), ("cfb", DFF)]:
        rows_d[n] = din("row_" + n, [ln])
    cd = {}
    for n in ["ident", "maskT", "cosP", "sinP", "dqT", "dkT", "dkP", "rotS", "sel", "i16"]:
        cd[n] = din("c_" + n, CONST[n].shape)

    yp = dout("yp", [SEQ, D])
    ys = dout("ys", [NS, D])
    o_retp = dout("o_retp", [4, 128, 128])
    o_hp = dout("o_hp", [4, 128])
    o_clp = dout("o_clp", [3, 512])
    o_cfp = dout("o_cfp", [2, DFF])
    o_rets = dout("o_rets", [NS, 4, 128, 128])
    o_hs = dout("o_hs", [NS, 512])
    o_cls = dout("o_cls", [NS, 3, 512])
    o_cfs = dout("o_cfs", [NS, 2, DFF])

    x1_scr = nc.dram_tensor("x1_scr", [SEQ, D], F32, kind="Internal").ap()
    x1s_scr = nc.dram_tensor("x1s_scr", [NS, D], F32, kind="Internal").ap()
    modf_scr = nc.dram_tensor("modf_scr", [NS, 3 * D], F32, kind="Internal").ap()
    gatef_scr = nc.dram_tensor("gatef_scr", [1, D], F32, kind="Internal").ap()

    es = ExitStack()
    with es:
        def sb(name, shape, dt=F32):
            return es.enter_context(nc.sbuf_tensor("sb_" + name, list(shape), dt))

        ps = [es.enter_context(nc.psum_tensor("ps%d" % i, [128, 512], F32)) for i in range(8)]
        sems = {e: es.enter_context(nc.semaphore("sem_" + e)) for e in ["pe", "act", "dve", "pool", "sp"]}
        dma_sems = [es.enter_context(nc.semaphore("dsem%d" % i)) for i in range(NDMASEM)]

        ident = sb("ident", [128, 128])
        identb = sb("identb", [128, 128], BF16)
        vfm = sb("vfm", [128, NVFM])
        modfm = sb("modfm", [128, 4, 8])
        gsm = sb("gsm", [128, 8])
        gsf = sb("gsf", [128, 8])
        cf_fm = sb("cf_fm", [128, 4])
        tmp_fm = sb("tmp_fm", [128, 4])

        k.dma(ident[:], cd["ident"], (), ["ident"])
        k.copy(identb[:], ident[:], ["ident"], ["identb"], eng="act")
        k.dma(vfm[:], vfm_d, (), ["vfm"])

        def V(name):
            o, w = VFM[name]
            return vfm[:, o:o + w]

        k.act(tmp_fm[:], V("lam"), AF.Exp, ["vfm"], ["tmp_fm"], scale=-1.0)
        k.act(tmp_fm[:], tmp_fm[:], AF.Ln, ["tmp_fm"], ["tmp_fm"], bias=1.0)
        k.ts(cf_fm[:], tmp_fm[:], -8.0, None, ALU.mult, None, ["tmp_fm"], ["cf_fm"])

        esA = ExitStack()
        with esA:
            def sbA(name, shape, dt=F32):
                return esA.enter_context(nc.sbuf_tensor("sb_" + name, list(shape), dt))

            w_in_sb = sbA("w_in_sb", [128, 8, DPROJ], BF16)
            w_out_sb = sbA("w_out_sb", [128, 8, D], BF16)
            wr_sb = sbA("wr_sb", [128, 4, 128], BF16)
            wi_sb = sbA("wi_sb", [128, 4, 128], BF16)
            gate_m_row = sbA("gate_m_row", [128, D])

            es0 = ExitStack()
            with es0:
                def sb0(name, shape, dt=F32):
                    return es0.enter_context(nc.sbuf_tensor("sb_" + name, list(shape), dt))

                mod = sb0("mod", [17, 6 * D])
                early = dict(
                    x=sb0("sm_x", [NS, D]), cat=sb0("sm_cat", [NS, D]), stc=sb0("sm_stc", [NS, 3, 512]),
                    clb=sb0("sm_clb", [NS, 512]), brr=sb0("sm_brr", [NS, 512]), bir=sb0("sm_bir", [NS, 512]),
                    lamr=sb0("sm_lamr", [NS, 512]), proj=sb0("sm_proj", [NS, DPROJ]), qr=sb0("sm_qr", [NS, 512]),
                    kr=sb0("sm_kr", [NS, 512]))
                k.dma(early["x"][:], xs, (), ["smx"])
                k.dma(early["cat"][:], rows_d["g_mix"].partition_broadcast(NS), (), ["sm_grow"])
                k.dma(early["stc"][:], st_cl, (), ["sm_stc"])
                k.dma(early["clb"][:], rows_d["clb"].partition_broadcast(NS), (), ["sm_clb"])
                k.dma(early["brr"][:], rows_d["b_r"].partition_broadcast(NS), (), ["sm_brr"])
                k.dma(early["bir"][:], rows_d["b_i"].partition_broadcast(NS), (), ["sm_bir"])
                k.dma(early["lamr"][:], rows_d["lam"].partition_broadcast(NS), (), ["sm_lamr"])
                k.act(early["lamr"][:], early["lamr"][:], AF.Exp, ["sm_lamr"], ["sm_lamr"], scale=-1.0)
                k.act(early["lamr"][:], early["lamr"][:], AF.Ln, ["sm_lamr"], ["sm_lamr"], bias=1.0)
                esM = ExitStack()
                with esM:
                    def sbM(name, shape, dt=F32):
                        return esM.enter_context(nc.sbuf_tensor("sb_" + name, list(shape), dt))

                    c_sb = sbM("c_sb", [17, D])
                    sc_sb = sbM("sc_sb", [17, D])
                    scT = sbM("scT", [128, 8, 17], BF16)
                    NWB = 2
                    wada = [sbM("wada%d" % i, [128, 8, 1024], BF16) for i in range(NWB)]
                    bada = [sbM("bada%d" % i, [17, 1024]) for i in range(NWB)]
                    sel = sbM("sel", [17, 129])

                    k.dma(c_sb[:], c17, (), ["c_sb"])
                    k.dma(sel[:], cd["sel"], (), ["sel"])
                    wada_v = w_ada.rearrange("(kc p) n -> p kc n", p=128)

                    def load_wada(cb):
                        k.dma(wada[cb % NWB][:], wada_v[:, :, cb * 1024:(cb + 1) * 1024], (), [("wada", cb % NWB)],
                              eng="pool")
                        k.dma(bada[cb % NWB][:], b_ada[cb * 1024:(cb + 1) * 1024].partition_broadcast(17), (),
                              [("bada", cb % NWB)])

                    win_v = w_in.rearrange("(kc p) n -> p kc n", p=128)
                    wout_v = w_out.rearrange("(kc p) n -> p kc n", p=128)
                    for cb_ in range(NWB):
                        load_wada(cb_)
                    for kc in range(8):
                        k.dma(w_in_sb[:, kc, :].rearrange("p (a b) -> p a b", b=1024),
                              win_v[:, kc, :].rearrange("p (a b) -> p a b", b=1024), (), [("w_in", kc)], eng="pool")

                    k.act(sc_sb[:], c_sb[:], AF.Silu, ["c_sb"], ["sc_sb"])
                    for kc in range(8):
                        k.tr(ps[0][:, kc * 17:(kc + 1) * 17], sc_sb[0:17, kc * 128:(kc + 1) * 128], ident[0:17, 0:17],
                             ["sc_sb", "ident"], [("ps", 0)])
                    k.copy(scT[:].rearrange("p a b -> p (a b)"), ps[0][:, 0:136], [("ps", 0)], ["scT"])

                    for cb in range(6):
                        for hh in range(2):
                            pb = ps[1 + hh]
                            for kc in range(8):
                                k.mm(pb[0:17, :], scT[:, kc, :], wada[cb % NWB][:, kc, hh * 512:(hh + 1) * 512], kc == 0,
                                     kc == 7, ["scT", ("wada", cb % NWB)], [("ps", 1 + hh)])
                            k.tt(mod[:, cb * 1024 + hh * 512:cb * 1024 + (hh + 1) * 512], pb[0:17, :],
                                 bada[cb % NWB][:, hh * 512:(hh + 1) * 512], ALU.add,
                                 [("ps", 1 + hh), ("bada", cb % NWB)], [("mod", 2 * cb + hh)])
                        if cb + NWB < 6:
                            load_wada(cb + NWB)
                        if cb == 2 and KPH >= 1:
                            sample_a(nc, k, sbM, ps, ident, mod, [("mod", i_) for i_ in range(6)], cd, w_in_sb, early)
                        if cb == 6 - NWB - 1:
                            for kc in range(0, 8, 2):
                                k.dma(w_out_sb[:, kc:kc + 2, :], wout_v[:, kc:kc + 2, :], (),
                                      [("w_out", kc), ("w_out", kc + 1)], eng="pool")
                            k.dma(wr_sb[:], wr_bd, (), ["wr_sb"], eng="pool")
                            k.dma(wi_sb[:], wi_bd, (), ["wi_sb"], eng="pool")
                    modkeys = [("mod", cb) for cb in range(12)]

                    for jj, j in enumerate([0, 1, 3, 4]):
                        for c in range(8):
                            col = jj * 8 + c
                            k.mm(ps[3][:, col:col + 1], mod[0:17, j * D + c * 128: j * D + (c + 1) * 128],
                                 sel[0:17, 0:1], True, True, modkeys + ["sel"], [("ps", 3)])
                    k.copy(modfm[:].rearrange("p a b -> p (a b)"), ps[3][:, 0:32], [("ps", 3)], ["modfm"])
                    k.stt(gsm[:], modfm[:, 1, :], 1.0, V("g_mix"), ALU.add, ALU.mult, ["modfm", "vfm"], ["gsm"])
                    k.stt(gsf[:], modfm[:, 3, :], 1.0, V("g_ffn"), ALU.add, ALU.mult, ["modfm", "vfm"], ["gsf"])
                    for gi, (j, dst, key) in enumerate([(2, gate_m_row, "gate_m_row")]):
                        for hf in range(2):
                            pb = ps[4 + hf]
                            k.mm(pb[:, :], sel[0:17, 1:129], mod[0:17, j * D + hf * 512: j * D + (hf + 1) * 512],
                                 True, True, modkeys + ["sel"], [("ps", 4 + hf)])
                            k.copy(dst[:, hf * 512:(hf + 1) * 512], pb[:, :], [("ps", 4 + hf)], [(key, hf)], eng="act")
                    k.dma(modf_scr, mod[0:NS, 3 * D:6 * D], modkeys, ["modf_scr"])
                    k.dma(gatef_scr, mod[16:17, 5 * D:6 * D], modkeys, ["gatef_scr"])
                barrier(S)
                modkeys = []

                if KPH >= 1:
                  sample_mixer(nc, k, es0, ps, ident, xs, mod, modkeys, rows_d, cd, w_in_sb, w_out_sb, wr_sb, wi_sb,
                               st_ret, st_h, st_cl, o_rets, o_hs, o_cls, x1s_scr, early)

            barrier(S)
            es1 = ExitStack()
            with es1:
                if KPH >= 2:
                  prompt_mixer(nc, k, es1, ps, ident, identb, xp, cd, V, gsm, modfm, gate_m_row, cf_fm,
                               w_in_sb, w_out_sb, wr_sb, wi_sb, x1_scr, o_retp, o_hp, o_clp)
            barrier(S)

        esB = ExitStack()
        with esB:
            def sbB(name, shape, dt=F32):
                return esB.enter_context(nc.sbuf_tensor("sb_" + name, list(shape), dt))

            wuc_sb = sbB("wuc_sb", [128, 8, DFF], BF16)
            wug_sb = sbB("wug_sb", [128, 8, DFF], BF16)
            wdn_sb = sbB("wdn_sb", [128, NFF, D], BF16)
            wuc_v = w_upc.rearrange("(kc p) n -> p kc n", p=128)
            wug_v = w_upg.rearrange("(kc p) n -> p kc n", p=128)
            wdn_v = w_dn.rearrange("(m p) n -> p m n", p=128)
            def _wl_up(gi, a, b):
                def f():
                    for h0 in (0, 4):
                        k.dma(wuc_sb[:, h0:h0 + 4, a:b], wuc_v[:, h0:h0 + 4, a:b], (),
                              [("wuc", kc, gi) for kc in range(h0, h0 + 4)], eng="pool")
                        k.dma(wug_sb[:, h0:h0 + 4, a:b], wug_v[:, h0:h0 + 4, a:b], (),
                              [("wug", kc, gi) for kc in range(h0, h0 + 4)], eng="pool")
                return f

            def _wl_dn():
                for m0 in range(0, NFF, 4):
                    m1 = min(NFF, m0 + 4)
                    k.dma(wdn_sb[:, m0:m1, :], wdn_v[:, m0:m1, :], (), [("wdn", m) for m in range(m0, m1)],
                          eng="pool")

            wload = [_wl_up(0, 0, 1024), _wl_up(1, 1024, 2048), _wl_up(2, 2048, DFF), _wl_dn]

            es3 = ExitStack()
            with es3:
                if KPH >= 4:
                  prompt_ffn(nc, k, es3, ps, ident, V, gsf, modfm, gatef_scr, rows_d, x1_scr, wuc_sb, wug_sb, wdn_sb,
                             yp, o_cfp, wload)
                else:
                  for f_ in wload:
                      f_()
            barrier(S)
            es2 = ExitStack()
            with es2:
                if KPH >= 3:
                  sample_ffn(nc, k, es2, ps, ident, x1s_scr, modf_scr, rows_d, wuc_sb, wug_sb, wdn_sb, st_cf, o_cfs, ys)

        with nc.Block() as block:
            S.emit(nc, block, sems, dma_sems)
    return nc


ALLK = "__all__"


def barrier(S, keep=None):
    keep = keep or (lambda key: False)
    allk = [kk for kk in (set(S.lastw.keys()) | set(S.lastr.keys())) if not keep(kk)]
    order = ["act", "dve", "pool", "sp", "pe"]
    for e in order:
        S.add(e, _drain, (), allk + [("barrier", e)])
    for e in order:
        S.add(e, _drain, [("barrier", x) for x in order], [("barrier2", e)])
    kw = {kk: v for kk, v in S.lastw.items() if keep(kk)}
    kr = {kk: v for kk, v in S.lastr.items() if keep(kk)}
    S.lastw = {("barrier2", e): S.lastw[("barrier2", e)] for e in order}
    S.lastw.update(kw)
    S.lastr = kr


def _drain(e):
    return e.drain()


def prompt_mixer(nc, k, es, ps, ident, identb, xp, cd, V, gsm, modfm, gate_m_row, cf_fm,
                 w_in_sb, w_out_sb, wr_sb, wi_sb, x1_scr, o_retp, o_hp, o_clp):
    S = k.S

    def sb(name, shape, dt=F32):
        return es.enter_context(nc.sbuf_tensor("sb_" + name, list(shape), dt))

    cosP = sb("cosP", [128, NT, 64]); sinP = sb("sinP", [128, NT, 64])
    maskT = sb("maskT", [128, 128]); dqT = sb("dqT", [128, 4, 128]); dkT = sb("dkT", [128, 4, 128])
    dkP = sb("dkP", [128, 4])
    for t_, n in [(cosP, "cosP"), (sinP, "sinP"), (maskT, "maskT"), (dqT, "dqT"), (dkT, "dkT"), (dkP, "dkP")]:
        k.dma(t_[:], cd[n], (), [n])
    neg_half = sb("neg_half", [128, 4])
    k.memset(neg_half[:], -0.5, ["neg_half"])
    hcf = sb("hcf", [128, 4]); hb_r = sb("hb_r", [128, 4]); hb_i = sb("hb_i", [128, 4])
    k.ts(hcf[:], cf_fm[:], 0.5, None, ALU.mult, None, ["cf_fm"], ["hcf"])
    k.ts(hb_r[:], V("b_r"), 0.5, None, ALU.mult, None, ["vfm"], ["hb_r"])
    k.ts(hb_i[:], V("b_i"), 0.5, None, ALU.mult, None, ["vfm"], ["hb_i"])

    xt = [sb("xt%d" % i, [128, D]) for i in range(2)]
    xr = sb("xr0", [128, D]); x1t = sb("x1t0", [128, D])
    ssb = sb("ssb", [128, 4]); msb = sb("msb", [128, 4]); rstd = sb("rstd", [128, 4])
    xn = sb("xn0", [128, D])
    hT = [sb("hT%d" % i, [128, 8, 512], BF16) for i in range(2)]
    catT = [sb("catT%d" % i, [128, 8, 512], BF16) for i in range(2)]
    xc = sb("xc", [128, 2, 512]); rr = sb("rr", [128, 2, 512]); aa = sb("aa", [128, 2, 512])
    ig = sb("ig", [128, 2, 512]); gl = sb("gl", [128, 2, 512]); xcb = sb("xcb", [128, 2, 512], BF16)
    xcar = sb("xcar", [128, 3, 4]); hcar = sb("hcar", [128, 4])
    st12 = sb("st12", [12, 128]); st4 = sb("st4", [4, 128])
    qrot = sb("qrot", [128, 512]); krot = sb("krot", [128, 512])
    mq = [sb("mq%d" % i, [128, 256]) for i in range(4)]
    ktok = [sb("ktok%d" % i, [128, 512], BF16) for i in range(2)]
    v_bf = [sb("v_bf%d" % i, [128, 512], BF16) for i in range(2)]
    sg = [sb("sg%d" % i, [128, 512]) for i in range(2)]
    qkT = [sb("qkT%d" % i, [128, 8, 128], BF16) for i in range(2)]
    PT = sb("PT", [128, 512], BF16)
    Z = sb("Z", [128, 512]); S_bf = sb("S_bf", [128, 512], BF16)
    stats = sb("stats", [128, 4, 6]); mv = sb("mv", [128, 4, 2]); vpe = sb("vpe", [128, 4]); rs = sb("rs", [128, 4])
    sgr = sb("sgr", [128, 512]); ret = [sb("ret%d" % i, [128, 512], BF16) for i in range(2)]
    Sl = sgr

    clw = V("clw"); clb = V("clb"); b_r = V("b_r"); b_i = V("b_i")
    shift_m = modfm[:, 0, :]
    gC = CONST["gC"]
    BK_L, BK_A0, BK_A1, BK_T, BK_KV, BK_O = 2, 3, 4, 5, 6, 7

    def h3(ap):
        return ap.rearrange("p (h d) -> p h d", h=4)

    def hTk(s):
        return [("hT", s % 2, c, t) for c in range(8) for t in range(4)]

    junkb = sb("junkb", [128, D], BF16)

    def N1(T):
        t = T % 4
        a = T % 2
        k.dma(xt[a][:], xp[T * 128:(T + 1) * 128, :], (), [("xt", a)])
        k.act(junkb[:], xt[a][:], AF.Square, [("xt", a)], ["junkb", ("ssb", t)], accum_out=ssb[:, t:t + 1])
        k.ts(msb[:, t:t + 1], ssb[:, t:t + 1], 1.0 / D, EPS, ALU.mult, ALU.add, [("ssb", t)], [("msb", t)])
        k.tt(rstd[:, t:t + 1], msb[:, t:t + 1], neg_half[:, 0:1], ALU.pow, [("msb", t), "neg_half"],
             [("rstd", t)], eng="pool")

    def N2(T):
        s, t = divmod(T, 4)
        a = T % 2
        tok = slice(t * 128, (t + 1) * 128)
        hTs = hT[s % 2]
        k.ts(xn[:], xt[a][:], rstd[:, t:t + 1], 0.0, ALU.mult, ALU.add, [("xt", a), ("rstd", t)], ["xn"], eng="pool")
        for half in range(2):
            b = half
            for c4 in range(4):
                c = half * 4 + c4
                k.tr(ps[b][:, c4 * 128:(c4 + 1) * 128], xn[:, c * 128:(c + 1) * 128], ident[:],
                     ["xn", "ident"], [("ps", b)])
            for c4 in range(4):
                c = half * 4 + c4
                if half == 0 or T < 4:
                    k.ts(hTs[:, c, tok], ps[b][:, c4 * 128:(c4 + 1) * 128], gsm[:, c:c + 1], shift_m[:, c:c + 1],
                         ALU.mult, ALU.add, [("ps", b), "gsm", "modfm"], [("hT", s % 2, c, t)])
                else:
                    k.act(hTs[:, c, tok], ps[b][:, c4 * 128:(c4 + 1) * 128], AF.Identity,
                          [("ps", b), "gsm", "modfm"], [("hT", s % 2, c, t)], scale=gsm[:, c:c + 1],
                          bias=shift_m[:, c:c + 1])

    def L1(s, hf):
        hTs = hT[s % 2]
        for ci in range(2):
            c = 2 * hf + ci
            bkx = ci
            for kc in range(8):
                k.mm(ps[bkx][:, :], w_in_sb[:, kc, 2048 + c * 128: 2048 + (c + 1) * 128], hTs[:, kc, :], kc == 0,
                     kc == 7, hTk(s) + [("w_in", kc)], [("ps", bkx)])
            for kc in range(8):
                k.mm(ps[BK_L][:, :], w_in_sb[:, kc, 2560 + c * 128: 2560 + (c + 1) * 128], hTs[:, kc, :], kc == 0,
                     kc == 7, hTk(s) + [("w_in", kc)], [("ps", BK_L)])
            k.act(xc[:, ci, :], ps[bkx][:, :], AF.Identity, [("ps", bkx), "vfm"], [("xc", ci)],
                  scale=clw[:, c * 4 + 3:c * 4 + 4], bias=clb[:, c:c + 1])
            k.act(gl[:, ci, :], ps[BK_L][:, :], AF.Gelu_apprx_tanh, [("ps", BK_L)], [("gl", ci)])
            for sh in (1, 2, 3):
                kk = 3 - sh
                wcol = clw[:, c * 4 + kk:c * 4 + kk + 1]
                k.stt(xc[:, ci, sh:512], ps[bkx][:, 0:512 - sh], wcol, xc[:, ci, sh:512], ALU.mult, ALU.add,
                      [("ps", bkx), ("xc", ci), "vfm"], [("xc", ci)])
                if s > 0:
                    k.stt(xc[:, ci, 0:sh], xcar[:, 3 - sh:3, c], wcol, xc[:, ci, 0:sh], ALU.mult, ALU.add,
                          [("xcar", c), ("xc", ci), "vfm"], [("xc", ci)])
            k.copy(xcar[:, :, c], ps[bkx][:, 509:512], [("ps", bkx)], [("xcar", c)])

    def L1c(s, hf):
        for ci in range(2):
            k.copy(xcb[:, ci, :], xc[:, ci, :], [("xc", ci)], [("xcb", ci)], eng="pool")

    def L2(s, hf):
        cts = catT[s % 2]
        for ci in range(2):
            c = 2 * hf + ci
            k.mm(ps[BK_L][:, :], wr_sb[:, c, :], xcb[:, ci, :], True, True, [("xcb", ci), "wr_sb"], [("ps", BK_L)])
            k.act(rr[:, ci, :], ps[BK_L][:, :], AF.Tanh, [("ps", BK_L), "hb_r"], [("rr", ci)], bias=hb_r[:, c:c + 1],
                  scale=0.5)
            k.mm(ps[ci][:, :], wi_sb[:, c, :], xcb[:, ci, :], True, True, [("xcb", ci), "wi_sb"], [("ps", ci)])
            k.act(ig[:, ci, :], ps[ci][:, :], AF.Tanh, [("ps", ci), "hb_i"], [("ig", ci)], bias=hb_i[:, c:c + 1],
                  scale=0.5)
        for ci in range(2):
            c = 2 * hf + ci
            k.act(aa[:, ci, :], rr[:, ci, :], AF.Exp, [("rr", ci), "hcf"], [("aa", ci)], scale=hcf[:, c:c + 1],
                  bias=hcf[:, c:c + 1])
            k.act(rr[:, ci, :], rr[:, ci, :], AF.Exp, [("rr", ci), "cf_fm"], [("rr", ci)], scale=cf_fm[:, c:c + 1],
                  bias=cf_fm[:, c:c + 1])
        for ci in range(2):
            k.act(rr[:, ci, :], rr[:, ci, :], AF.Sqrt, [("rr", ci)], [("rr", ci)], scale=-1.0, bias=1.0)

    def L2b(s, hf):
        cts = catT[s % 2]
        for ci in range(2):
            c = 2 * hf + ci
            if s == 0:
                k.memset(rr[:, ci, 0:1], 1.0, [("rr", ci)])
            k.stt(ig[:, ci, :], ig[:, ci, :], 1.0, xc[:, ci, :], ALU.add, ALU.mult, [("ig", ci), ("xc", ci)], [("ig", ci)])
            k.stt(ig[:, ci, :], ig[:, ci, :], 0.5, rr[:, ci, :], ALU.mult, ALU.mult, [("ig", ci), ("rr", ci)], [("ig", ci)])
            init = 0.0 if s == 0 else hcar[:, c:c + 1]
            S.add("dve", (lambda ci=ci, init=init: (lambda e: e.tensor_tensor_scan(
                xc[:, ci, :], aa[:, ci, :], ig[:, ci, :], init, ALU.mult, ALU.add)))(),
                [("aa", ci), ("ig", ci), ("hcar", c), ("xc", ci)], [("xc", ci)])
            k.copy(hcar[:, c:c + 1], xc[:, ci, 511:512], [("xc", ci)], [("hcar", c)])
            k.tt(cts[:, 4 + c, :], xc[:, ci, :], gl[:, ci, :], ALU.mult, [("xc", ci), ("gl", ci)],
                 [("catT", s % 2, 4 + c)])

    def A(T, js=(0, 1, 2, 3)):
        s, t = divmod(T, 4)
        a = T % 2
        tok = slice(t * 128, (t + 1) * 128)
        hTs = hT[s % 2]
        cosb = cosP[:, T:T + 1, :].broadcast_to([128, 4, 64])
        sinb = sinP[:, T:T + 1, :].broadcast_to([128, 4, 64])
        for j in js:
            bk = BK_A0 + j % 2
            for kc in range(8):
                k.mm(ps[bk][:, :], hTs[:, kc, tok], w_in_sb[:, kc, j * 512:(j + 1) * 512], kc == 0, kc == 7,
                     [("hT", s % 2, c_, t) for c_ in range(8)] + [("w_in", kc)], [("ps", bk)])
            p3 = h3(ps[bk][:, :])
            if j < 2:
                dst = qrot if j == 0 else krot
                nm = "q" if j == 0 else "k"
                m3 = [x[:, :].rearrange("p (h d) -> p h d", h=4) for x in mq]
                k.tt(m3[0], p3[:, :, 0:64], cosb, ALU.mult, [("ps", bk), "cosP"], [("qkm", 0)])
                k.tt(m3[1], p3[:, :, 64:128], sinb, ALU.mult, [("ps", bk), "sinP"], [("qkm", 1)])
                k.tt(m3[2], p3[:, :, 0:64], sinb, ALU.mult, [("ps", bk), "sinP"], [("qkm", 2)])
                k.tt(m3[3], p3[:, :, 64:128], cosb, ALU.mult, [("ps", bk), "cosP"], [("qkm", 3)])
                d3 = h3(dst[:, :])
                k.tt(d3[:, :, 0:64], m3[0], m3[1], ALU.subtract, [("qkm", 0), ("qkm", 1)], [(nm + "rot", 0)])
                k.tt(d3[:, :, 64:128], m3[2], m3[3], ALU.add, [("qkm", 2), ("qkm", 3)], [(nm + "rot", 1)])
                if j == 1:
                    for h in range(4):
                        k.ts(ktok[a][:, h * 128:(h + 1) * 128], krot[:, h * 128:(h + 1) * 128], dkP[:, h:h + 1], 0.0,
                             ALU.mult, ALU.add, [("krot", 0), ("krot", 1), "dkP"], [("ktok", a, h)], eng="pool")
            elif j == 2:
                k.copy(v_bf[a][:, :], ps[bk][:, :], [("ps", bk)], [("v_bf", a)], eng="act")

    def Ag_ev(T):
        a = T % 2
        bk = BK_A0 + 1
        k.act(sg[a][:, :], ps[bk][:, :], AF.Tanh, [("ps", bk)], [("sg", a)], scale=0.5)
        k.stt(sg[a][:, :], sg[a][:, :], 1.0, ps[bk][:, :], ALU.add, ALU.mult, [("ps", bk), ("sg", a)], [("sg", a)])

    def A2(T):
        for h in range(4):
            k.tr(ps[BK_T][:, h * 128:(h + 1) * 128], qrot[:, h * 128:(h + 1) * 128], ident[:],
                 [("qrot", 0), ("qrot", 1), "ident"], [("ps", BK_T)])
        for h in range(4):
            k.tr(ps[BK_KV][:, h * 128:(h + 1) * 128], krot[:, h * 128:(h + 1) * 128], ident[:],
                 [("krot", 0), ("krot", 1), "ident"], [("ps", BK_KV)])

    def A2ev(T):
        a = T % 2
        k.tt(qkT[a][:, 0:4, :], h3(ps[BK_T][:, :]), dqT[:], ALU.mult, [("ps", BK_T), "dqT"], [("qT", a)])
        k.tt(qkT[a][:, 4:8, :], h3(ps[BK_KV][:, :]), dkT[:], ALU.mult, [("ps", BK_KV), "dkT"], [("kT", a)])

    def B1(T):
        a = T % 2
        qk = qkT[a]
        for h in range(4):
            k.mm(ps[BK_T][:, h * 128:(h + 1) * 128], qk[:, 4 + h, :], qk[:, h, :], True, True,
                 [("qT", a), ("kT", a)], [("ps", BK_T)])
        k.tt(h3(PT[:, :]), h3(ps[BK_T][:, :]), maskT[:, :].unsqueeze(1).broadcast_to([128, 4, 128]), ALU.mult,
             [("ps", BK_T), "maskT"], ["PT"])
        for h in range(4):
            hs = slice(h * 128, (h + 1) * 128)
            k.mm(ps[BK_KV][:, hs], ktok[a][:, hs], v_bf[a][:, hs], True, True, [("ktok", a, h), ("v_bf", a)],
                 [("ps", BK_KV)])
        for h in range(4):
            hs = slice(h * 128, (h + 1) * 128)
            k.mm(ps[BK_O][:, hs], PT[:, hs], v_bf[a][:, hs], True, T == 0, ["PT", ("v_bf", a)], [("ps", BK_O)])
            if T > 0:
                k.mm(ps[BK_O][:, hs], qk[:, h, :], S_bf[:, hs], False, True, [("qT", a), ("S_bf", h)], [("ps", BK_O)])
        for h in range(4):
            hs = slice(h * 128, (h + 1) * 128)
            if T == 0:
                k.copy(Z[:, hs], ps[BK_KV][:, hs], [("ps", BK_KV)], [("Z", h)])
            else:
                k.stt(Z[:, hs], Z[:, hs], gC[h], ps[BK_KV][:, hs], ALU.mult, ALU.add, [("ps", BK_KV), ("Z", h)],
                      [("Z", h)])
            if T < NT - 1:
                k.ts(S_bf[:, hs], Z[:, hs], gC[h], None, ALU.mult, None, [("Z", h)], [("S_bf", h)])

    def B1b(T):
        a = T % 2
        for h in range(4):
            hs = slice(h * 128, (h + 1) * 128)
            S.add("dve", (lambda h=h, hs=hs: (lambda e: e.bn_stats(stats[:, h, :], ps[BK_O][:, hs])))(),
                  [("ps", BK_O)], [("stats", h)])
            S.add("dve", (lambda h=h: (lambda e: e.bn_aggr(mv[:, h, :], stats[:, h, :])))(),
                  [("stats", h)], [("mv", h)])
        mvk = [("mv", h) for h in range(4)]
        k.ts(vpe[:, :], mv[:, :, 1], EPS, 4.0, ALU.add, ALU.mult, mvk, ["vpe"])
        k.tt(rs[:, :], vpe[:, :], neg_half[:, :], ALU.pow, ["vpe", "neg_half"], ["rs"], eng="pool")

    def B1c(T):
        a = T % 2
        for h in range(4):
            hs = slice(h * 128, (h + 1) * 128)
            k.act(sgr[:, hs], sg[a][:, hs], AF.Copy, [("sg", a), "rs"], [("sgr", h)], scale=rs[:, h:h + 1])
            k.stt(ret[a][:, hs], ps[BK_O][:, hs], mv[:, h, 0:1], sgr[:, hs], ALU.subtract, ALU.mult,
                  [("ps", BK_O), ("mv", h), ("sgr", h)], [("ret", a, h)])
        if T == NT - 1:
            for h in range(4):
                hs = slice(h * 128, (h + 1) * 128)
                k.act(Sl[:, hs], Z[:, hs], AF.Copy, [("Z", h)], [("sgr", h)], scale=gC[h])
            k.dma(o_retp.rearrange("h k v -> k h v"), h3(Sl[:, :]), [("sgr", h_) for h_ in range(4)], ["o_retp"])

    def B2(T):
        s, t = divmod(T, 4)
        a = T % 2
        tok = slice(t * 128, (t + 1) * 128)
        pbf = ps[BK_KV][:, :].bitcast(BF16)
        for h in range(4):
            k.tr(pbf[:, h * 128:(h + 1) * 128], ret[a][:, h * 128:(h + 1) * 128], identb[:],
                 [("ret", a, h), "identb"], [("ps", BK_KV)])
        k.copy(catT[s % 2][:, 0:4, tok], pbf[:, 0:512].rearrange("p (h d) -> p h d", h=4), [("ps", BK_KV)],
               [("catTr", s % 2, t)])

    def O(T):
        s, t = divmod(T, 4)
        tok = slice(t * 128, (t + 1) * 128)
        cts = catT[s % 2]
        catk = [("catT", s % 2, 4 + c) for c in range(4)]
        k.dma(xr[:], xp[T * 128:(T + 1) * 128, :], (), ["xr"])
        for cb in range(2):
            for kc in range(8):
                k.mm(ps[cb][:, :], cts[:, kc, tok], w_out_sb[:, kc, cb * 512:(cb + 1) * 512], kc == 0, kc == 7,
                     catk + [("catTr", s % 2, t), ("w_out", kc)], [("ps", cb)])
            k.tt(x1t[:, cb * 512:(cb + 1) * 512], ps[cb][:, :], gate_m_row[:, cb * 512:(cb + 1) * 512],
                 ALU.mult, [("ps", cb), ("gate_m_row", 0), ("gate_m_row", 1)], [("x1t", cb)])

    def O2(T):
        k.tt(x1t[:, :], x1t[:, :], xr[:, :], ALU.add, [("x1t", 0), ("x1t", 1), "xr"], [("x1t", 0), ("x1t", 1)],
             eng="pool")
        k.dma(x1_scr[T * 128:(T + 1) * 128, :], x1t[:, :], [("x1t", 0), ("x1t", 1)], [("x1_scr", T)])

    def ok(T):
        return 0 <= T < NT

    def Lpiece(kind, idx):
        if not (0 <= idx < 2 * NST):
            return
        s, hf = divmod(idx, 2)
        {"L1": L1, "L2a": L2, "L2b": L2b, "L1c": L1c}[kind](s, hf)

    N1(0)
    N1(1)
    N2(0)
    N1(2)
    N2(1)
    N1(3)
    N2(2)
    N2(3)
    N1(4)
    for tau in range(NT + 6):
        if ok(tau):
            A(tau, (0, 1))
        if ok(tau - 1):
            A2ev(tau - 1)
        if ok(tau - 2):
            B1c(tau - 2)
        if ok(tau):
            A(tau, (2, 3))
        if ok(tau - 1):
            B1(tau - 1)
        if ok(tau):
            Ag_ev(tau)
        if ok(tau + 4) and tau + 4 >= 4:
            N2(tau + 4)
        if tau % 2 == 0:
            Lpiece("L2b", tau // 2 - 1)
            Lpiece("L1", tau // 2)
        else:
            Lpiece("L2a", tau // 2)
        if tau == 4 * NST:
            k.tr(ps[0][0:12, 0:128], xcar[:].rearrange("p k c -> p (k c)"), ident[:],
                 [("xcar", c) for c in range(4)] + ["ident"], [("ps", 0)])
            k.copy(st12[:, :], ps[0][0:12, 0:128], [("ps", 0)], ["st12"])
            k.dma(o_clp.rearrange("k (c p) -> (k c) p", p=128), st12[:, :], ["st12"], ["o_clp"])
            k.tr(ps[1][0:4, 0:128], hcar[:, :], ident[:], [("hcar", c) for c in range(4)] + ["ident"], [("ps", 1)])
            k.copy(st4[:, :], ps[1][0:4, 0:128], [("ps", 1)], ["st4"])
            k.dma(o_hp, st4[:, :], ["st4"], ["o_hp"])
        if ok(tau - 2):
            B2(tau - 2)
        if ok(tau - 5):
            O(tau - 5)
        if ok(tau - 1):
            B1b(tau - 1)
        if ok(tau + 5):
            N1(tau + 5)
        if ok(tau - 5):
            O2(tau - 5)
        if ok(tau):
            A2(tau)
        if tau % 2 == 0:
            Lpiece("L1c", tau // 2)


def prompt_ffn(nc, k, es, ps, ident, V, gsf, modfm, gatef_scr, rows_d, x1_scr, wuc_sb, wug_sb, wdn_sb, yp, o_cfp,
               wload):
    S = k.S

    def sb(name, shape, dt=F32):
        return es.enter_context(nc.sbuf_tensor("sb_" + name, list(shape), dt))

    gfin = sb("gfin", [128, D])
    k.dma(gfin[:], rows_d["g_final"].partition_broadcast(128), (), ["gfin"])
    gate_f_row = sb("gate_f_row", [128, D])
    k.dma(gate_f_row[:], gatef_scr[0].partition_broadcast(128), ["gatef_scr"], [("gate_f_row", 0), ("gate_f_row", 1)])
    neg_half = sb("neg_half2", [128, 1])
    k.memset(neg_half[:], -0.5, ["neg_half2"])
    xa = [sb("xa%d" % i, [128, D]) for i in range(2)]
    xb = [sb("xb0", [128, D])] * 2
    xn2 = [sb("xn2_0", [128, D])] * 2
    yt = sb("yt", [128, D]); yo = sb("yo", [128, D])
    ss = sb("ss2", [128, 4]); ms = sb("ms2", [128, 4]); rstd = sb("rstd2", [128, 4])
    ss3 = sb("ss3", [128, 2]); ms3 = sb("ms3", [128, 2]); rstd3 = sb("rstd3", [128, 2])
    h2T = sb("h2T", [128, 8, 512], BF16)
    aT = sb("aT", [128, NFF, 512], BF16)
    acc = [sb("acc%d" % i, [128, 512]) for i in range(2)]
    ucar = sb("ucar", [128, 2, NFF])
    st44 = sb("st44", [44, 128])
    junkb = sb("junkb2", [128, D], BF16)
    cfw = V("cfw"); cfb = V("cfb")
    shift_f = modfm[:, 2, :]

    def N2load(s, t):
        T = 4 * s + t
        a = T % 2
        k.dma(xa[a][:], x1_scr[T * 128:(T + 1) * 128, :], [("x1_scr", T)], [("xa", a)])

    def N2pre(s, t, load=True, part="all"):
        T = 4 * s + t
        a = T % 2
        if load:
            N2load(s, t)
        if part == "copy":
            k.act(xn2[a][:], xa[a][:], AF.Copy, [("xa", a), ("rstd2", t)], [("xn2", 0)], scale=rstd[:, t:t + 1])
            return
        k.act(junkb[:], xa[a][:], AF.Square, [("xa", a)], ["junkb2", ("ss2", t)], accum_out=ss[:, t:t + 1])
        k.ts(ms[:, t:t + 1], ss[:, t:t + 1], 1.0 / D, EPS, ALU.mult, ALU.add, [("ss2", t)], [("ms2", t)])
        k.tt(rstd[:, t:t + 1], ms[:, t:t + 1], neg_half[:, 0:1], ALU.pow, [("ms2", t), "neg_half2"],
             [("rstd2", t)], eng="pool")
        if part == "stats":
            return
        k.act(xn2[a][:], xa[a][:], AF.Copy, [("xa", a), ("rstd2", t)], [("xn2", 0)], scale=rstd[:, t:t + 1])

    def N2tr(s, t):
        T = 4 * s + t
        a = T % 2
        tok = slice(t * 128, (t + 1) * 128)
        for half in range(2):
            b = half
            for c4 in range(4):
                c = half * 4 + c4
                k.tr(ps[b][:, c4 * 128:(c4 + 1) * 128], xn2[a][:, c * 128:(c + 1) * 128], ident[:],
                     [("xn2", 0), "ident"], [("ps", b)])
            for c4 in range(4):
                c = half * 4 + c4
                if half == 0 or s == 0:
                    k.ts(h2T[:, c, tok], ps[b][:, c4 * 128:(c4 + 1) * 128], gsf[:, c:c + 1], shift_f[:, c:c + 1],
                         ALU.mult, ALU.add, [("ps", b), "gsf", "modfm"], [("h2T", c, t)])
                else:
                    k.act(h2T[:, c, tok], ps[b][:, c4 * 128:(c4 + 1) * 128], AF.Identity,
                          [("ps", b), "gsf", "modfm"], [("h2T", c, t)], scale=gsf[:, c:c + 1],
                          bias=shift_f[:, c:c + 1])

    h2k = [("h2T", c, t) for c in range(8) for t in range(4)]
    aTk = [("aT", m) for m in range(NFF)]

    def UP(s):
        for m in range(NFF):
            bu = 2 + 2 * (m % 2)
            bg = bu + 1
            ms_ = slice(m * 128, (m + 1) * 128)
            for kc in range(8):
                k.mm(ps[bu][:, :], wuc_sb[:, kc, ms_], h2T[:, kc, :], kc == 0, kc == 7, h2k + [("wuc", kc, m // 8)], [("ps", bu)])
            for kc in range(8):
                k.mm(ps[bg][:, :], wug_sb[:, kc, ms_], h2T[:, kc, :], kc == 0, kc == 7, h2k + [("wug", kc, m // 8)], [("ps", bg)])
            ac = acc[m % 2]
            ak = ("acc", m % 2)
            k.act(ac[:, :], ps[bu][:, :], AF.Identity, [("ps", bu), "vfm"], [ak], scale=cfw[:, m * 3 + 2:m * 3 + 3],
                  bias=cfb[:, m:m + 1])
            for sh in (1, 2):
                kk = 2 - sh
                wcol = cfw[:, m * 3 + kk:m * 3 + kk + 1]
                k.stt(ac[:, sh:512], ps[bu][:, 0:512 - sh], wcol, ac[:, sh:512], ALU.mult, ALU.add,
                      [("ps", bu), ak, "vfm"], [ak])
                if s > 0:
                    k.stt(ac[:, 0:sh], ucar[:, 2 - sh:2, m], wcol, ac[:, 0:sh], ALU.mult, ALU.add,
                          [("ucar", m), ak, "vfm"], [ak])
            k.copy(ucar[:, :, m], ps[bu][:, 510:512], [("ps", bu)], [("ucar", m)])
            k.act(ac[:, :], ac[:, :], AF.Gelu_apprx_tanh, [ak], [ak])
            k.tt(aT[:, m, :], ac[:, :], ps[bg][:, :], ALU.mult, [ak, ("ps", bg)], [("aT", m)])

    def DOWN(s, t):
        if True:
            T = 4 * s + t
            a = T % 2
            tok = slice(t * 128, (t + 1) * 128)
            k.dma(xb[a][:], x1_scr[T * 128:(T + 1) * 128, :], [("x1_scr", T)], [("xb", 0)])
            for cb in range(2):
                for m in range(NFF):
                    k.mm(ps[6 + cb][:, :], aT[:, m, tok], wdn_sb[:, m, cb * 512:(cb + 1) * 512], m == 0, m == NFF - 1,
                         aTk + [("wdn", m)], [("ps", 6 + cb)])
                k.tt(yt[:, cb * 512:(cb + 1) * 512], ps[6 + cb][:, :], gate_f_row[:, cb * 512:(cb + 1) * 512], ALU.mult,
                     [("ps", 6 + cb), ("gate_f_row", 0), ("gate_f_row", 1)], [("yt", cb)])

    def DOWN2(s, t):
        if True:
            T = 4 * s + t
            a = T % 2
            k.tt(yt[:, :], yt[:, :], xb[a][:, :], ALU.add, [("yt", 0), ("yt", 1), ("xb", 0)], [("yt", 0), ("yt", 1)],
                 eng="pool")
            k.act(yo[:, :], yt[:, :], AF.Square, [("yt", 0), ("yt", 1)], ["yo", ("ss3", a)], accum_out=ss3[:, a:a + 1])
            k.ts(ms3[:, a:a + 1], ss3[:, a:a + 1], 1.0 / D, EPS, ALU.mult, ALU.add, [("ss3", a)], [("ms3", a)])
            k.tt(rstd3[:, a:a + 1], ms3[:, a:a + 1], neg_half[:, 0:1], ALU.pow, [("ms3", a), "neg_half2"],
                 [("rstd3", a)], eng="pool")
            k.stt(yo[:, :], yt[:, :], rstd3[:, a:a + 1], gfin[:, :], ALU.mult, ALU.mult,
                  [("yt", 0), ("yt", 1), ("rstd3", a), "gfin"], ["yo"])
            k.dma(yp[T * 128:(T + 1) * 128, :], yo[:, :], ["yo"], [("yp", T)])

    N2load(0, 0)
    N2load(0, 1)
    wload[0]()
    N2pre(0, 0, load=False, part="stats")
    N2pre(0, 1, load=False, part="stats")
    for t in range(4):
        N2pre(0, t, load=False, part="copy")
        N2tr(0, t)
        if t + 2 < 4:
            N2load(0, t + 2)
            N2pre(0, t + 2, load=False, part="stats")
    wload[1]()
    wload[2]()
    wload[3]()
    for s in range(NST):
        UP(s)
        if s == NST - 1:
            k.tr(ps[0][0:44, 0:128], ucar[:].rearrange("p k m -> p (k m)"), ident[:],
                 [("ucar", m) for m in range(NFF)] + ["ident"], [("ps", 0)])
            k.copy(st44[:, :], ps[0][0:44, 0:128], [("ps", 0)], ["st44"])
            k.dma(o_cfp.rearrange("k (m p) -> (k m) p", p=128), st44[:, :], ["st44"], ["o_cfp"])
        for t in range(4):
            if s + 1 < NST:
                N2pre(s + 1, t)
            DOWN(s, t)
            if s + 1 < NST:
                N2tr(s + 1, t)
            DOWN2(s, t)


def _rms_rstd(k, sbf, pfx, x, junk):
    ss = sbf(pfx + "_ss", [NS, 1]); ms = sbf(pfx + "_ms", [NS, 1]); rstd = sbf(pfx + "_rstd", [NS, 1])
    nh = sbf(pfx + "_nh", [NS, 1])
    k.memset(nh[:], -0.5, [pfx + "nh"])
    k.act(junk, x, AF.Square, [pfx + "x"], [pfx + "junk", pfx + "ss"], accum_out=ss[:, 0:1])
    k.ts(ms[:], ss[:], 1.0 / D, EPS, ALU.mult, ALU.add, [pfx + "ss"], [pfx + "ms"])
    k.tt(rstd[:], ms[:], nh[:], ALU.pow, [pfx + "ms", pfx + "nh"], [pfx + "rstd"], eng="pool")
    return rstd


def _to_fm(k, ps_bank, bank_id, src, nchunk, dst, ident, rkeys, wkey):
    for c in range(nchunk):
        k.tr(ps_bank[:, c * NS:(c + 1) * NS], src[0:NS, c * 128:(c + 1) * 128], ident[0:NS, 0:NS],
             rkeys + ["ident"], [("ps", bank_id)])
    k.copy(dst[:].rearrange("p a b -> p (a b)"), ps_bank[:, 0:nchunk * NS], [("ps", bank_id)], [wkey])


def sample_a(nc, k, sbf, ps, ident, mod, modkeys, cd, w_in_sb, early):
    x = early["x"]; grow = early["cat"]; proj = early["proj"]; qr = early["qr"]; kr = early["kr"]
    xn = sbf("sm_xn", [NS, D]); gs = sbf("sm_gs", [NS, D])
    rstd = _rms_rstd(k, sbf, "sm", x[:], gs[:])
    k.act(xn[:], x[:], AF.Copy, ["smx", "smrstd"], ["sm_xn"], scale=rstd[:, 0:1])
    k.stt(gs[:], mod[0:NS, D:2 * D], 1.0, grow[:], ALU.add, ALU.mult, modkeys + ["sm_grow", "smjunk"], ["sm_gs", "smjunk"])
    k.tt(xn[:], xn[:], gs[:], ALU.mult, ["sm_xn", "sm_gs"], ["sm_xn"])
    k.tt(xn[:], xn[:], mod[0:NS, 0:D], ALU.add, ["sm_xn"] + modkeys, ["sm_xn"])
    hT = sbf("sm_hT", [128, 8, NS], BF16)
    _to_fm(k, ps[0], 0, xn, 8, hT, ident, ["sm_xn"], "sm_hT")
    for cb in range(6):
        b = 1 + cb % 2
        for kc in range(8):
            k.mm(ps[b][0:NS, :], hT[:, kc, :], w_in_sb[:, kc, cb * 512:(cb + 1) * 512], kc == 0, kc == 7,
                 ["sm_hT", ("w_in", kc)], [("ps", b)])
        k.copy(proj[:, cb * 512:(cb + 1) * 512], ps[b][0:NS, :], [("ps", b)], [("sm_proj", cb)], eng="act")

    def p3(cb):
        return proj[:, cb * 512:(cb + 1) * 512].rearrange("p (h d) -> p h d", h=4)

    rot = sbf("sm_rot", [NS, 4, 64])
    k.dma(rot[:], cd["rotS"].partition_broadcast(NS), (), ["sm_rot"])
    mt = [sbf("sm_m%d" % i, [NS, 4, 64]) for i in range(4)]
    for j, dst in enumerate([qr, kr]):
        src3 = p3(j)
        cosb = rot[:, 2 * j:2 * j + 1, :].broadcast_to([NS, 4, 64])
        sinb = rot[:, 2 * j + 1:2 * j + 2, :].broadcast_to([NS, 4, 64])
        d3 = dst[:, :].rearrange("p (h d) -> p h d", h=4)
        k.tt(mt[0][:], src3[:, :, 0:64], cosb, ALU.mult, [("sm_proj", j), "sm_rot"], [("sm_m", 0)])
        k.tt(mt[1][:], src3[:, :, 64:128], sinb, ALU.mult, [("sm_proj", j), "sm_rot"], [("sm_m", 1)])
        k.tt(mt[2][:], src3[:, :, 0:64], sinb, ALU.mult, [("sm_proj", j), "sm_rot"], [("sm_m", 2)])
        k.tt(mt[3][:], src3[:, :, 64:128], cosb, ALU.mult, [("sm_proj", j), "sm_rot"], [("sm_m", 3)])
        k.tt(d3[:, :, 0:64], mt[0][:], mt[1][:], ALU.subtract, [("sm_m", 0), ("sm_m", 1)], [("sm_rot_o", j, 0)])
        k.tt(d3[:, :, 64:128], mt[2][:], mt[3][:], ALU.add, [("sm_m", 2), ("sm_m", 3)], [("sm_rot_o", j, 1)])


def sample_mixer(nc, k, es, ps, ident, xs, mod, modkeys, rows_d, cd, w_in_sb, w_out_sb, wr_sb, wi_sb,
                 st_ret, st_h, st_cl, o_rets, o_hs, o_cls, x1s_scr, early):
    S = k.S

    def sb(name, shape, dt=F32):
        return es.enter_context(nc.sbuf_tensor("sb_" + name, list(shape), dt))

    g1 = CONST["g1"]
    x = early["x"]; cat = early["cat"]; proj = early["proj"]; qr = early["qr"]; kr = early["kr"]
    S_s = sb("sm_S", [128, NS, 4, 128])
    sv = st_ret.rearrange("b h k v -> k b h v")
    for h_ in range(4):
        k.dma(S_s[:, :, h_, :], sv[:, :, h_, :], (), [("sm_S_in", h_)])
    es = ExitStack()
    es.__enter__()

    def p3(cb):
        return proj[:, cb * 512:(cb + 1) * 512].rearrange("p (h d) -> p h d", h=4)

    qrk = []
    krk = []
    stc = early["stc"]; clb = early["clb"]; brr = early["brr"]; bir = early["bir"]; lamr = early["lamr"]
    clw = sb("sm_clw", [NS, 4, 512]); h0 = sb("sm_h0", [NS, 512])
    k.dma(clw[:], rows_d["clw"].partition_broadcast(NS), (), ["sm_clw"])
    k.dma(h0[:], st_h, (), ["sm_h0"])
    xl = proj[:, 2048:2560]
    k.dma(o_cls[:, 0:2, :], stc[:, 1:3, :], ["sm_stc"], ["o_cls01"])
    k.dma(o_cls[:, 2, :], xl, [("sm_proj", 4)], ["o_cls2"])
    xcs = sb("sm_xcs", [NS, 512]); t2 = sb("sm_t2", [NS, 512])
    k.tt(xcs[:], xl, clw[:, 3, :], ALU.mult, [("sm_proj", 4), "sm_clw"], ["sm_xcs"])
    k.tt(xcs[:], xcs[:], clb[:], ALU.add, ["sm_xcs", "sm_clb"], ["sm_xcs"])
    for kk in range(3):
        k.tt(t2[:], stc[:, kk, :], clw[:, kk, :], ALU.mult, ["sm_stc", "sm_clw"], ["sm_t2"])
        k.tt(xcs[:], xcs[:], t2[:], ALU.add, ["sm_xcs", "sm_t2"], ["sm_xcs"])
    xcT = sb("sm_xcT", [128, 4, NS], BF16)
    _to_fm(k, ps[3], 3, xcs, 4, xcT, ident, ["sm_xcs"], "sm_xcT")
    for c in range(4):
        cs_ = slice(c * 128, (c + 1) * 128)
        k.mm(ps[1][0:NS, cs_], xcT[:, c, :], wr_sb[:, c, :], True, True, ["sm_xcT", "wr_sb"], [("ps", 1)])
        k.mm(ps[2][0:NS, cs_], xcT[:, c, :], wi_sb[:, c, :], True, True, ["sm_xcT", "wi_sb"], [("ps", 2)])
    rg = sb("sm_rg", [NS, 512]); igs = sb("sm_ig", [NS, 512]); cfr = lamr; av = sb("sm_a", [NS, 512])
    k.tt(rg[:], ps[1][0:NS, :], brr[:], ALU.add, [("ps", 1), "sm_brr"], ["sm_rg"])
    k.tt(igs[:], ps[2][0:NS, :], bir[:], ALU.add, [("ps", 2), "sm_bir"], ["sm_ig"])
    k.act(rg[:], rg[:], AF.Sigmoid, ["sm_rg"], ["sm_rg"])
    k.act(igs[:], igs[:], AF.Sigmoid, ["sm_ig"], ["sm_ig"])
    k.stt(rg[:], cfr[:], -8.0, rg[:], ALU.mult, ALU.mult, ["sm_rg"], ["sm_rg"])
    k.act(av[:], rg[:], AF.Exp, ["sm_rg"], ["sm_a"])
    k.act(rg[:], rg[:], AF.Exp, ["sm_rg"], ["sm_rg"], scale=2.0)
    k.act(rg[:], rg[:], AF.Sqrt, ["sm_rg"], ["sm_rg"], scale=-1.0, bias=1.0)
    k.tt(igs[:], igs[:], xcs[:], ALU.mult, ["sm_ig", "sm_xcs"], ["sm_ig"])
    k.tt(igs[:], igs[:], rg[:], ALU.mult, ["sm_ig", "sm_rg"], ["sm_ig"])
    k.tt(av[:], av[:], h0[:], ALU.mult, ["sm_a", "sm_h0"], ["sm_a"])
    k.tt(av[:], av[:], igs[:], ALU.add, ["sm_a", "sm_ig"], ["sm_a"])
    k.dma(o_hs, av[:], ["sm_a"], ["o_hs"])
    k.act(t2[:], proj[:, 2560:3072], AF.Gelu_apprx_tanh, [("sm_proj", 5), "sm_t2"], ["sm_t2"])
    k.tt(cat[:, 512:1024], av[:], t2[:], ALU.mult, ["sm_a", "sm_t2"], [("sm_cat", 1)])
    barrier(S, keep=lambda kk: isinstance(kk, tuple) and kk[0] in ("o_rets", "sm_S_out", "sm_S_in"))
    es.__exit__(None, None, None)
    es = ExitStack()
    es.__enter__()
    tmp = sb("sm_tmp", [NS, 512]); qk = sb("sm_qk", [NS, 4]); o1 = sb("sm_o1", [NS, 512]); osb = sb("sm_o", [NS, 512])
    k.tt(tmp[:], qr[:], kr[:], ALU.mult, qrk + krk, ["sm_tmp"])
    S.add("dve", lambda e: e.tensor_reduce(qk[:], tmp[:, :].rearrange("p (h d) -> p h d", h=4), AX.X, ALU.add),
          ["sm_tmp"], ["sm_qk"])
    k.tt(o1[:, :].rearrange("p (h d) -> p h d", h=4), p3(2), qk[:, :].unsqueeze(2).broadcast_to([NS, 4, 128]),
         ALU.mult, [("sm_proj", 2), "sm_qk"], ["sm_o1"])
    Sk = [("sm_S_in", g) for g in range(4)]
    i16 = sb("sm_i16", [128, NS, NS])
    k.dma(i16[:], cd["i16"], (), ["sm_i16"])
    qT = sb("sm_qT", [128, 4, NS])
    _to_fm(k, ps[3], 3, qr, 4, qT, ident, qrk, "sm_qT")
    QM = sb("sm_QM", [128, 4, NS, NS])
    for h in range(4):
        k.tt(QM[:, h, :, :], qT[:, h, :].unsqueeze(2).broadcast_to([128, NS, NS]), i16[:], ALU.mult,
             ["sm_qT", "sm_i16"], [("sm_QM", h)])
    Vm = [sb("sm_Vm%d" % i, [NS, NS, 128], BF16) for i in range(4)]
    kr_bf = sb("sm_kr_bf", [NS, 512], BF16)
    k.copy(kr_bf[:], kr[:], krk, ["sm_kr_bf"], eng="act")
    for h in range(4):
        k.tt(Vm[h][:], p3(2)[:, h, :].unsqueeze(1).broadcast_to([NS, NS, 128]),
             ident[0:NS, 0:NS].unsqueeze(2).broadcast_to([NS, NS, 128]), ALU.mult,
             [("sm_proj", 2), "ident"], [("sm_Vm", h)])
    for h in range(4):
        for b in range(NS):
            k.mm(ps[4][0:NS, h * 128:(h + 1) * 128], QM[:, h, b, :], S_s[:, b, h, :], b == 0, b == NS - 1,
                 [("sm_QM", h), ("sm_S_in", h)], [("ps", 4)])
    for h in range(4):
        hs = slice(h * 128, (h + 1) * 128)
        k.stt(osb[:, hs], ps[4][0:NS, hs], g1[h], o1[:, hs], ALU.mult, ALU.add, [("ps", 4), "sm_o1"], [("sm_o", h)])
    v3 = p3(2)
    for h in range(4):
        vm = Vm[h]
        for g in range(4):
            bk = 4 + g if h % 2 == 0 else g
            k.mm(ps[bk][:, :], kr_bf[0:NS, h * 128:(h + 1) * 128], vm[0:NS, 4 * g:4 * g + 4, :], True, True,
                 ["sm_kr_bf", ("sm_Vm", h)], [("ps", bk)])
            k.stt(S_s[:, 4 * g:4 * g + 4, h, :], S_s[:, 4 * g:4 * g + 4, h, :], g1[h],
                  ps[bk][:, :].rearrange("p (b v) -> p b v", b=4), ALU.mult, ALU.add,
                  [("ps", bk)] + Sk, [("sm_S_out", g, h)])
    stats = sb("sm_stats", [NS, 4, 6]); mv = sb("sm_mv", [NS, 4, 2]); vpe = sb("sm_vpe", [NS, 4]); rs = sb("sm_rs", [NS, 4])
    nh4 = sb("sm_nh4", [NS, 4])
    k.memset(nh4[:], -0.5, ["sm_nh4"])
    for h in range(4):
        hs = slice(h * 128, (h + 1) * 128)
        S.add("dve", (lambda h=h, hs=hs: (lambda e: e.bn_stats(stats[:, h, :], osb[:, hs])))(), [("sm_o", h)],
              [("sm_stats", h)])
        S.add("dve", (lambda h=h: (lambda e: e.bn_aggr(mv[:, h, :], stats[:, h, :])))(), [("sm_stats", h)],
              [("sm_mv", h)])
    mvk = [("sm_mv", h) for h in range(4)]
    k.ts(vpe[:], mv[:, :, 1], EPS, None, ALU.add, None, mvk, ["sm_vpe"])
    k.tt(rs[:], vpe[:], nh4[:], ALU.pow, ["sm_vpe", "sm_nh4"], ["sm_rs"], eng="pool")
    sgs = sb("sm_sgs", [NS, 512])
    k.act(sgs[:], proj[:, 1536:2048], AF.Silu, [("sm_proj", 3)], ["sm_sgs"])
    for h in range(4):
        hs = slice(h * 128, (h + 1) * 128)
        k.ts(osb[:, hs], osb[:, hs], mv[:, h, 0:1], rs[:, h:h + 1], ALU.subtract, ALU.mult,
             [("sm_o", h), ("sm_mv", h), "sm_rs"], [("sm_o", h)])
    k.tt(cat[:, 0:512], osb[:], sgs[:], ALU.mult, [("sm_o", h) for h in range(4)] + ["sm_sgs"], [("sm_cat", 0)])
    ov = o_rets.rearrange("b h k v -> k b h v")
    for g in range(4):
        k.dma(ov[:, 4 * g:4 * g + 4, :, :], S_s[:, 4 * g:4 * g + 4, :, :], [("sm_S_out", g, h) for h in range(4)],
              [("o_rets", g)])
    catT = sb("sm_catT", [128, 8, NS], BF16)
    _to_fm(k, ps[0], 0, cat, 8, catT, ident, [("sm_cat", 0), ("sm_cat", 1)], "sm_catT")
    x1 = sb("sm_x1", [NS, D])
    for cb in range(2):
        b = 1 + cb
        for kc in range(8):
            k.mm(ps[b][0:NS, :], catT[:, kc, :], w_out_sb[:, kc, cb * 512:(cb + 1) * 512], kc == 0, kc == 7,
                 ["sm_catT", ("w_out", kc)], [("ps", b)])
        k.tt(x1[:, cb * 512:(cb + 1) * 512], ps[b][0:NS, :], mod[0:NS, 2 * D + cb * 512:2 * D + (cb + 1) * 512], ALU.mult,
             [("ps", b)] + modkeys, [("sm_x1", cb)])
    k.tt(x1[:], x1[:], x[:], ALU.add, [("sm_x1", 0), ("sm_x1", 1), "smx"], [("sm_x1", 0), ("sm_x1", 1)])
    k.dma(x1s_scr, x1[:], [("sm_x1", 0), ("sm_x1", 1)], ["x1s_scr"])
    barrier(S)
    es.__exit__(None, None, None)


def sample_ffn(nc, k, es, ps, ident, x1s_scr, modf_scr, rows_d, wuc_sb, wug_sb, wdn_sb, st_cf, o_cfs, ys):
    S = k.S

    def sb(name, shape, dt=F32):
        return es.enter_context(nc.sbuf_tensor("sb_" + name, list(shape), dt))

    x1 = sb("sf_x1", [NS, D]); modf = sb("sf_modf", [NS, 3 * D]); xn = sb("sf_xn", [NS, D]); junk = xn
    grow = sb("sf_grow", [NS, D]); gs = grow; gfin = sb("sf_gfin", [NS, D])
    k.dma(x1[:], x1s_scr, ["x1s_scr"], ["sfx"])
    k.dma(modf[:], modf_scr, ["modf_scr"], ["sf_modf"])
    k.dma(grow[:], rows_d["g_ffn"].partition_broadcast(NS), (), ["sf_grow"])
    k.dma(gfin[:], rows_d["g_final"].partition_broadcast(NS), (), ["sf_gfin"])
    rstd = _rms_rstd(k, sb, "sf", x1[:], junk[:])
    k.act(xn[:], x1[:], AF.Copy, ["sfx", "sfrstd"], ["sf_xn", "sfjunk"], scale=rstd[:, 0:1])
    k.stt(gs[:], modf[:, D:2 * D], 1.0, grow[:], ALU.add, ALU.mult, ["sf_modf", "sf_grow"], ["sf_grow"])
    k.tt(xn[:], xn[:], gs[:], ALU.mult, ["sf_xn", "sf_grow"], ["sf_xn"])
    k.tt(xn[:], xn[:], modf[:, 0:D], ALU.add, ["sf_xn", "sf_modf"], ["sf_xn"])
    hT = sb("sf_hT", [128, 8, NS], BF16)
    _to_fm(k, ps[0], 0, xn, 8, hT, ident, ["sf_xn"], "sf_hT")
    aT = sb("sf_aT", [128, NFF, NS], BF16)
    cfw_v = rows_d["cfw"].rearrange("(k f) -> k f", k=3)
    blocks = [(0, 512), (512, 512), (1024, 512), (1536, 512), (2048, 512), (2560, 256)]
    bufs = {}
    for a in range(2):
        bufs[a] = dict(
            cfw=sb("sf_cfw%d" % a, [NS, 3, 512]), cfb=sb("sf_cfb%d" % a, [NS, 512]), stf=sb("sf_stf%d" % a, [NS, 2, 512]),
            u=sb("sf_u%d" % a, [NS, 512]), uc=sb("sf_uc%d" % a, [NS, 512]), t=sb("sf_t%d" % a, [NS, 512]))

    def stage1(bi):
        c0, wd = blocks[bi]
        a = bi % 2
        B = bufs[a]
        cs_ = slice(c0, c0 + wd)
        k.dma(B["cfw"][:, :, 0:wd], cfw_v[:, cs_].partition_broadcast(NS), (), [("sf_cfw", a)])
        k.dma(B["cfb"][:, 0:wd], rows_d["cfb"][cs_].partition_broadcast(NS), (), [("sf_cfb", a)])
        k.dma(B["stf"][:, :, 0:wd], st_cf[:, :, cs_], (), [("sf_stf", a)])
        bu, bg = 1 + 2 * a, 2 + 2 * a
        for kc in range(8):
            k.mm(ps[bu][0:NS, 0:wd], hT[:, kc, :], wuc_sb[:, kc, cs_], kc == 0, kc == 7,
                 ["sf_hT", ("wuc", kc, c0 // 1024)], [("ps", bu)])
        for kc in range(8):
            k.mm(ps[bg][0:NS, 0:wd], hT[:, kc, :], wug_sb[:, kc, cs_], kc == 0, kc == 7,
                 ["sf_hT", ("wug", kc, c0 // 1024)], [("ps", bg)])
        tt_ = B["t"]
        for kk in range(2):
            k.tt(tt_[:, 0:wd], B["stf"][:, kk, 0:wd], B["cfw"][:, kk, 0:wd], ALU.mult, [("sf_stf", a), ("sf_cfw", a)],
                 [("sf_t", a)])
            k.tt(B["cfb"][:, 0:wd], B["cfb"][:, 0:wd], tt_[:, 0:wd], ALU.add, [("sf_cfb", a), ("sf_t", a)],
                 [("sf_cfb", a)])
        k.dma(o_cfs[:, 0, cs_], B["stf"][:, 1, 0:wd], [("sf_stf", a)], [("o_cfs0", bi)], eng="act")

    def stage2(bi):
        c0, wd = blocks[bi]
        a = bi % 2
        B = bufs[a]
        cs_ = slice(c0, c0 + wd)
        bu, bg = 1 + 2 * a, 2 + 2 * a
        u = B["u"]; uc = B["uc"]
        k.tt(uc[:, 0:wd], ps[bu][0:NS, 0:wd], B["cfw"][:, 2, 0:wd], ALU.mult, [("ps", bu), ("sf_cfw", a)], [("sf_uc", a)])
        k.copy(u[:, 0:wd], ps[bu][0:NS, 0:wd], [("ps", bu)], [("sf_u", a)], eng="act")
        k.tt(uc[:, 0:wd], uc[:, 0:wd], B["cfb"][:, 0:wd], ALU.add, [("sf_uc", a), ("sf_cfb", a)], [("sf_uc", a)])
        k.act(uc[:, 0:wd], uc[:, 0:wd], AF.Gelu_apprx_tanh, [("sf_uc", a)], [("sf_uc", a)])
        k.tt(uc[:, 0:wd], uc[:, 0:wd], ps[bg][0:NS, 0:wd], ALU.mult, [("sf_uc", a), ("ps", bg)], [("sf_uc", a)])
        k.dma(o_cfs[:, 1, cs_], u[:, 0:wd], [("sf_u", a)], [("o_cfs1", bi)], eng="act")
        nchk = wd // 128
        for c in range(nchk):
            k.tr(ps[5 + a][:, c * NS:(c + 1) * NS], uc[0:NS, c * 128:(c + 1) * 128], ident[0:NS, 0:NS],
                 [("sf_uc", a), "ident"], [("ps", 5 + a)])
        m0 = c0 // 128
        k.copy(aT[:, m0:m0 + nchk, :].rearrange("p a b -> p (a b)"), ps[5 + a][:, 0:nchk * NS], [("ps", 5 + a)],
               [("sf_aT", bi)])

    stage1(0)
    for bi in range(6):
        if bi + 1 < 6:
            stage1(bi + 1)
        stage2(bi)
    aTk = [("sf_aT", bi) for bi in range(6)]
    y = xn
    for cb in range(2):
        b = 1 + cb
        for m in range(NFF):
            k.mm(ps[b][0:NS, :], aT[:, m, :], wdn_sb[:, m, cb * 512:(cb + 1) * 512], m == 0, m == NFF - 1,
                 aTk + [("wdn", m)], [("ps", b)])
        k.tt(y[:, cb * 512:(cb + 1) * 512], ps[b][0:NS, :], modf[:, 2 * D + cb * 512:2 * D + (cb + 1) * 512], ALU.mult,
             [("ps", b), "sf_modf"], [("sf_y", cb), "sf_xn"])
    yk = [("sf_y", 0), ("sf_y", 1)]
    k.tt(y[:], y[:], x1[:], ALU.add, yk + ["sfx"], yk)
    ss = sb("sf_ss2", [NS, 1]); ms = sb("sf_ms2", [NS, 1]); r2 = sb("sf_r2", [NS, 1]); nh = sb("sf_nh2", [NS, 1])
    k.memset(nh[:], -0.5, ["sf_nh2"])
    k.act(grow[:], y[:], AF.Square, yk + ["sf_grow"], ["sf_grow", "sf_ss2"], accum_out=ss[:, 0:1])
    k.ts(ms[:], ss[:], 1.0 / D, EPS, ALU.mult, ALU.add, ["sf_ss2"], ["sf_ms2"])
    k.tt(r2[:], ms[:], nh[:], ALU.pow, ["sf_ms2", "sf_nh2"], ["sf_r2"], eng="pool")
    k.stt(y[:], y[:], r2[:, 0:1], gfin[:], ALU.mult, ALU.mult, yk + ["sf_r2", "sf_gfin"], yk)
    k.dma(ys, y[:], yk, ["ys"])


_NC_CACHE = {}


def _blockdiag(w):
    out = np.zeros((128, 4, 128), np.float32)
    for n in range(8):
        c, hh = n // 2, n % 2
        out[hh * 64:(hh + 1) * 64, c, hh * 64:(hh + 1) * 64] = w[n]
    return out


def _fm(v):
    return np.ascontiguousarray(v.reshape(-1, 128).T)


def kernel(x_prompt, x_sample, c_prompt, c_sample, state_ret, state_lru_h, state_lru_conv, state_ffn_conv,
           w_ada, b_ada, g_mix, w_in, conv_lru_w, conv_lru_b, w_r, b_r, w_i, b_i, lam, w_out, g_ffn,
           w_up_conv, w_up_gate, conv_ffn_w, conv_ffn_b, w_down, g_final):
    f = lambda a: np.ascontiguousarray(np.asarray(a, dtype=np.float32))
    if "nc" not in _NC_CACHE:
        _NC_CACHE["nc"] = build_program()
    nc = _NC_CACHE["nc"]
    clw_fm = np.stack([_fm(f(conv_lru_w)[0, kk]) for kk in range(4)], axis=2).reshape(128, 16)
    cfw_fm = np.stack([_fm(f(conv_ffn_w)[0, kk]) for kk in range(3)], axis=2).reshape(128, 66)
    vfm = np.concatenate([_fm(f(g_mix)[0]), _fm(f(g_ffn)[0]), clw_fm, _fm(f(conv_lru_b)[0]), _fm(f(b_r)[0]),
                          _fm(f(b_i)[0]), _fm(f(lam)[0]), cfw_fm, _fm(f(conv_ffn_b)[0])], axis=1)
    shared = {
        "w_ada": f(w_ada)[0], "b_ada": f(b_ada)[0], "w_in": f(w_in)[0], "w_out": f(w_out)[0],
        "w_upc": f(w_up_conv)[0], "w_upg": f(w_up_gate)[0], "w_dn": f(w_down)[0],
        "wr_bd": _blockdiag(f(w_r)[0]), "wi_bd": _blockdiag(f(w_i)[0]), "vfm": np.ascontiguousarray(vfm),
        "row_g_mix": f(g_mix)[0], "row_g_ffn": f(g_ffn)[0], "row_g_final": f(g_final),
        "row_clw": f(conv_lru_w)[0].reshape(-1), "row_clb": f(conv_lru_b)[0], "row_b_r": f(b_r)[0],
        "row_b_i": f(b_i)[0], "row_lam": f(lam)[0], "row_cfw": f(conv_ffn_w)[0].reshape(-1),
        "row_cfb": f(conv_ffn_b)[0],
    }
    for n in ["ident", "maskT", "cosP", "sinP", "dqT", "dkT", "dkP", "rotS", "sel", "i16"]:
        shared["c_" + n] = CONST[n]
    in_maps = []
    for i in range(N_CORES):
        sl = slice(i * NS, (i + 1) * NS)
        m = dict(shared)
        m["xp"] = f(x_prompt)[i]
        m["xs"] = f(x_sample)[sl, 0, :]
        m["c17"] = np.concatenate([f(c_sample)[sl], f(c_prompt)[i:i + 1]], axis=0)
        m["st_ret"] = f(state_ret)[0, sl]
        m["st_h"] = f(state_lru_h)[0, sl]
        m["st_cl"] = f(state_lru_conv)[0, sl]
        m["st_cf"] = f(state_ffn_conv)[0, sl]
        in_maps.append(m)
    res = run_bass_kernel_spmd(nc, in_maps, core_ids=list(range(N_CORES)))
    R = res.results
    y_prompt = np.stack([R[i]["yp"] for i in range(N_CORES)], axis=0)
    y_sample = np.concatenate([R[i]["ys"] for i in range(N_CORES)], axis=0)[:, None, :]
    ret_p = np.stack([R[i]["o_retp"] for i in range(N_CORES)], axis=0)[None]
    h_p = np.stack([R[i]["o_hp"].reshape(512) for i in range(N_CORES)], axis=0)[None]
    cl_p = np.stack([R[i]["o_clp"] for i in range(N_CORES)], axis=0)[None]
    cf_p = np.stack([R[i]["o_cfp"] for i in range(N_CORES)], axis=0)[None]
    ret_s = np.concatenate([R[i]["o_rets"] for i in range(N_CORES)], axis=0)[None]
    h_s = np.concatenate([R[i]["o_hs"] for i in range(N_CORES)], axis=0)[None]
    cl_s = np.concatenate([R[i]["o_cls"] for i in range(N_CORES)], axis=0)[None]
    cf_s = np.concatenate([R[i]["o_cfs"] for i in range(N_CORES)], axis=0)[None]
    outs = (y_prompt, y_sample, ret_p, h_p, cl_p, cf_p, ret_s, h_s, cl_s, cf_s)
    return tuple(np.ascontiguousarray(o, dtype=np.float32) for o in outs)
```
